# Optimizing a Trainium2 kernel written in Bass

```python
import math
import jax, jax.numpy as jnp
from jax import lax
import numpy as np

D_MODEL = 2048
BATCH = 16
SEQ = 2048
DEPTH = 4

HEAD_DIM = 128
FOX_HEADS = 6
DIFF_HEADS = 4
DIFF_QK_DIM = HEAD_DIM // 2
NSA_HEADS = 6
NSA_KV_GROUPS = 2
NSA_HPG = NSA_HEADS // NSA_KV_GROUPS
CMP_LEN = 32
CMP_STRIDE = 16
SEL_LEN = 64
SEL_TOPN = 8
WINDOW = 512
Q_BLOCK = 128
N_BUCKETS = 32
MAX_DISTANCE = 128
D_FF = 5632
CONV_W = 3
EPS = 1e-6
NEG_INF = -1e30
FORCE_SCORE = 1e4

IN_SIZES = (
    FOX_HEADS * HEAD_DIM, FOX_HEADS * HEAD_DIM, FOX_HEADS * HEAD_DIM, FOX_HEADS,
    DIFF_HEADS * 2 * DIFF_QK_DIM, DIFF_HEADS * 2 * DIFF_QK_DIM, DIFF_HEADS * HEAD_DIM,
    NSA_HEADS * HEAD_DIM,
    NSA_KV_GROUPS * HEAD_DIM, NSA_KV_GROUPS * HEAD_DIM,
    NSA_KV_GROUPS * HEAD_DIM, NSA_KV_GROUPS * HEAD_DIM,
    NSA_KV_GROUPS * HEAD_DIM, NSA_KV_GROUPS * HEAD_DIM,
    NSA_HEADS * 3,
)
N_IN = sum(IN_SIZES)

kernel_name = "hybrid_fox_diff_nsa_convffn_trunk"


def rmsnorm(x, g):
    xf = x.astype(jnp.float32)
    y = xf * lax.rsqrt(jnp.mean(xf * xf, axis=-1, keepdims=True) + EPS)
    return (y * g.astype(jnp.float32)).astype(x.dtype)


def masked_softmax(scores, mask):
    s = jnp.where(mask, scores.astype(jnp.float32), NEG_INF)
    return jax.nn.softmax(s, axis=-1)


def t5_bucket(rel):
    n = jnp.maximum(rel, 0)
    max_exact = N_BUCKETS // 2
    nf = jnp.maximum(n, 1).astype(jnp.float32)
    large = max_exact + (jnp.log(nf / max_exact) / math.log(MAX_DISTANCE / max_exact)
                         * (N_BUCKETS - max_exact)).astype(jnp.int32)
    large = jnp.minimum(large, N_BUCKETS - 1)
    return jnp.where(n < max_exact, n, large)


def to_blocks(a):
    b, s = a.shape[:2]
    a = a.reshape((b, s // Q_BLOCK, Q_BLOCK) + a.shape[2:])
    return jnp.moveaxis(a, 1, 0)


def from_blocks(a):
    a = jnp.moveaxis(a, 0, 1)
    return a.reshape((a.shape[0], a.shape[1] * a.shape[2]) + a.shape[3:])


def fox_attention(q, k, v, f_logit, f_bias):
    b, s, h, dh = q.shape
    log_f = jax.nn.log_sigmoid(f_logit.astype(jnp.float32) + f_bias.astype(jnp.float32))
    cum = jnp.cumsum(log_f, axis=1)
    cum_k = jnp.transpose(cum, (0, 2, 1))
    k_pos = jnp.arange(s)
    scale = dh ** -0.5

    def block(args):
        qb, cb, i = args
        q_pos = i * Q_BLOCK + jnp.arange(Q_BLOCK)
        sc = jnp.einsum('bqhd,bkhd->bhqk', qb, k).astype(jnp.float32) * scale
        sc = sc + jnp.transpose(cb, (0, 2, 1))[..., None] - cum_k[:, :, None, :]
        p = masked_softmax(sc, k_pos[None, :] <= q_pos[:, None])
        return jnp.einsum('bhqk,bkhd->bqhd', p.astype(v.dtype), v)

    out = lax.map(block, (to_blocks(q), to_blocks(cum), jnp.arange(s // Q_BLOCK)))
    return from_blocks(out).reshape(b, s, h * dh)


def diff_attention(q, k, v, rel_table, lam, lam_init, subln_g):
    b, s, h, _, dqk = q.shape
    k_pos = jnp.arange(s)
    scale = dqk ** -0.5

    def block(args):
        qb, i = args
        q_pos = i * Q_BLOCK + jnp.arange(Q_BLOCK)
        bias = jnp.moveaxis(rel_table[t5_bucket(q_pos[:, None] - k_pos[None, :])], -1, 0).astype(jnp.float32)
        sc = jnp.einsum('bqhmd,bkhmd->mbhqk', qb, k).astype(jnp.float32) * scale + bias
        p = masked_softmax(sc, k_pos[None, :] <= q_pos[:, None])
        attn = p[0] - lam * p[1]
        return jnp.einsum('bhqk,bkhd->bqhd', attn.astype(v.dtype), v)

    out = from_blocks(lax.map(block, (to_blocks(q), jnp.arange(s // Q_BLOCK))))
    out = rmsnorm(out, subln_g) * (1.0 - lam_init)
    return out.reshape(b, s, h * v.shape[-1])


def nsa_attention(q, kc, vc, ks, vs, kw, vw, gate_logit, rel_table, cmp_pos, wk1, wk2, wv1, wv2):
    b, s, g, hpg, dh = q.shape
    scale = dh ** -0.5
    t_pos = jnp.arange(s)

    n_cmp = (s - CMP_LEN) // CMP_STRIDE + 1
    n_sel = s // SEL_LEN
    n_top = min(SEL_TOPN, n_sel)
    cmp_starts = np.arange(n_cmp) * CMP_STRIDE
    sel_starts = np.arange(n_sel) * SEL_LEN
    tok = cmp_starts[:, None] + np.arange(CMP_LEN)

    def compress(a, w1, w2):
        blocks = a[:, tok] + cmp_pos[None, None, :, None, :]
        flat = jnp.transpose(blocks, (0, 1, 3, 2, 4)).reshape(b, n_cmp, g, CMP_LEN * dh)
        return jax.nn.gelu(flat @ w1) @ w2

    k_cmp = compress(kc, wk1, wk2)
    v_cmp = compress(vc, wv1, wv2)
    block_end = jnp.asarray(cmp_starts + CMP_LEN - 1)
    sc = jnp.einsum('bsghd,bngd->bghsn', q, k_cmp).astype(jnp.float32) * scale
    sc = sc + jnp.transpose(rel_table[t5_bucket(t_pos[:, None] - block_end[None, :])], (2, 3, 0, 1)).astype(jnp.float32)
    cmp_mask = block_end[None, :] <= t_pos[:, None]
    p_cmp = masked_softmax(sc, cmp_mask) * jnp.any(cmp_mask, axis=-1)[:, None]
    o_cmp = jnp.einsum('bghsn,bngd->bsghd', p_cmp.astype(vc.dtype), v_cmp)

    overlap = np.clip(np.minimum(cmp_starts[:, None] + CMP_LEN, sel_starts[None, :] + SEL_LEN)
                      - np.maximum(cmp_starts[:, None], sel_starts[None, :]), 0, None).astype(np.float32) / CMP_LEN
    imp = jnp.einsum('bghsn,nm->bgsm', p_cmp, jnp.asarray(overlap))
    blk_t = t_pos // SEL_LEN
    j = jnp.arange(n_sel)
    valid = j[None, :] <= blk_t[:, None]
    forced = (j[None, :] == 0) | (j[None, :] == blk_t[:, None]) | (j[None, :] == blk_t[:, None] - 1)
    score = jnp.where(valid, imp + jnp.where(forced, FORCE_SCORE, 0.0), NEG_INF)
    top_score, sel_idx = lax.top_k(score, n_top)
    sel_valid = top_score > 0.5 * NEG_INF
    sel_idx = jnp.transpose(sel_idx, (0, 2, 1, 3))
    sel_valid = jnp.transpose(sel_valid, (0, 2, 1, 3))

    ks_blk = jnp.transpose(ks.reshape(b, n_sel, SEL_LEN, g, dh), (0, 3, 1, 2, 4))
    vs_blk = jnp.transpose(vs.reshape(b, n_sel, SEL_LEN, g, dh), (0, 3, 1, 2, 4))
    kw_pad = jnp.pad(kw, ((0, 0), (WINDOW, 0), (0, 0), (0, 0)))
    vw_pad = jnp.pad(vw, ((0, 0), (WINDOW, 0), (0, 0), (0, 0)))
    rel_g = jnp.transpose(rel_table, (1, 0, 2))
    bi = jnp.arange(b)[:, None, None, None]
    gi = jnp.arange(g)[None, :, None, None]

    def block(args):
        qb, idxb, validb, i = args
        q_pos = i * Q_BLOCK + jnp.arange(Q_BLOCK)
        idx = jnp.transpose(idxb, (0, 2, 1, 3))
        kg = ks_blk[bi, gi, idx]
        vg = vs_blk[bi, gi, idx]
        tok_pos = idx[..., None] * SEL_LEN + jnp.arange(SEL_LEN)
        bias = rel_g[gi[..., None], t5_bucket(q_pos[:, None, None] - tok_pos)]
        scs = jnp.einsum('bqghd,bgqnkd->bghqnk', qb, kg).astype(jnp.float32) * scale \
            + jnp.moveaxis(bias, -1, 2).astype(jnp.float32)
        smask = jnp.transpose(validb, (0, 2, 1, 3))[..., None] & (tok_pos <= q_pos[:, None, None])
        ps = masked_softmax(scs.reshape(b, g, hpg, Q_BLOCK, n_top * SEL_LEN),
                            smask[:, :, None].reshape(b, g, 1, Q_BLOCK, n_top * SEL_LEN))
        o_sel = jnp.einsum('bghqk,bgqkd->bqghd', ps.astype(vs.dtype),
                           vg.reshape(b, g, Q_BLOCK, n_top * SEL_LEN, dh))
        kwb = lax.dynamic_slice_in_dim(kw_pad, i * Q_BLOCK, WINDOW + Q_BLOCK, axis=1)
        vwb = lax.dynamic_slice_in_dim(vw_pad, i * Q_BLOCK, WINDOW + Q_BLOCK, axis=1)
        k_pos = i * Q_BLOCK - WINDOW + jnp.arange(WINDOW + Q_BLOCK)
        rel = q_pos[:, None] - k_pos[None, :]
        wmask = (rel >= 0) & (rel < WINDOW) & (k_pos[None, :] >= 0)
        wbias = jnp.transpose(rel_table[t5_bucket(rel)], (2, 3, 0, 1)).astype(jnp.float32)
        scw = jnp.einsum('bqghd,bkgd->bghqk', qb, kwb).astype(jnp.float32) * scale + wbias
        pw = masked_softmax(scw, wmask)
        o_win = jnp.einsum('bghqk,bkgd->bqghd', pw.astype(vw.dtype), vwb)
        return o_sel, o_win

    o_sel, o_win = lax.map(block, (to_blocks(q), to_blocks(sel_idx), to_blocks(sel_valid), jnp.arange(s // Q_BLOCK)))
    o_sel = from_blocks(o_sel)
    o_win = from_blocks(o_win)
    gates = jax.nn.sigmoid(gate_logit.astype(jnp.float32)).astype(q.dtype)
    out = gates[..., 0:1] * o_cmp + gates[..., 1:2] * o_sel + gates[..., 2:3] * o_win
    return out.reshape(b, s, g * hpg * dh)


def conv_ffn(u, w_up, conv_w, conv_b, w_down):
    s = u.shape[1]
    hdn = u @ w_up
    hp = jnp.pad(hdn, ((0, 0), (CONV_W - 1, 0), (0, 0)))
    hdn = sum(hp[:, tap:tap + s] * conv_w[tap] for tap in range(CONV_W)) + conv_b
    gate, up = jnp.split(hdn, 2, axis=-1)
    return (jax.nn.silu(gate) * up) @ w_down


def setup_inputs(seed: int = 0) -> dict:
    key = jax.random.key(seed)
    ks = jax.random.split(key, 24)
    f32 = jnp.float32
    nrm = lambda k, shape, scale: jax.random.normal(k, shape, f32) * scale
    return {
        "x": jax.random.normal(ks[0], (BATCH, SEQ, D_MODEL), f32),
        "attn_norm_g": 1.0 + nrm(ks[1], (DEPTH, D_MODEL), 0.02),
        "w_in": nrm(ks[2], (DEPTH, D_MODEL, N_IN), D_MODEL ** -0.5),
        "fox_f_bias": jax.random.uniform(ks[3], (DEPTH, FOX_HEADS), f32, 1.0, 4.0),
        "diff_lq1": nrm(ks[4], (DEPTH, DIFF_QK_DIM), 0.1),
        "diff_lk1": nrm(ks[5], (DEPTH, DIFF_QK_DIM), 0.1),
        "diff_lq2": nrm(ks[6], (DEPTH, DIFF_QK_DIM), 0.1),
        "diff_lk2": nrm(ks[7], (DEPTH, DIFF_QK_DIM), 0.1),
        "diff_subln_g": 1.0 + nrm(ks[8], (DEPTH, 2 * DIFF_QK_DIM), 0.02),
        "nsa_cmp_pos": nrm(ks[9], (DEPTH, CMP_LEN, HEAD_DIM), 0.02),
        "nsa_cmp_wk1": nrm(ks[10], (DEPTH, CMP_LEN * HEAD_DIM, HEAD_DIM), (CMP_LEN * HEAD_DIM) ** -0.5),
        "nsa_cmp_wk2": nrm(ks[11], (DEPTH, HEAD_DIM, HEAD_DIM), HEAD_DIM ** -0.5),
        "nsa_cmp_wv1": nrm(ks[12], (DEPTH, CMP_LEN * HEAD_DIM, HEAD_DIM), (CMP_LEN * HEAD_DIM) ** -0.5),
        "nsa_cmp_wv2": nrm(ks[13], (DEPTH, HEAD_DIM, HEAD_DIM), HEAD_DIM ** -0.5),
        "w_out": nrm(ks[14], (DEPTH, D_MODEL, D_MODEL), D_MODEL ** -0.5),
        "ffn_norm_g": 1.0 + nrm(ks[15], (DEPTH, D_MODEL), 0.02),
        "ffn_w_up": nrm(ks[16], (DEPTH, D_MODEL, 2 * D_FF), D_MODEL ** -0.5),
        "ffn_conv_w": nrm(ks[17], (DEPTH, CONV_W, 2 * D_FF), CONV_W ** -0.5),
        "ffn_conv_b": nrm(ks[18], (DEPTH, 2 * D_FF), 0.01),
        "ffn_w_down": nrm(ks[19], (DEPTH, D_FF, D_MODEL), D_FF ** -0.5),
        "rel_bias": nrm(ks[20], (N_BUCKETS, DIFF_HEADS + NSA_HEADS), 0.2),
        "final_norm_g": 1.0 + nrm(ks[21], (D_MODEL,), 0.02),
    }


def reference(x, attn_norm_g, w_in, fox_f_bias, diff_lq1, diff_lk1, diff_lq2, diff_lk2, diff_subln_g,
              nsa_cmp_pos, nsa_cmp_wk1, nsa_cmp_wk2, nsa_cmp_wv1, nsa_cmp_wv2, w_out, ffn_norm_g,
              ffn_w_up, ffn_conv_w, ffn_conv_b, ffn_w_down, rel_bias, final_norm_g):
    b, s, _ = x.shape
    split_points = np.cumsum(IN_SIZES)[:-1].tolist()
    diff_table = rel_bias[:, :DIFF_HEADS]
    nsa_table = rel_bias[:, DIFF_HEADS:].reshape(N_BUCKETS, NSA_KV_GROUPS, NSA_HPG)
    G, HPG, Dh = NSA_KV_GROUPS, NSA_HPG, HEAD_DIM
    h = x
    for l in range(DEPTH):
        u = rmsnorm(h, attn_norm_g[l])
        proj = u @ w_in[l]
        (fq, fk, fv, ff, dq, dk, dv, nq, nkc, nvc, nks, nvs, nkw, nvw, ng) = jnp.split(proj, split_points, axis=-1)

        fox_o = fox_attention(fq.reshape(b, s, FOX_HEADS, Dh), fk.reshape(b, s, FOX_HEADS, Dh),
                              fv.reshape(b, s, FOX_HEADS, Dh), ff, fox_f_bias[l])

        lam_init = 0.8 - 0.6 * math.exp(-0.3 * l)
        lam = (jnp.exp(jnp.sum(diff_lq1[l].astype(jnp.float32) * diff_lk1[l].astype(jnp.float32)))
               - jnp.exp(jnp.sum(diff_lq2[l].astype(jnp.float32) * diff_lk2[l].astype(jnp.float32))) + lam_init)
        diff_o = diff_attention(dq.reshape(b, s, DIFF_HEADS, 2, DIFF_QK_DIM), dk.reshape(b, s, DIFF_HEADS, 2, DIFF_QK_DIM),
                                dv.reshape(b, s, DIFF_HEADS, Dh), diff_table, lam, lam_init, diff_subln_g[l])

        nsa_o = nsa_attention(nq.reshape(b, s, G, HPG, Dh),
                              nkc.reshape(b, s, G, Dh), nvc.reshape(b, s, G, Dh),
                              nks.reshape(b, s, G, Dh), nvs.reshape(b, s, G, Dh),
                              nkw.reshape(b, s, G, Dh), nvw.reshape(b, s, G, Dh),
                              ng.reshape(b, s, G, HPG, 3), nsa_table, nsa_cmp_pos[l],
                              nsa_cmp_wk1[l], nsa_cmp_wk2[l], nsa_cmp_wv1[l], nsa_cmp_wv2[l])

        mixed = jnp.concatenate([fox_o, diff_o, nsa_o], axis=-1)
        h = h + mixed @ w_out[l]
        h = h + conv_ffn(rmsnorm(h, ffn_norm_g[l]), ffn_w_up[l], ffn_conv_w[l], ffn_conv_b[l], ffn_w_down[l])
    return rmsnorm(h, final_norm_g)
```

```python
import math
from contextlib import ExitStack
import numpy as np
import concourse.bass as bass
import concourse.mybir as mybir
from concourse.bass_utils import run_bass_kernel_spmd

F32 = mybir.dt.float32
BF16 = mybir.dt.bfloat16
AF = mybir.ActivationFunctionType
ALU = mybir.AluOpType
AX = mybir.AxisListType

S = 2048
D = 2048
NCH = 16
DFF = 5632
NFF = 44
EPS = 1e-6
BIG = 30000.0
N_IN = 6168
NCMP = 127
NSEL = 32
ENGS = ['pe', 'act', 'dve', 'pool', 'sp']
ENG_ATTR = {'pe': 'tensor', 'act': 'scalar', 'dve': 'vector', 'pool': 'gpsimd', 'sp': 'sync'}
SAME_ENG_SYNC = True
NW = 45056

C_FQ, C_FK, C_FV, C_FF = 0, 768, 1536, 2304
C_DQ, C_DK, C_DV = 2310, 2822, 3334
C_NQ = 3846
C_NKC, C_NVC, C_NKS, C_NVS, C_NKW, C_NVW, C_NG = 4614, 4870, 5126, 5382, 5638, 5894, 6150
PT_FQ, PT_FK, PT_DQ, PT_DK, PT_NQ, PT_NKC, PT_NVC, PT_NKS, PT_NKW = 0, 6, 12, 16, 20, 26, 28, 30, 32
NPT = 34
VT_FV, VT_DV, VT_NVS, VT_NVW = 0, 768, 1280, 1536
VTC = 1792


class Prog:
    def __init__(self, nc, nslots=8):
        self.nc = nc
        self.streams = {e: [] for e in ENGS}
        self.cnt = {e: 0 for e in ENGS}
        self.known = {e: {} for e in ENGS}
        self.last_w = {}
        self.readers = {}
        self.slots = {q: [[f"{q}_d{i}", 0] for i in range(nslots)] for q in ('sp', 'pool', 'act')}
        self.rr = {q: 0 for q in self.slots}
        self.nops = 0

    def op(self, eng, fn, rd=(), wr=(), dma=False):
        wdeps = {}
        rdeps = {}

        def add(dd, ev):
            if ev is None:
                return
            k, v = ev
            if dd.get(k, 0) < v:
                dd[k] = v
        for t in rd:
            add(wdeps, self.last_w.get(t))
            if t[0] == 'ps':
                for k, v in self.readers.get(t, {}).items():
                    add(rdeps, (k, v))
        for t in wr:
            add(wdeps, self.last_w.get(t))
            for k, v in self.readers.get(t, {}).items():
                add(rdeps, (k, v))
        if dma:
            sl = self.slots[eng][self.rr[eng]]
            self.rr[eng] = (self.rr[eng] + 1) % len(self.slots[eng])
            add(wdeps, (sl[0], sl[1]))
            sl[1] += 16
            ev = (sl[0], sl[1])
            inc = (sl[0], 16)
        else:
            self.cnt[eng] += 1
            ev = (eng, self.cnt[eng])
            inc = (eng, 1)
        waits = []
        kn = self.known[eng]
        for dd, isw in ((wdeps, True), (rdeps, False)):
            for k, v in dd.items():
                if v <= 0:
                    continue
                if k == eng:
                    if (not isw) or eng == 'pe' or not SAME_ENG_SYNC:
                        continue
                if kn.get(k, 0) >= v:
                    continue
                kn[k] = v
                waits.append((k, v))
        self.streams[eng].append((waits, fn, inc))
        for t in rd:
            r = self.readers.setdefault(t, {})
            if r.get(ev[0], 0) < ev[1]:
                r[ev[0]] = ev[1]
        for t in wr:
            self.last_w[t] = ev
            self.readers[t] = {}
        self.nops += 1
        return ev

    def dma(self, q, out, in_, rd=(), wr=(), slow=False):
        if slow:
            return self.op(q, lambda e, o=out, i=in_: e.dma_start(out=o, in_=i, allow_slow_non_contiguous=True),
                           rd=rd, wr=wr, dma=True)
        return self.op(q, lambda e, o=out, i=in_: e.dma_start(out=o, in_=i), rd=rd, wr=wr, dma=True)

    def all_events(self):
        evs = [(e, self.cnt[e]) for e in ENGS]
        for q in self.slots:
            for nm, v in self.slots[q]:
                evs.append((nm, v))
        return evs

    def barrier(self):
        evs = self.all_events()
        for eng in ENGS:
            waits = []
            kn = self.known[eng]
            for k, v in evs:
                if k == eng or v <= 0 or kn.get(k, 0) >= v:
                    continue
                kn[k] = v
                waits.append((k, v))
            if waits:
                self.streams[eng].append((waits, None, None))
        self.last_w = {}
        self.readers = {}

    def emit(self):
        nc = self.nc
        keys = list(ENGS)
        for q in self.slots:
            keys += [s[0] for s in self.slots[q]]
        with ExitStack() as st:
            sems = {k: st.enter_context(nc.semaphore(k)) for k in keys}
            block = st.enter_context(nc.Block())
            for eng in ENGS:
                stream = self.streams[eng]

                def body(e, stream=stream):
                    for waits, fn, inc in stream:
                        for k, v in waits:
                            e.wait_ge(sems[k], v)
                        if fn is not None:
                            ins = fn(e)
                            ins.then_inc(sems[inc[0]], inc[1])
                getattr(block, ENG_ATTR[eng])(body)


class Arena:
    def __init__(self, big):
        self.big = big
        self.off = 0

    def take(self, nelem, dtype=F32):
        nb = nelem * (4 if dtype == F32 else 2)
        nw = (nb + 3) // 4
        a = self.off
        self.off += (nw + 7) // 8 * 8
        assert self.off <= NW, f"SBUF arena overflow {self.off}"
        ap = self.big[:, a:a + nw]
        if dtype != F32:
            ap = ap.bitcast(dtype)[:, :nelem]
        return ap

    def mark(self):
        return self.off

    def release(self, m):
        self.off = m


def fm_slabs():
    tiles = []
    for h in range(6):
        tiles.append((C_FQ + 128 * h, PT_FQ + h))
    for h in range(6):
        tiles.append((C_FK + 128 * h, PT_FK + h))
    for h in range(4):
        tiles.append((C_DQ + 128 * h, PT_DQ + h))
    for h in range(4):
        tiles.append((C_DK + 128 * h, PT_DK + h))
    for h in range(6):
        tiles.append((C_NQ + 128 * h, PT_NQ + h))
    for g in range(2):
        tiles.append((C_NKC + 128 * g, PT_NKC + g))
    for g in range(2):
        tiles.append((C_NVC + 128 * g, PT_NVC + g))
    for g in range(2):
        tiles.append((C_NKS + 128 * g, PT_NKS + g))
    for g in range(2):
        tiles.append((C_NKW + 128 * g, PT_NKW + g))
    slabs = []
    for i in range(0, len(tiles), 4):
        grp = tiles[i:i + 4]
        segs = []
        for c0, _ in grp:
            if segs and segs[-1][0] + segs[-1][1] == c0:
                segs[-1] = (segs[-1][0], segs[-1][1] + 128)
            else:
                segs.append((c0, 128))
        slabs.append((segs, [t[1] for t in grp]))
    return slabs


def tok_slabs():
    return [
        ([(C_FV, 512)], 0, 512, False),
        ([(C_FV + 512, 256), (C_DV, 256)], 512, 512, False),
        ([(C_DV + 256, 256), (C_NVS, 256)], 1024, 512, False),
        ([(C_NVW, 256), (C_NG, 18)], 1536, 256, True),
    ]


class Builder:
    def __init__(self, L, NSEQ, debug=False):
        self.L, self.NSEQ, self.debug = L, NSEQ, debug
        nc = self.nc = bass.Bass("TRN2", target_bir_lowering=False)
        self.P = Prog(nc)
        dk = "ExternalOutput" if debug else "Internal"
        din = lambda n, sh: nc.dram_tensor(n, sh, F32, kind="ExternalInput").ap()
        self.x = din("x", [NSEQ, S, D])
        self.attn_norm_g = din("attn_norm_g", [L, D])
        self.w_in = din("w_in", [L, D, N_IN])
        self.fox_f_bias = din("fox_f_bias", [L, 6])
        self.lam4 = din("diff_lam4", [L, 4, 64])
        self.diff_subln_g = din("diff_subln_g", [L, 128])
        self.nsa_cmp_pos = din("nsa_cmp_pos", [L, 32, 128])
        self.nsa_cmp_w1 = din("nsa_cmp_w1", [L, 2, 4096, 128])
        self.nsa_cmp_w2 = din("nsa_cmp_w2", [L, 2, 128, 128])
        self.w_out = din("w_out", [L, D, D])
        self.ffn_norm_g = din("ffn_norm_g", [L, D])
        self.ffn_w_up = din("ffn_w_up", [L, D, 2 * DFF])
        self.ffn_conv_w = din("ffn_conv_w", [L, 3, 2 * DFF])
        self.ffn_conv_b = din("ffn_conv_b", [L, 2 * DFF])
        self.ffn_w_down = din("ffn_w_down", [L, DFF, D])
        self.rel_bias = din("rel_bias", [32, 10])
        self.final_norm_g = din("final_norm_g", [D])
        self.cst = din("cst", [128, CST_W])
        self.out = nc.dram_tensor("out", [NSEQ, S, D], F32, kind="ExternalOutput").ap()
        self.hT = nc.dram_tensor("hT", [NSEQ, 128, NCH, S], F32, kind=dk).ap()
        self.PT = nc.dram_tensor("PT", [NPT, 128, S], BF16, kind=dk).ap()
        self.VT = nc.dram_tensor("VT", [16, 128, VTC], BF16, kind=dk).ap()
        self.GT = nc.dram_tensor("GT", [16, 128, 18], F32, kind=dk).ap()
        self.FAR = nc.dram_tensor("FAR", [6, 6, S], BF16, kind=dk).ap()
        self.FAL = nc.dram_tensor("FAL", [6, 6, S], BF16, kind=dk).ap()
        self.TCr_h = nc.dram_tensor("TCr", [10, 4352], F32, kind=dk)
        self.TCr = self.TCr_h.ap()
        self.MT = nc.dram_tensor("MT", [16, 128, S], BF16, kind=dk).ap()
        self.AT = nc.dram_tensor("AT", [NFF, 128, S], BF16, kind="Internal").ap()

    def build(self, phases=("x0", "A")):
        nc, P = self.nc, self.P
        with ExitStack() as st:
            big = st.enter_context(nc.sbuf_tensor("big", [128, NW], F32))
            self.ps = st.enter_context(nc.psum_tensor("ps", [128, 8, 512], F32))
            self.sb = Arena(big)
            self.bank_rr = 0
            self.setup_consts()
            for s in range(self.NSEQ):
                if "x0" in phases:
                    self.phase_x0(s)
            for l in range(self.L):
                for s in range(self.NSEQ):
                    if "A" in phases:
                        self.phase_A(l, s)
                    if "Bf" in phases:
                        self.phase_B_fox(l, s)
                    if "Bd" in phases:
                        self.phase_B_diff(l, s)
                    if "Bn" in phases:
                        self.phase_B_nsa(l, s)
                    if "C" in phases:
                        self.phase_C(l, s)
            for s in range(self.NSEQ):
                if "Z" in phases:
                    self.phase_Z(s)
            P.barrier()
            P.emit()
        return nc

    def bank(self):
        b = self.bank_rr
        self.bank_rr = (self.bank_rr + 1) % 8
        return b

    def setup_consts(self):
        P, sb, nc, ps = self.P, self.sb, self.nc, self.ps
        L = self.L
        cst = self.cst
        self.identF = sb.take(128, F32)
        self.NM0F = sb.take(128, F32)
        self.W4F = sb.take(128, F32)
        self.FM = sb.take(512, F32).rearrange("p (a m) -> p a m", a=16)
        P.dma('sp', self.identF, cst[:, CST_IDENT:CST_IDENT + 128], wr=[('identF',)])
        P.dma('sp', self.NM0F, cst[:, CST_NM0:CST_NM0 + 128], wr=[('NM0F',)])
        P.dma('sp', self.W4F, cst[:, CST_W4:CST_W4 + 128], wr=[('W4F',)])
        P.dma('sp', self.FM, cst[:, CST_FM:CST_FM + 512].rearrange("p (a m) -> p a m", a=16), wr=[('FM',)])
        self.identB = sb.take(128, BF16)
        P.op('dve', lambda e: e.tensor_copy(out=self.identB, in_=self.identF), rd=[('identF',)], wr=[('identB',)])
        self.onesB = sb.take(128, BF16)
        P.op('dve', lambda e: e.memset(self.onesB, 1.0), wr=[('onesB',)])
        self.OVb = sb.take(32, BF16)
        P.dma('pool', self.OVb[0:NCMP, :], cst[0:NCMP, CST_OV:CST_OV + 32], wr=[('OVb',)])
        self.Epad = sb.take(S, BF16)
        P.op('pool', lambda e: e.memset(self.Epad, 0.0), wr=[('Epad',)])
        P.dma('pool', self.Epad[0:32, :], cst[0:32, CST_E:CST_E + S], wr=[('Epad',)])
        self.BT = sb.take(10 * 2 * 128, F32).rearrange("p (h k c) -> p h k c", h=10, k=2)
        self.CB = sb.take(10, F32)
        P.dma('sp', self.CB, self.rel_bias[31:32, :].partition_broadcast(128), wr=[('CB',)])
        self.g1 = sb.take(L * NCH, F32)
        self.g2 = sb.take(L * NCH, F32)
        self.g3 = sb.take(NCH, F32)
        P.dma('sp', self.g1.rearrange("p (l c) -> p l c", l=L),
              self.attn_norm_g.rearrange("l (c p) -> p l c", p=128), wr=[('g1',)], slow=True)
        P.dma('sp', self.g2.rearrange("p (l c) -> p l c", l=L),
              self.ffn_norm_g.rearrange("l (c p) -> p l c", p=128), wr=[('g2',)], slow=True)
        P.dma('sp', self.g3, self.final_norm_g.rearrange("(c p) -> p c", p=128), wr=[('g3',)], slow=True)
        self.nfb = sb.take(L, F32)
        P.dma('sp', self.nfb[0:6, :], self.fox_f_bias.rearrange("l h -> h l"), wr=[('nfb',)], slow=True)
        P.op('dve', lambda e: e.tensor_scalar(out=self.nfb[0:6, :], in0=self.nfb[0:6, :], scalar1=-1.0, scalar2=None,
                                              op0=ALU.mult), rd=[('nfb',)], wr=[('nfb',)])
        m = sb.mark()
        onesrow = sb.take(S, BF16)
        P.op('pool', lambda e: e.memset(onesrow[0:6, :], 1.0), wr=[('onesrow',)])
        for r in range(3):
            P.dma('sp', self.FAR[:, 3 + r, :], onesrow[0:6, :], rd=[('onesrow',)], wr=[('FAR1', r)])
            P.dma('sp', self.FAL[:, r, :], onesrow[0:6, :], rd=[('onesrow',)], wr=[('FAL1', r)])
        ohF = sb.take(256, F32)
        invsc = sb.take(1, F32)
        relS = sb.take(10, F32)
        tt = sb.take(256, F32)
        fill = sb.take(2048, F32)
        P.dma('sp', ohF[0:32, :], cst[0:32, CST_OH:CST_OH + 256], wr=[('ohF',)])
        P.dma('sp', invsc[0:10, :], cst[0:10, CST_INVSC:CST_INVSC + 1], wr=[('invsc',)], slow=True)
        P.dma('sp', relS[0:32, :], self.rel_bias[:, :], wr=[('relS',)])
        P.op('pe', lambda e: e.matmul(ps[0:10, 0, 0:256], relS[0:32, 0:10], ohF[0:32, 0:256], start=True, stop=True),
             rd=[('ohF',), ('relS',)], wr=[('ps', 0)])
        P.op('dve', lambda e: e.tensor_scalar(out=tt[0:10, :], in0=ps[0:10, 0, 0:256], scalar1=invsc[0:10, 0:1],
                                              scalar2=None, op0=ALU.mult), rd=[('ps', 0), ('invsc',)], wr=[('tt',)])
        P.dma('sp', self.TCr[:, 2048:2304], tt[0:10, :], rd=[('tt',)], wr=[('TCr', 1)])
        P.op('pool', lambda e: e.memset(fill[0:10, :], 0.0), wr=[('fill',)])
        P.dma('sp', self.TCr[:, 0:2048], fill[0:10, :], rd=[('fill',)], wr=[('TCr', 0)])
        P.op('pool', lambda e: e.memset(fill[0:10, :], -BIG), rd=[], wr=[('fill',)])
        P.dma('sp', self.TCr[:, 2304:4352], fill[0:10, :], rd=[('fill',)], wr=[('TCr', 2)])
        P.barrier()
        Y = [sb.take(128, F32) for _ in range(2)]
        n = 0
        for h in range(10):
            for k, off in enumerate((0, 128)):
                y = Y[n % 2]
                src = bass.AP(tensor=self.TCr_h, offset=h * 4352 + 2176 - off, ap=[[1, 128], [1, 128]])
                P.dma('sp', y, src, wr=[('Y', n % 2)])
                P.op('dve', lambda e, y=y, h=h, k=k: e.tensor_copy(out=self.BT[:, h, k, :], in_=y[:, ::-1]),
                     rd=[('Y', n % 2)], wr=[('BT', h, k)])
                n += 1
        P.barrier()
        sb.release(m)

    def phase_x0(self, s):
        P, sb, ps = self.P, self.sb, self.ps
        m = sb.mark()
        xin = sb.take(4 * D, F32).rearrange("p (a c) -> p a c", a=4)
        stg = sb.take(NCH * 512, F32).rearrange("p (c t) -> p c t", c=NCH)
        for j in range(4):
            P.dma('sp', xin, self.x[s, j * 512:(j + 1) * 512, :].rearrange("(a p) c -> p a c", p=128),
                  wr=[('xin',)])
            for cc in range(NCH):
                b = self.bank()
                for a in range(4):
                    P.op('pe', lambda e, b=b, a=a, cc=cc: e.transpose(
                        out=ps[:, b, a * 128:(a + 1) * 128], in_=xin[:, a, cc * 128:(cc + 1) * 128],
                        identity=self.identF), rd=[('xin',), ('cst',)], wr=[('ps', b)])
                if cc % 2 == 0:
                    P.op('act', lambda e, b=b, cc=cc: e.copy(out=stg[:, cc, :], in_=ps[:, b, :]),
                         rd=[('ps', b)], wr=[('stg', cc)])
                else:
                    P.op('dve', lambda e, b=b, cc=cc: e.tensor_copy(out=stg[:, cc, :], in_=ps[:, b, :]),
                         rd=[('ps', b)], wr=[('stg', cc)])
            P.dma('sp', self.hT[s, :, :, j * 512:(j + 1) * 512], stg,
                  rd=[('stg', c) for c in range(NCH)], wr=[('hT', s, j)])
        P.barrier()
        sb.release(m)

    def rmsnorm_to_uT(self, src_tok, hT_s, gain, goff, uT, out_cb=None):
        P, sb, ps = self.P, self.sb, self.ps
        hbuf = [sb.take(NCH * 256, F32).rearrange("p (c t) -> p c t", c=NCH) for _ in range(2)]
        sq = sb.take(NCH * 256, BF16).rearrange("p (c t) -> p c t", c=NCH)
        t1 = sb.take(256, F32)
        t2 = sb.take(256, F32)
        rstd = sb.take(256, F32)
        for i in range(8):
            hb = hbuf[i % 2]
            P.dma('sp', hb, hT_s[:, :, i * 256:(i + 1) * 256], rd=[src_tok(i // 2)], wr=[('hb', i % 2)])
            P.op('act', lambda e, hb=hb: e.activation(out=sq, in_=hb, func=AF.Square),
                 rd=[('hb', i % 2)], wr=[('sq',)])
            b = self.bank()
            for c in range(NCH):
                P.op('pe', lambda e, b=b, c=c: e.matmul(ps[:, b, 0:256], self.onesB, sq[:, c, :],
                                                       start=(c == 0), stop=(c == NCH - 1)),
                     rd=[('sq',), ('onesB',)], wr=[('ps', b)])
            P.op('dve', lambda e, b=b: e.tensor_scalar(out=t1, in0=ps[:, b, 0:256], scalar1=EPS * D, scalar2=1.0 / D,
                                                      op0=ALU.add, op1=ALU.mult), rd=[('ps', b)], wr=[('t1',)])
            P.op('act', lambda e: e.activation(out=t2, in_=t1, func=AF.Sqrt), rd=[('t1',)], wr=[('t2',)])
            P.op('dve', lambda e: e.reciprocal(out=rstd, in_=t2), rd=[('t2',)], wr=[('rstd',)])
            if out_cb is not None:
                out_cb(i, hb, rstd)
                continue
            for c in range(NCH):
                P.op('dve', lambda e, hb=hb, c=c, i=i: e.scalar_tensor_tensor(
                    out=uT[:, c, i * 256:(i + 1) * 256], in0=hb[:, c, :], scalar=gain[:, goff + c:goff + c + 1],
                    in1=rstd, op0=ALU.mult, op1=ALU.mult),
                    rd=[('hb', i % 2), ('rstd',)], wr=[('uT', c, i)])

    def load_w_slab(self, wb, key, wsrc, segs):
        P = self.P
        off = 0
        toks = []
        for si, (c0, n) in enumerate(segs):
            tok = ('wb', key, si)
            P.dma('pool', wb[:, :, off:off + n], wsrc[:, c0:c0 + n].rearrange("(cc p) f -> p cc f", p=128),
                  wr=[tok])
            toks.append(tok)
            off += n
        return toks

    def phase_A(self, l, s):
        P, sb, ps, nc = self.P, self.sb, self.ps, self.nc
        m = sb.mark()
        uT = sb.take(NCH * S, BF16).rearrange("p (c t) -> p c t", c=NCH)
        m2 = sb.mark()
        self.rmsnorm_to_uT(lambda j: ('hT', s, j), self.hT[s], self.g1, l * NCH, uT)
        P.barrier()
        sb.release(m2)
        wbuf = [sb.take(NCH * 512, BF16).rearrange("p (c f) -> p c f", c=NCH) for _ in range(2)]
        stgF = [sb.take(S, BF16) for _ in range(2)]
        stgT = [sb.take(512, BF16) for _ in range(2)]
        stgG = [sb.take(18, F32) for _ in range(2)]
        wsrc = self.w_in[l]
        utoks = lambda c, t0, t1: [('uT', c, i) for i in range(t0 // 256, (t1 + 255) // 256)]
        nslab = 0
        nst = 0
        for segs, tiles in fm_slabs():
            k = nslab % 2
            nslab += 1
            wb = wbuf[k]
            wtoks = self.load_w_slab(wb, k, wsrc, segs)
            for ft, pti in enumerate(tiles):
                sk = nst % 2
                nst += 1
                for j in range(4):
                    b = self.bank()
                    for c in range(NCH):
                        P.op('pe', lambda e, b=b, c=c, wb=wb, ft=ft, j=j: e.matmul(
                            ps[:, b, :], wb[:, c, ft * 128:(ft + 1) * 128], uT[:, c, j * 512:(j + 1) * 512],
                            start=(c == 0), stop=(c == NCH - 1)),
                            rd=wtoks + utoks(c, j * 512, (j + 1) * 512), wr=[('ps', b)])
                    if j % 2 == 0:
                        P.op('act', lambda e, b=b, sk=sk, j=j: e.copy(out=stgF[sk][:, j * 512:(j + 1) * 512],
                                                                     in_=ps[:, b, :]),
                             rd=[('ps', b)], wr=[('stgF', sk, j)])
                    else:
                        P.op('dve', lambda e, b=b, sk=sk, j=j: e.tensor_copy(out=stgF[sk][:, j * 512:(j + 1) * 512],
                                                                            in_=ps[:, b, :]),
                             rd=[('ps', b)], wr=[('stgF', sk, j)])
                P.dma('sp', self.PT[pti], stgF[sk], rd=[('stgF', sk, j) for j in range(4)], wr=[('PT', pti)])
        ntt = 0
        for segs, voff, nv, gates in tok_slabs():
            k = nslab % 2
            nslab += 1
            wb = wbuf[k]
            wtoks = self.load_w_slab(wb, k, wsrc, segs)
            ncol = sum(n for _, n in segs)
            for tt in range(16):
                sk = ntt % 2
                ntt += 1
                b = self.bank()
                for c in range(NCH):
                    P.op('pe', lambda e, b=b, c=c, wb=wb, tt=tt, ncol=ncol: e.matmul(
                        ps[:, b, 0:ncol], uT[:, c, tt * 128:(tt + 1) * 128], wb[:, c, 0:ncol],
                        start=(c == 0), stop=(c == NCH - 1)),
                        rd=wtoks + utoks(c, tt * 128, (tt + 1) * 128), wr=[('ps', b)])
                if tt % 2 == 0:
                    P.op('act', lambda e, b=b, sk=sk, nv=nv: e.copy(out=stgT[sk][:, 0:nv], in_=ps[:, b, 0:nv]),
                         rd=[('ps', b)], wr=[('stgT', sk)])
                else:
                    P.op('dve', lambda e, b=b, sk=sk, nv=nv: e.tensor_copy(out=stgT[sk][:, 0:nv], in_=ps[:, b, 0:nv]),
                         rd=[('ps', b)], wr=[('stgT', sk)])
                P.dma('sp', self.VT[tt, :, voff:voff + nv], stgT[sk][:, 0:nv], rd=[('stgT', sk)],
                      wr=[('VT', tt, voff)])
                if gates:
                    P.op('act', lambda e, b=b, sk=sk, nv=nv: e.activation(out=stgG[sk], in_=ps[:, b, nv:nv + 18],
                                                                         func=AF.Sigmoid),
                         rd=[('ps', b)], wr=[('stgG', sk)])
                    P.dma('sp', self.GT[tt], stgG[sk], rd=[('stgG', sk)], wr=[('GT', tt)])
        wff = sb.take(NCH * 6, BF16).rearrange("p (c f) -> p c f", c=NCH)
        P.dma('pool', wff, wsrc[:, C_FF:C_FF + 6].rearrange("(cc p) f -> p cc f", p=128), wr=[('wff',)])
        e1 = sb.take(S, F32)
        onesF = sb.take(S, F32)
        cs = sb.take(S, F32)
        P.op('pool', lambda e: e.memset(onesF[0:6, :], 1.0), wr=[('onesF',)])
        for j in range(4):
            b = self.bank()
            for c in range(NCH):
                P.op('pe', lambda e, b=b, c=c, j=j: e.matmul(ps[0:6, b, :], wff[:, c, :], uT[:, c, j * 512:(j + 1) * 512],
                                                            start=(c == 0), stop=(c == NCH - 1)),
                     rd=[('wff',)] + utoks(c, j * 512, (j + 1) * 512), wr=[('ps', b)])
            P.op('act', lambda e, b=b, j=j: e.activation(out=e1[0:6, j * 512:(j + 1) * 512], in_=ps[0:6, b, :],
                                                        func=AF.Exp, scale=-1.0, bias=self.nfb[0:6, l:l + 1]),
                 rd=[('ps', b), ('nfb',)], wr=[('e1', j)])
        e1t = [('e1', j) for j in range(4)]
        P.op('dve', lambda e: e.tensor_scalar(out=e1[0:6, :], in0=e1[0:6, :], scalar1=1.0, scalar2=None, op0=ALU.add),
             rd=e1t, wr=e1t)
        P.op('act', lambda e: e.activation(out=e1[0:6, :], in_=e1[0:6, :], func=AF.Ln), rd=e1t, wr=e1t)
        P.op('dve', lambda e: e.tensor_tensor_scan(out=cs[0:6, :], data0=onesF[0:6, :], data1=e1[0:6, :], initial=0.0,
                                                  op0=ALU.mult, op1=ALU.add), rd=e1t + [('onesF',)], wr=[('cs',)])
        sq128 = math.sqrt(128.0)
        P.op('dve', lambda e: e.tensor_scalar(out=cs[0:6, :], in0=cs[0:6, :], scalar1=-sq128, scalar2=None,
                                              op0=ALU.mult), rd=[('cs',)], wr=[('cs',)])
        pcs = [sb.take(S, BF16) for _ in range(3)]
        ncs = [sb.take(S, BF16) for _ in range(3)]
        for r in range(3):
            P.op('dve', lambda e, r=r: e.tensor_copy(out=pcs[r][0:6, :], in_=cs[0:6, :]), rd=[('cs',)], wr=[('pcs', r)])
            if r < 2:
                P.op('dve', lambda e, r=r: e.tensor_tensor(out=cs[0:6, :], in0=cs[0:6, :], in1=pcs[r][0:6, :],
                                                          op=ALU.subtract), rd=[('cs',), ('pcs', r)], wr=[('cs',)])
            P.op('dve', lambda e, r=r: e.tensor_scalar(out=ncs[r][0:6, :], in0=pcs[r][0:6, :], scalar1=-1.0,
                                                      scalar2=None, op0=ALU.mult), rd=[('pcs', r)], wr=[('ncs', r)])
            P.dma('sp', self.FAR[:, r, :], pcs[r][0:6, :], rd=[('pcs', r)], wr=[('FAR', r)])
            P.dma('sp', self.FAL[:, 3 + r, :], ncs[r][0:6, :], rd=[('ncs', r)], wr=[('FAL', r)])
        P.barrier()
        sb.release(m)


    class Stream:
        pass

    def emit_S(self, st, qc, buf, bufid):
        P, ps = self.P, self.ps
        if st.mode == 'causal':
            kts = range(0, 4 * qc + 4)
        else:
            kts = range(max(0, 4 * qc - 4), 4 * qc + 4)
        ktmin = kts[0] if st.mode == 'window' else 0
        for kt in kts:
            qi0 = max(kt, 4 * qc)
            qi1 = 4 * qc + 3 if st.mode == 'causal' else min(kt + 4, 4 * qc + 3)
            q0, q1 = qi0 * 128, (qi1 + 1) * 128
            n = q1 - q0
            b = self.bank()
            mms = [(st.lhs(kt), st.QT[:, q0:q1])] + st.extra(kt, q0, q1)
            for idx, (lh, rh) in enumerate(mms):
                P.op('pe', lambda e, b=b, n=n, lh=lh, rh=rh, idx=idx, last=len(mms) - 1: e.matmul(
                    ps[:, b, 0:n], lh, rh, start=(idx == 0), stop=(idx == last)),
                    rd=st.rtoks, wr=[('ps', b)])
            for qi in range(qi0, qi1 + 1):
                tab = st.table(qi - kt)
                if tab is not None:
                    a = (qi - qi0) * 128
                    P.op('dve', lambda e, b=b, a=a, tab=tab: e.tensor_tensor(
                        out=ps[:, b, a:a + 128], in0=ps[:, b, a:a + 128], in1=tab, op=ALU.add),
                        rd=[('ps', b)], wr=[('ps', b)])
            off = q0 - qc * 512
            slot = kt - ktmin
            if st.cb is not None:
                fn = lambda e, b=b, n=n, off=off, slot=slot: e.activation(
                    out=buf[:, slot, off:off + n], in_=ps[:, b, 0:n], func=AF.Exp, scale=st.scale, bias=st.cb)
            else:
                fn = lambda e, b=b, n=n, off=off, slot=slot: e.activation(
                    out=buf[:, slot, off:off + n], in_=ps[:, b, 0:n], func=AF.Exp, scale=st.scale)
            P.op('act', fn, rd=[('ps', b)], wr=[('pt', bufid, slot)])

    def emit_PV(self, st, qc, buf, bufid, ji):
        P, ps = self.P, self.ps
        ktmin = max(0, 4 * qc - 4) if st.mode == 'window' else 0
        for qi in range(4 * qc, 4 * qc + 4):
            kts = range(0, qi + 1) if st.mode == 'causal' else range(max(0, qi - 4), qi + 1)
            b = self.bank()
            a = (qi - 4 * qc) * 128
            for idx, kt in enumerate(kts):
                slot = kt - ktmin
                P.op('pe', lambda e, b=b, a=a, slot=slot, kt=kt, idx=idx, last=len(kts) - 1: e.matmul(
                    ps[:, b, 0:st.ncols], buf[:, slot, a:a + 128], st.V(kt), start=(idx == 0), stop=(idx == last)),
                    rd=[('pt', bufid, slot)] + st.vtoks, wr=[('ps', b)])
            st.epi(qi, b, ji)

    def run_jobs(self, jobs, ring):
        prev = None
        pend = None
        for i, (st, qc) in enumerate(jobs):
            self.emit_S(st, qc, ring[i % 2], i % 2)
            if prev is not None:
                pst, pqc, pi = prev
                self.emit_PV(pst, pqc, ring[pi % 2], pi % 2, pi)
                if pend is not None:
                    pend[0].post(pend[1], pend[2])
                pend = prev
            prev = (st, qc, i)
        pst, pqc, pi = prev
        self.emit_PV(pst, pqc, ring[pi % 2], pi % 2, pi)
        if pend is not None:
            pend[0].post(pend[1], pend[2])
        pst.post(pqc, pi)

    def post_transpose(self, obf, par, mixstg, k, qc, head, evac_eng):
        P, ps = self.P, self.ps
        bt = self.bank()
        psb = ps[:, bt, 0:256].bitcast(BF16)
        for j in range(4):
            P.op('pe', lambda e, j=j: e.transpose(out=psb[:, j * 128:(j + 1) * 128], in_=obf[par][:, j, :],
                                                  identity=self.identB),
                 rd=[('obf', par, j)], wr=[('ps', bt)])
        if evac_eng == 'act':
            P.op('act', lambda e: e.copy(out=mixstg[k][:, qc * 512:(qc + 1) * 512], in_=psb),
                 rd=[('ps', bt)], wr=[('mix', k, qc)])
        else:
            P.op('dve', lambda e: e.tensor_copy(out=mixstg[k][:, qc * 512:(qc + 1) * 512], in_=psb),
                 rd=[('ps', bt)], wr=[('mix', k, qc)])
        if qc == 3:
            P.dma('sp', self.MT[head], mixstg[k], rd=[('mix', k, q) for q in range(4)], wr=[('MT', head)])

    def load_V(self, Vt, k, col):
        self.P.dma('sp', Vt[:, :, 0:128], self.VT[:, :, col:col + 128].rearrange("k p d -> p k d"), wr=[('V', k)])

    def phase_B_fox(self, l, s):
        P, sb, ps = self.P, self.sb, self.ps
        m = sb.mark()
        ring = [sb.take(16 * 512, BF16).rearrange("p (k q) -> p k q", k=16) for _ in range(2)]
        KT = [sb.take(S, BF16) for _ in range(2)]
        QT = [sb.take(S, BF16) for _ in range(2)]
        AL = [sb.take(S, BF16) for _ in range(2)]
        AR = [sb.take(S, BF16) for _ in range(2)]
        V = [sb.take(16 * 130, BF16).rearrange("p (k d) -> p k d", k=16) for _ in range(2)]
        mixstg = [sb.take(S, BF16) for _ in range(2)]
        obf = [sb.take(4 * 128, BF16).rearrange("p (j d) -> p j d", j=4) for _ in range(2)]
        rv = sb.take(64, F32)
        for k in range(2):
            P.op('pool', lambda e, k=k: e.memset(AL[k], 0.0), wr=[('AL', k)])
            P.op('pool', lambda e, k=k: e.memset(AR[k], 0.0), wr=[('AR', k)])
            P.op('pool', lambda e, k=k: e.memset(V[k][:, :, 128:129], 1.0), wr=[('V', k)])
            P.op('pool', lambda e, k=k: e.memset(V[k][:, :, 129:130], 0.0), wr=[('V', k)])
        jobs = []
        sc = 128.0 ** -0.5
        for h in range(6):
            k = h % 2
            st = self.Stream()
            st.mode = 'causal'
            st.head = h
            st.k = k
            st.lhs = lambda kt, k=k: KT[k][:, kt * 128:(kt + 1) * 128]
            st.QT = QT[k]
            st.extra = lambda kt, q0, q1, k=k: [(AL[k][:, kt * 128:(kt + 1) * 128], AR[k][:, q0:q1])]
            st.table = lambda mm: self.NM0F if mm == 0 else None
            st.cb = None
            st.scale = sc
            st.V = lambda kt, k=k: V[k][:, kt, 0:129]
            st.ncols = 129
            st.rtoks = [('KT', k), ('QT', k), ('AL', k), ('AR', k)]
            st.vtoks = [('V', k)]
            st.loaded = False

            def epi(qi, b, ji):
                j = qi % 4
                par = ji % 2
                slot = (ji * 4 + j) % 64
                P.op('dve', lambda e: e.reciprocal(out=rv[:, slot:slot + 1], in_=ps[:, b, 128:129]),
                     rd=[('ps', b)], wr=[('rv', slot)])
                P.op('act', lambda e: e.activation(out=obf[par][:, j, :], in_=ps[:, b, 0:128], func=AF.Identity,
                                                   scale=rv[:, slot:slot + 1]),
                     rd=[('ps', b), ('rv', slot)], wr=[('obf', par, j)])
            st.epi = epi
            st.post = lambda qc, ji, h=h, k=k: self.post_transpose(obf, ji % 2, mixstg, k, qc, h, 'dve')
            for qc in range(4):
                jobs.append((st, qc))
        def load(h):
            k = h % 2
            P.dma('sp', KT[k], self.PT[PT_FK + h], wr=[('KT', k)])
            P.dma('sp', QT[k], self.PT[PT_FQ + h], wr=[('QT', k)])
            self.load_V(V[k], k, VT_FV + 128 * h)
            P.dma('sp', AL[k][0:6, :], self.FAL[h], wr=[('AL', k)])
            P.dma('sp', AR[k][0:6, :], self.FAR[h], wr=[('AR', k)])
        load(0)
        load(1)
        self.run_jobs_with_loads(jobs, ring, 4, lambda hi: load(hi + 2) if hi + 2 < 6 else None)
        P.barrier()
        sb.release(m)

    def run_jobs_with_loads(self, jobs, ring, per_head, after_head):
        prev = None
        pend = None
        done_hook = []
        n = len(jobs)

        def maybe_hook(pi):
            if (pi + 1) % per_head == 0:
                after_head(pi // per_head)
        for i, (st, qc) in enumerate(jobs):
            self.emit_S(st, qc, ring[i % 2], i % 2)
            if prev is not None:
                pst, pqc, pi = prev
                self.emit_PV(pst, pqc, ring[pi % 2], pi % 2, pi)
                if pend is not None:
                    pend[0].post(pend[1], pend[2])
                    maybe_hook(pend[2])
                pend = prev
            prev = (st, qc, i)
        pst, pqc, pi = prev
        self.emit_PV(pst, pqc, ring[pi % 2], pi % 2, pi)
        if pend is not None:
            pend[0].post(pend[1], pend[2])
            maybe_hook(pend[2])
        pst.post(pqc, pi)
        maybe_hook(pi)

    def phase_B_diff(self, l, s):
        P, sb, ps = self.P, self.sb, self.ps
        m = sb.mark()
        ring = [sb.take(16 * 512, BF16).rearrange("p (k q) -> p k q", k=16) for _ in range(2)]
        KT0 = [sb.take(S, BF16) for _ in range(2)]
        KT1 = [sb.take(S, BF16) for _ in range(2)]
        QT = [sb.take(S, BF16) for _ in range(2)]
        V = [sb.take(16 * 130, BF16).rearrange("p (k d) -> p k d", k=16) for _ in range(2)]
        mixstg = [sb.take(S, BF16) for _ in range(2)]
        obf = [sb.take(4 * 128, BF16).rearrange("p (j d) -> p j d", j=4) for _ in range(2)]
        t0 = [sb.take(4 * 128, F32).rearrange("p (j d) -> p j d", j=4) for _ in range(2)]
        od = [sb.take(128, F32) for _ in range(2)]
        sqd = [sb.take(128, F32) for _ in range(2)]
        rv = sb.take(64, F32)
        r1 = sb.take(64, F32)
        ssq = sb.take(64, F32)
        gsub = sb.take(128, F32)
        lamb = sb.take(256, F32)
        pr = sb.take(128, F32)
        s12 = sb.take(2, F32)
        nlam = sb.take(1, F32)
        lam_init = 0.8 - 0.6 * math.exp(-0.3 * l)
        P.dma('sp', lamb, self.lam4[l:l + 1].rearrange("o a d -> o (a d)").partition_broadcast(128), wr=[('lamb',)])
        P.dma('sp', gsub, self.diff_subln_g[l:l + 1, :].partition_broadcast(128), wr=[('gsub',)])
        P.op('dve', lambda e: e.tensor_scalar(out=gsub, in0=gsub, scalar1=1.0 - lam_init, scalar2=None, op0=ALU.mult),
             rd=[('gsub',)], wr=[('gsub',)])
        P.op('dve', lambda e: e.tensor_tensor(out=pr[:, 0:64], in0=lamb[:, 0:64], in1=lamb[:, 64:128], op=ALU.mult),
             rd=[('lamb',)], wr=[('pr', 0)])
        P.op('dve', lambda e: e.tensor_tensor(out=pr[:, 64:128], in0=lamb[:, 128:192], in1=lamb[:, 192:256], op=ALU.mult),
             rd=[('lamb',)], wr=[('pr', 1)])
        P.op('dve', lambda e: e.tensor_reduce(out=s12, in_=pr.rearrange("p (a d) -> p a d", a=2), axis=AX.X, op=ALU.add),
             rd=[('pr', 0), ('pr', 1)], wr=[('s12',)])
        P.op('act', lambda e: e.activation(out=s12, in_=s12, func=AF.Exp), rd=[('s12',)], wr=[('s12',)])
        P.op('dve', lambda e: e.tensor_tensor(out=nlam, in0=s12[:, 1:2], in1=s12[:, 0:1], op=ALU.subtract),
             rd=[('s12',)], wr=[('nlam',)])
        P.op('dve', lambda e: e.tensor_scalar(out=nlam, in0=nlam, scalar1=-lam_init, scalar2=None, op0=ALU.add),
             rd=[('nlam',)], wr=[('nlam',)])
        for k in range(2):
            P.op('pool', lambda e, k=k: e.memset(KT0[k][64:128, :], 0.0), wr=[('KT0z', k)])
            P.op('pool', lambda e, k=k: e.memset(KT1[k][0:64, :], 0.0), wr=[('KT1z', k)])
            P.op('pool', lambda e, k=k: e.memset(V[k][:, :, 128:129], 1.0), wr=[('V', k)])
            P.op('pool', lambda e, k=k: e.memset(V[k][:, :, 129:130], 0.0), wr=[('V', k)])
        jobs = []
        for h in range(4):
            k = h % 2
            for mp in range(2):
                st = self.Stream()
                st.mode = 'causal'
                KTm = KT0 if mp == 0 else KT1
                st.lhs = lambda kt, k=k, KTm=KTm: KTm[k][:, kt * 128:(kt + 1) * 128]
                st.QT = QT[k]
                st.extra = lambda kt, q0, q1: []
                st.table = lambda mm, h=h: self.BT[:, h, 0, :] if mm == 0 else (self.BT[:, h, 1, :] if mm == 1 else None)
                st.cb = self.CB[:, h:h + 1]
                st.scale = 0.125
                st.V = lambda kt, k=k: V[k][:, kt, 0:129]
                st.ncols = 129
                st.rtoks = [('KT0', k), ('KT1', k), ('KT0z', k), ('KT1z', k), ('QT', k)]
                st.vtoks = [('V', k)]
                if mp == 0:
                    def epi(qi, b, ji):
                        j = qi % 4
                        par = (ji // 2) % 2
                        slot = (ji * 4 + j) % 64
                        P.op('dve', lambda e: e.reciprocal(out=rv[:, slot:slot + 1], in_=ps[:, b, 128:129]),
                             rd=[('ps', b)], wr=[('rv', slot)])
                        P.op('act', lambda e: e.activation(out=t0[par][:, j, :], in_=ps[:, b, 0:128], func=AF.Identity,
                                                           scale=rv[:, slot:slot + 1]),
                             rd=[('ps', b), ('rv', slot)], wr=[('t0', par, j)])
                    st.epi = epi
                    st.post = lambda qc, ji: None
                else:
                    def epi(qi, b, ji):
                        j = qi % 4
                        par = (ji // 2) % 2
                        slot = (ji * 4 + j) % 64
                        o2 = (ji * 4 + j) % 2
                        P.op('dve', lambda e: e.reciprocal(out=r1[:, slot:slot + 1], in_=ps[:, b, 128:129]),
                             rd=[('ps', b)], wr=[('r1', slot)])
                        P.op('dve', lambda e: e.tensor_tensor(out=r1[:, slot:slot + 1], in0=r1[:, slot:slot + 1],
                                                              in1=nlam, op=ALU.mult),
                             rd=[('r1', slot), ('nlam',)], wr=[('r1', slot)])
                        P.op('dve', lambda e: e.scalar_tensor_tensor(out=od[o2], in0=ps[:, b, 0:128],
                                                                     scalar=r1[:, slot:slot + 1], in1=t0[par][:, j, :],
                                                                     op0=ALU.mult, op1=ALU.add),
                             rd=[('ps', b), ('r1', slot), ('t0', par, j)], wr=[('od', o2)])
                        P.op('pool', lambda e: e.tensor_tensor(out=sqd[o2], in0=od[o2], in1=od[o2], op=ALU.mult),
                             rd=[('od', o2)], wr=[('sqd', o2)])
                        P.op('dve', lambda e: e.tensor_reduce(out=ssq[:, slot:slot + 1], in_=sqd[o2], axis=AX.X, op=ALU.add),
                             rd=[('sqd', o2)], wr=[('ssq', slot)])
                        P.op('dve', lambda e: e.tensor_scalar(out=ssq[:, slot:slot + 1], in0=ssq[:, slot:slot + 1],
                                                              scalar1=EPS * 128.0, scalar2=1.0 / 128.0, op0=ALU.add, op1=ALU.mult),
                             rd=[('ssq', slot)], wr=[('ssq', slot)])
                        P.op('act', lambda e: e.activation(out=ssq[:, slot:slot + 1], in_=ssq[:, slot:slot + 1], func=AF.Sqrt),
                             rd=[('ssq', slot)], wr=[('ssq', slot)])
                        P.op('dve', lambda e: e.reciprocal(out=ssq[:, slot:slot + 1], in_=ssq[:, slot:slot + 1]),
                             rd=[('ssq', slot)], wr=[('ssq', slot)])
                        P.op('dve', lambda e: e.scalar_tensor_tensor(out=obf[par][:, j, :], in0=od[o2],
                                                                     scalar=ssq[:, slot:slot + 1], in1=gsub,
                                                                     op0=ALU.mult, op1=ALU.mult),
                             rd=[('od', o2), ('ssq', slot), ('gsub',)], wr=[('obf', par, j)])
                    st.epi = epi
                    st.post = lambda qc, ji, h=h, k=k: self.post_transpose(obf, (ji // 2) % 2, mixstg, k, qc, 6 + h, 'act')
                st.mp = mp
                jobs.append(st)
        joblist = []
        for h in range(4):
            for qc in range(4):
                joblist.append((jobs[2 * h], qc))
                joblist.append((jobs[2 * h + 1], qc))

        def load(h):
            k = h % 2
            P.dma('sp', KT0[k][0:64, :], self.PT[PT_DK + h, 0:64, :], wr=[('KT0', k)])
            P.dma('sp', KT1[k][64:128, :], self.PT[PT_DK + h, 64:128, :], wr=[('KT1', k)])
            P.dma('sp', QT[k], self.PT[PT_DQ + h], wr=[('QT', k)])
            self.load_V(V[k], k, VT_DV + 128 * h)
        load(0)
        load(1)
        self.run_jobs_with_loads(joblist, ring, 8, lambda hi: load(hi + 2) if hi + 2 < 4 else None)
        P.barrier()
        sb.release(m)

    def phase_B_nsa(self, l, s):
        P, sb, ps = self.P, self.sb, self.ps
        m = sb.mark()
        sc = 128.0 ** -0.5
        NB = 4096.0
        kcmpT = [sb.take(128, BF16) for _ in range(2)]
        Rt = [sb.take(162, BF16) for _ in range(2)]
        gates = sb.take(16 * 18, F32).rearrange("p (a c) -> p a c", a=16)
        P.dma('sp', gates, self.GT.rearrange("a p c -> p a c"), wr=[('gates',)])
        for g in range(2):
            P.op('pool', lambda e, g=g: e.memset(Rt[g][:, 128:129], 1.0), wr=[('Rt1', g)])
            P.op('dve', lambda e, g=g: e.tensor_copy(out=Rt[g][0:NCMP, 129:161], in_=self.OVb[0:NCMP, :]), wr=[('Rt2', g)])
        m1 = sb.mark()
        w1 = [sb.take(32 * 128, BF16).rearrange("p (l j) -> p l j", l=32) for _ in range(2)]
        w2 = [sb.take(128, BF16) for _ in range(2)]
        posT = sb.take(32, BF16)
        xcT = [sb.take(S, BF16) for _ in range(2)]
        pb = sb.take(2, F32)
        xs = sb.take(128, F32)
        x2 = sb.take(128, F32)
        yy = sb.take(128, F32)
        gT = sb.take(128, BF16)
        for kv in range(2):
            P.dma('pool', w1[kv], self.nsa_cmp_w1[l, kv].rearrange("(l d) j -> d l j", d=128), wr=[('w1', kv)])
            P.dma('pool', w2[kv], self.nsa_cmp_w2[l, kv], wr=[('w2', kv)])
        P.dma('pool', posT, self.nsa_cmp_pos[l].rearrange("l d -> d l"), wr=[('posT',)], slow=True)
        for kv in range(2):
            b2 = self.bank()
            for li in range(32):
                P.op('pe', lambda e, b2=b2, kv=kv, li=li: e.matmul(ps[:, b2, 0:1], w1[kv][:, li, :], posT[:, li:li + 1],
                                                                  start=(li == 0), stop=(li == 31)),
                     rd=[('w1', kv), ('posT',)], wr=[('ps', b2)])
            P.op('dve', lambda e, b2=b2, kv=kv: e.tensor_copy(out=pb[:, kv:kv + 1], in_=ps[:, b2, 0:1]),
                 rd=[('ps', b2)], wr=[('pb', kv)])
        n = 0
        for g in range(2):
            for kv in range(2):
                xc = xcT[n % 2]
                xk = n % 2
                n += 1
                P.dma('sp', xc, self.PT[(PT_NKC if kv == 0 else PT_NVC) + g], wr=[('xc', xk)])
                b = self.bank()
                for li in range(32):
                    P.op('pe', lambda e, b=b, kv=kv, li=li, xc=xc: e.matmul(
                        ps[:, b, 0:NCMP], w1[kv][:, li, :], xc[:, li:li + 16 * (NCMP - 1) + 1:16],
                        start=(li == 0), stop=(li == 31)), rd=[('w1', kv), ('xc', xk)], wr=[('ps', b)])
                P.op('act', lambda e, b=b, kv=kv: e.activation(out=xs[:, 0:NCMP], in_=ps[:, b, 0:NCMP], func=AF.Identity,
                                                              bias=pb[:, kv:kv + 1]),
                     rd=[('ps', b), ('pb', kv)], wr=[('xs',)])
                P.op('dve', lambda e: e.tensor_tensor(out=x2[:, 0:NCMP], in0=xs[:, 0:NCMP], in1=xs[:, 0:NCMP], op=ALU.mult),
                     rd=[('xs',)], wr=[('x2',)])
                P.op('dve', lambda e: e.tensor_scalar(out=x2[:, 0:NCMP], in0=x2[:, 0:NCMP], scalar1=0.044715, scalar2=1.0,
                                                      op0=ALU.mult, op1=ALU.add), rd=[('x2',)], wr=[('x2',)])
                P.op('dve', lambda e: e.tensor_tensor(out=yy[:, 0:NCMP], in0=x2[:, 0:NCMP], in1=xs[:, 0:NCMP], op=ALU.mult),
                     rd=[('x2',), ('xs',)], wr=[('yy',)])
                P.op('act', lambda e: e.activation(out=yy[:, 0:NCMP], in_=yy[:, 0:NCMP], func=AF.Sigmoid,
                                                   scale=1.5957691216057308), rd=[('yy',)], wr=[('yy',)])
                P.op('dve', lambda e: e.tensor_tensor(out=gT[:, 0:NCMP], in0=xs[:, 0:NCMP], in1=yy[:, 0:NCMP], op=ALU.mult),
                     rd=[('xs',), ('yy',)], wr=[('gT',)])
                b3 = self.bank()
                if kv == 0:
                    P.op('pe', lambda e, b3=b3: e.matmul(ps[:, b3, 0:NCMP], w2[0], gT[:, 0:NCMP], start=True, stop=True),
                         rd=[('w2', 0), ('gT',)], wr=[('ps', b3)])
                    P.op('dve', lambda e, b3=b3, g=g: e.tensor_copy(out=kcmpT[g][:, 0:NCMP], in_=ps[:, b3, 0:NCMP]),
                         rd=[('ps', b3)], wr=[('kcmpT', g)])
                else:
                    P.op('pe', lambda e, b3=b3: e.matmul(ps[0:NCMP, b3, 0:128], gT[:, 0:NCMP], w2[1], start=True, stop=True),
                         rd=[('w2', 1), ('gT',)], wr=[('ps', b3)])
                    P.op('dve', lambda e, b3=b3, g=g: e.tensor_copy(out=Rt[g][0:NCMP, 0:128], in_=ps[0:NCMP, b3, 0:128]),
                         rd=[('ps', b3)], wr=[('Rt0', g)])
        P.barrier()
        sb.release(m1)
        ring = [sb.take(16 * 512, BF16).rearrange("p (k q) -> p k q", k=16) for _ in range(2)]
        QT3 = [sb.take(S, BF16) for _ in range(3)]
        KS = sb.take(S, BF16)
        KW = sb.take(S, BF16)
        VS = sb.take(16 * 130, BF16).rearrange("p (k d) -> p k d", k=16)
        VW = sb.take(16 * 130, BF16).rearrange("p (k d) -> p k d", k=16)
        ET = sb.take(S, BF16)
        Yc = sb.take(S, F32)
        Ycr = Yc[:, ::-1]
        negselT = sb.take(S, BF16)
        imp = sb.take(16 * 32, F32).rearrange("p (a m) -> p a m", a=16)
        ocmp = [sb.take(16 * 128, F32).rearrange("p (a d) -> p a d", a=16) for _ in range(3)]
        mixstg = [sb.take(S, BF16) for _ in range(2)]
        acc = [sb.take(4 * 128, F32).rearrange("p (j d) -> p j d", j=4) for _ in range(2)]
        obf = [sb.take(4 * 128, BF16).rearrange("p (j d) -> p j d", j=4) for _ in range(2)]
        rc = sb.take(64, F32)
        rg = sb.take(64, F32)
        rv = sb.take(64, F32)
        scb = [sb.take(32, F32) for _ in range(2)]
        selm = [sb.take(32, F32) for _ in range(2)]
        mx = sb.take(16 * 8, F32).rearrange("p (a m) -> p a m", a=16)
        nsb = [sb.take(32, BF16) for _ in range(8)]
        P.op('pool', lambda e: e.memset(negselT, 0.0), wr=[('nsT', q) for q in range(4)])
        for Vt, nm in ((VS, 'VS'), (VW, 'VW')):
            P.op('pool', lambda e, Vt=Vt: e.memset(Vt[:, :, 128:129], 1.0), wr=[(nm,)])
            P.op('pool', lambda e, Vt=Vt: e.memset(Vt[:, :, 129:130], 0.0), wr=[(nm,)])
        for g in range(2):
            P.dma('sp', KS, self.PT[PT_NKS + g], wr=[('KS',)])
            P.dma('sp', KW, self.PT[PT_NKW + g], wr=[('KW',)])
            P.dma('sp', VS[:, :, 0:128], self.VT[:, :, VT_NVS + 128 * g:VT_NVS + 128 * (g + 1)].rearrange("k p d -> p k d"),
                  wr=[('VS',)])
            P.dma('sp', VW[:, :, 0:128], self.VT[:, :, VT_NVW + 128 * g:VT_NVW + 128 * (g + 1)].rearrange("k p d -> p k d"),
                  wr=[('VW',)])
            cnt = 0
            for hh in range(3):
                h = 3 * g + hh
                P.dma('sp', QT3[hh], self.PT[PT_NQ + h], wr=[('QT3', hh)])
                P.dma('sp', Yc[0:NCMP, :], bass.AP(tensor=self.TCr_h, offset=(4 + h) * 4352 + 287,
                                                   ap=[[16, NCMP], [1, S]]), wr=[('Yc',)])
                for qc in range(4):
                    b = self.bank()
                    q0, q1 = qc * 512, (qc + 1) * 512
                    P.op('pe', lambda e, b=b, hh=hh, q0=q0, q1=q1, g=g: e.matmul(
                        ps[0:NCMP, b, :], kcmpT[g][:, 0:NCMP], QT3[hh][:, q0:q1], start=True, stop=True),
                        rd=[('kcmpT', g), ('QT3', hh)], wr=[('ps', b)])
                    P.op('dve', lambda e, b=b, q0=q0, q1=q1: e.tensor_tensor(
                        out=ps[0:NCMP, b, :], in0=ps[0:NCMP, b, :], in1=Ycr[0:NCMP, q0:q1], op=ALU.add),
                        rd=[('ps', b), ('Yc',)], wr=[('ps', b)])
                    P.op('act', lambda e, b=b, q0=q0, q1=q1, h=h: e.activation(
                        out=ET[0:NCMP, q0:q1], in_=ps[0:NCMP, b, :], func=AF.Exp, scale=sc,
                        bias=self.CB[0:NCMP, 4 + h:5 + h]), rd=[('ps', b)], wr=[('ET', qc)])
                for qi in range(16):
                    b = self.bank()
                    slot = cnt % 64
                    cnt += 1
                    P.op('pe', lambda e, b=b, qi=qi, g=g: e.matmul(
                        ps[:, b, 0:161], ET[0:NCMP, qi * 128:(qi + 1) * 128], Rt[g][0:NCMP, 0:161], start=True, stop=True),
                        rd=[('ET', qi // 4), ('Rt0', g), ('Rt1', g), ('Rt2', g)], wr=[('ps', b)])
                    P.op('dve', lambda e, b=b, slot=slot: e.tensor_scalar(
                        out=rc[:, slot:slot + 1], in0=ps[:, b, 128:129], scalar1=1e-30, scalar2=None, op0=ALU.max),
                        rd=[('ps', b)], wr=[('rc', slot)])
                    P.op('dve', lambda e, slot=slot: e.reciprocal(out=rc[:, slot:slot + 1], in_=rc[:, slot:slot + 1]),
                         rd=[('rc', slot)], wr=[('rc', slot)])
                    P.op('dve', lambda e, slot=slot, qi=qi, h=h: e.tensor_tensor(
                        out=rg[:, slot:slot + 1], in0=rc[:, slot:slot + 1], in1=gates[:, qi, 3 * h:3 * h + 1], op=ALU.mult),
                        rd=[('rc', slot), ('gates',)], wr=[('rg', slot)])
                    P.op('act', lambda e, b=b, slot=slot, qi=qi, hh=hh: e.activation(
                        out=ocmp[hh][:, qi, :], in_=ps[:, b, 0:128], func=AF.Identity, scale=rg[:, slot:slot + 1]),
                        rd=[('ps', b), ('rg', slot)], wr=[('ocmp', hh, qi)])
                    if hh == 0:
                        P.op('dve', lambda e, b=b, slot=slot, qi=qi: e.tensor_scalar(
                            out=imp[:, qi, :], in0=ps[:, b, 129:161], scalar1=rc[:, slot:slot + 1], scalar2=None,
                            op0=ALU.mult), rd=[('ps', b), ('rc', slot)], wr=[('imp', qi)])
                    else:
                        P.op('dve', lambda e, b=b, slot=slot, qi=qi: e.scalar_tensor_tensor(
                            out=imp[:, qi, :], in0=ps[:, b, 129:161], scalar=rc[:, slot:slot + 1], in1=imp[:, qi, :],
                            op0=ALU.mult, op1=ALU.add), rd=[('ps', b), ('rc', slot), ('imp', qi)], wr=[('imp', qi)])
            for qc in range(4):
                bt = self.bank()
                psb = ps[0:32, bt, 0:256].bitcast(BF16)
                for j in range(4):
                    qi = 4 * qc + j
                    k2 = qi % 2
                    k8 = qi % 8
                    P.op('dve', lambda e, qi=qi, k2=k2: e.tensor_tensor(out=scb[k2], in0=imp[:, qi, :], in1=self.FM[:, qi, :],
                                                                        op=ALU.add), rd=[('imp', qi)], wr=[('scb', k2)])
                    P.op('dve', lambda e, qi=qi, k2=k2: e.max(out=mx[:, qi, :], in_=scb[k2]),
                         rd=[('scb', k2)], wr=[('mx', qi)])
                    P.op('dve', lambda e, qi=qi, k2=k2: e.tensor_scalar(out=selm[k2], in0=scb[k2], scalar1=mx[:, qi, 7:8],
                                                                        scalar2=None, op0=ALU.is_ge),
                         rd=[('scb', k2), ('mx', qi)], wr=[('selm', k2)])
                    P.op('dve', lambda e, k2=k2, k8=k8: e.tensor_scalar(out=nsb[k8], in0=selm[k2], scalar1=-1.0, scalar2=NB,
                                                                        op0=ALU.add, op1=ALU.mult),
                         rd=[('selm', k2)], wr=[('nsb', k8)])
                    P.op('pe', lambda e, j=j, k8=k8, psb=psb: e.transpose(out=psb[:, j * 128:(j + 1) * 128], in_=nsb[k8],
                                                                          identity=self.identB),
                         rd=[('nsb', k8)], wr=[('ps', bt)])
                P.op('act', lambda e, psb=psb, qc=qc: e.copy(out=negselT[0:32, qc * 512:(qc + 1) * 512], in_=psb),
                     rd=[('ps', bt)], wr=[('nsT', qc)])
            joblist = []
            for hh in range(3):
                h = 3 * g + hh
                tab = lambda mm, h=h: (self.BT[:, 4 + h, 0, :] if mm == 0 else (self.BT[:, 4 + h, 1, :] if mm == 1 else None))
                ss = self.Stream()
                ss.mode = 'causal'
                ss.lhs = lambda kt: KS[:, kt * 128:(kt + 1) * 128]
                ss.QT = QT3[hh]
                ss.extra = lambda kt, q0, q1: [(self.Epad[:, kt * 128:(kt + 1) * 128], negselT[:, q0:q1])]
                ss.table = tab
                ss.cb = self.CB[:, 4 + h:5 + h]
                ss.scale = sc
                ss.V = lambda kt: VS[:, kt, 0:129]
                ss.ncols = 129
                ss.rtoks = [('KS',), ('QT3', hh)] + [('nsT', q) for q in range(4)]
                ss.vtoks = [('VS',)]

                def epi_s(qi, b, ji, h=h, hh=hh):
                    j = qi % 4
                    par = (ji // 2) % 2
                    slot = (ji * 4 + j) % 64
                    P.op('dve', lambda e: e.reciprocal(out=rv[:, slot:slot + 1], in_=ps[:, b, 128:129]),
                         rd=[('ps', b)], wr=[('rv', slot)])
                    P.op('dve', lambda e: e.tensor_tensor(out=rv[:, slot:slot + 1], in0=rv[:, slot:slot + 1],
                                                          in1=gates[:, qi, 3 * h + 1:3 * h + 2], op=ALU.mult),
                         rd=[('rv', slot), ('gates',)], wr=[('rv', slot)])
                    P.op('dve', lambda e: e.scalar_tensor_tensor(out=acc[par][:, j, :], in0=ps[:, b, 0:128],
                                                                 scalar=rv[:, slot:slot + 1], in1=ocmp[hh][:, qi, :],
                                                                 op0=ALU.mult, op1=ALU.add),
                         rd=[('ps', b), ('rv', slot), ('ocmp', hh, qi)], wr=[('acc', par, j)])
                ss.epi = epi_s
                ss.post = lambda qc, ji: None
                sw = self.Stream()
                sw.mode = 'window'
                sw.lhs = lambda kt: KW[:, kt * 128:(kt + 1) * 128]
                sw.QT = QT3[hh]
                sw.extra = lambda kt, q0, q1: []
                sw.table = lambda mm, tab=tab: (self.W4F if mm == 4 else tab(mm))
                sw.cb = self.CB[:, 4 + h:5 + h]
                sw.scale = sc
                sw.V = lambda kt: VW[:, kt, 0:129]
                sw.ncols = 129
                sw.rtoks = [('KW',), ('QT3', hh)]
                sw.vtoks = [('VW',)]

                def epi_w(qi, b, ji, h=h, hh=hh):
                    j = qi % 4
                    par = (ji // 2) % 2
                    slot = (ji * 4 + j) % 64
                    P.op('dve', lambda e: e.reciprocal(out=rv[:, slot:slot + 1], in_=ps[:, b, 128:129]),
                         rd=[('ps', b)], wr=[('rv', slot)])
                    P.op('dve', lambda e: e.tensor_tensor(out=rv[:, slot:slot + 1], in0=rv[:, slot:slot + 1],
                                                          in1=gates[:, qi, 3 * h + 2:3 * h + 3], op=ALU.mult),
                         rd=[('rv', slot), ('gates',)], wr=[('rv', slot)])
                    P.op('dve', lambda e: e.scalar_tensor_tensor(out=obf[par][:, j, :], in0=ps[:, b, 0:128],
                                                                 scalar=rv[:, slot:slot + 1], in1=acc[par][:, j, :],
                                                                 op0=ALU.mult, op1=ALU.add),
                         rd=[('ps', b), ('rv', slot), ('acc', par, j)], wr=[('obf', par, j)])
                sw.epi = epi_w
                sw.post = lambda qc, ji, h=h, hh=hh: self.post_transpose(obf, (ji // 2) % 2, mixstg, hh % 2, qc, 10 + h, 'act')
                for qc in range(4):
                    joblist.append((ss, qc))
                    joblist.append((sw, qc))
            self.run_jobs(joblist, ring)
        P.barrier()
        sb.release(m)

    def phase_C(self, l, s):
        P, sb, ps = self.P, self.sb, self.ps
        m = sb.mark()
        csub = "1234"
        mixT = sb.take(NCH * S, BF16).rearrange("p (c t) -> p c t", c=NCH)
        for hd in range(16 if "1" in csub else 0):
            P.dma('sp', mixT[:, hd, :], self.MT[hd], wr=[('mixT', hd)])
        wbuf = [sb.take(NCH * 512, BF16).rearrange("p (c f) -> p c f", c=NCH) for _ in range(2)]
        hold = [sb.take(S, F32) for _ in range(2)]
        hnew = [sb.take(S, F32) for _ in range(2)]
        n = 0
        for i in range(4 if "1" in csub else 0):
            wb = wbuf[i % 2]
            wtoks = self.load_w_slab(wb, i % 2, self.w_out[l], [(512 * i, 512)])
            for ft in range(4):
                co = 4 * i + ft
                k2 = n % 2
                n += 1
                P.dma('sp', hold[k2], self.hT[s, :, co, :], rd=[('hTc', co)], wr=[('hold', k2)])
                for j in range(4):
                    b = self.bank()
                    for c in range(NCH):
                        P.op('pe', lambda e, b=b, c=c, wb=wb, ft=ft, j=j: e.matmul(
                            ps[:, b, :], wb[:, c, ft * 128:(ft + 1) * 128], mixT[:, c, j * 512:(j + 1) * 512],
                            start=(c == 0), stop=(c == NCH - 1)), rd=wtoks + [('mixT', c)], wr=[('ps', b)])
                    P.op('dve', lambda e, b=b, k2=k2, j=j: e.tensor_tensor(
                        out=hnew[k2][:, j * 512:(j + 1) * 512], in0=ps[:, b, :], in1=hold[k2][:, j * 512:(j + 1) * 512],
                        op=ALU.add), rd=[('ps', b), ('hold', k2)], wr=[('hnew', k2, j)])
                P.dma('sp', self.hT[s, :, co, :], hnew[k2], rd=[('hnew', k2, j) for j in range(4)], wr=[('hTc', co)])
        P.barrier()
        sb.release(m)
        u2T = sb.take(NCH * S, BF16).rearrange("p (c t) -> p c t", c=NCH)
        m2 = sb.mark()
        if "2" in csub:
            self.rmsnorm_to_uT(lambda j: ('none',), self.hT[s], self.g2, l * NCH, u2T)
        P.barrier()
        sb.release(m2)
        cw = sb.take(3 * 88, F32).rearrange("p (t f) -> p t f", t=3)
        cbv = sb.take(88, F32)
        m3 = sb.mark()
        cwA = sb.take(4 * 128, F32).rearrange("p (t q) -> p t q", t=4)
        P.dma('sp', cwA[0:88, 0:3, :], self.ffn_conv_w[l].rearrange("t (f p) -> f t p", p=128), wr=[('cwA', 0)])
        P.dma('sp', cwA[0:88, 3, :], self.ffn_conv_b[l].rearrange("(f p) -> f p", p=128), wr=[('cwA', 1)])
        for t in range(4):
            b = self.bank()
            P.op('pe', lambda e, b=b, t=t: e.transpose(out=ps[:, b, 0:88], in_=cwA[0:88, t, :], identity=self.identF[0:88, 0:88]),
                 rd=[('cwA', 0), ('cwA', 1)], wr=[('ps', b)])
            if t < 3:
                P.op('dve', lambda e, b=b, t=t: e.tensor_copy(out=cw[:, t, :], in_=ps[:, b, 0:88]), rd=[('ps', b)], wr=[('cw',)])
            else:
                P.op('dve', lambda e, b=b: e.tensor_copy(out=cbv, in_=ps[:, b, 0:88]), rd=[('ps', b)], wr=[('cw',)])
        P.barrier()
        sb.release(m3)
        wbuf = [sb.take(NCH * 512, BF16).rearrange("p (c f) -> p c f", c=NCH) for _ in range(2)]
        hgu = [sb.take(S + 2, F32) for _ in range(4)]
        cgu = [sb.take(S, F32) for _ in range(2)]
        aT = [sb.take(S, BF16) for _ in range(2)]
        for k in range(4):
            P.op('pool', lambda e, k=k: e.memset(hgu[k][:, 0:2], 0.0), wr=[('hgz', k)])
        utoks = lambda c, t0, t1: [('uT', c, i) for i in range(t0 // 256, (t1 + 255) // 256)]
        nt = 0
        for i in range(22 if "3" in csub else 0):
            wb = wbuf[i % 2]
            wtoks = self.load_w_slab(wb, i % 2, self.ffn_w_up[l], [(256 * i, 256), (DFF + 256 * i, 256)])
            for ft in range(2):
                f = 2 * i + ft
                par = nt % 2
                nt += 1
                for br in range(2):
                    hb = hgu[2 * par + br]
                    hk = 2 * par + br
                    wcol = br * 256 + ft * 128
                    fi = f + 44 * br
                    for j in range(4):
                        b = self.bank()
                        for c in range(NCH):
                            P.op('pe', lambda e, b=b, c=c, wb=wb, wcol=wcol, j=j: e.matmul(
                                ps[:, b, :], wb[:, c, wcol:wcol + 128], u2T[:, c, j * 512:(j + 1) * 512],
                                start=(c == 0), stop=(c == NCH - 1)), rd=wtoks, wr=[('ps', b)])
                        P.op('act', lambda e, b=b, hb=hb, j=j: e.copy(out=hb[:, 2 + j * 512:2 + (j + 1) * 512], in_=ps[:, b, :]),
                             rd=[('ps', b), ('hgz', hk)], wr=[('hgu', hk, j)])
                    cg = cgu[br]
                    htok = [('hgu', hk, j) for j in range(4)]
                    P.op('dve', lambda e, hb=hb, cg=cg, fi=fi: e.tensor_scalar(
                        out=cg, in0=hb[:, 2:S + 2], scalar1=cw[:, 2, fi:fi + 1], scalar2=cbv[:, fi:fi + 1],
                        op0=ALU.mult, op1=ALU.add), rd=htok, wr=[('cgu', br)])
                    P.op('dve', lambda e, hb=hb, cg=cg, fi=fi: e.scalar_tensor_tensor(
                        out=cg, in0=hb[:, 1:S + 1], scalar=cw[:, 1, fi:fi + 1], in1=cg, op0=ALU.mult, op1=ALU.add),
                        rd=htok + [('cgu', br)], wr=[('cgu', br)])
                    P.op('dve', lambda e, hb=hb, cg=cg, fi=fi: e.scalar_tensor_tensor(
                        out=cg, in0=hb[:, 0:S], scalar=cw[:, 0, fi:fi + 1], in1=cg, op0=ALU.mult, op1=ALU.add),
                        rd=htok + [('cgu', br)], wr=[('cgu', br)])
                P.op('act', lambda e: e.activation(out=cgu[0], in_=cgu[0], func=AF.Silu), rd=[('cgu', 0)], wr=[('cgu', 0)])
                P.op('dve', lambda e, par=par: e.tensor_tensor(out=aT[par], in0=cgu[0], in1=cgu[1], op=ALU.mult),
                     rd=[('cgu', 0), ('cgu', 1)], wr=[('aT', par)])
                P.dma('sp', self.AT[f], aT[par], rd=[('aT', par)], wr=[('AT', f)])
        P.barrier()
        sb.release(m)
        m = sb.mark()
        actH = sb.take(NFF * 1024, BF16).rearrange("p (f t) -> p f t", f=NFF)
        wd = [sb.take(NFF * 256, BF16).rearrange("p (f c) -> p f c", f=NFF) for _ in range(2)]
        hold4 = [sb.take(1024, F32) for _ in range(2)]
        hnew4 = [sb.take(1024, F32) for _ in range(2)]
        nw = 0
        n = 0
        for half in range(2 if "4" in csub else 0):
            t0 = half * 1024
            for f in range(NFF):
                P.dma('sp', actH[:, f, :], self.AT[f, :, t0:t0 + 1024], wr=[('actH', f)])
            for i in range(8):
                wk = nw % 2
                nw += 1
                P.dma('pool', wd[wk], self.ffn_w_down[l, :, 256 * i:256 * (i + 1)].rearrange("(fc p) c -> p fc c", p=128),
                      wr=[('wd', wk)])
                for ft in range(2):
                    co = 2 * i + ft
                    k2 = n % 2
                    n += 1
                    P.dma('sp', hold4[k2], self.hT[s, :, co, t0:t0 + 1024], rd=[('hTd', co, half)], wr=[('hold4', k2)])
                    for j in range(2):
                        b = self.bank()
                        for fc in range(NFF):
                            P.op('pe', lambda e, b=b, fc=fc, wk=wk, ft=ft, j=j: e.matmul(
                                ps[:, b, :], wd[wk][:, fc, ft * 128:(ft + 1) * 128], actH[:, fc, j * 512:(j + 1) * 512],
                                start=(fc == 0), stop=(fc == NFF - 1)), rd=[('wd', wk), ('actH', fc)], wr=[('ps', b)])
                        P.op('dve', lambda e, b=b, k2=k2, j=j: e.tensor_tensor(
                            out=hnew4[k2][:, j * 512:(j + 1) * 512], in0=ps[:, b, :], in1=hold4[k2][:, j * 512:(j + 1) * 512],
                            op=ALU.add), rd=[('ps', b), ('hold4', k2)], wr=[('hnew4', k2, j)])
                    P.dma('sp', self.hT[s, :, co, t0:t0 + 1024], hnew4[k2], rd=[('hnew4', k2, j) for j in range(2)],
                          wr=[('hTd', co, half)])
        P.barrier()
        sb.release(m)

    def phase_Z(self, s):
        P, sb, ps = self.P, self.sb, self.ps
        m = sb.mark()
        yb = [sb.take(NCH * 256, F32).rearrange("p (c t) -> p c t", c=NCH) for _ in range(2)]
        ostg = [sb.take(D, F32) for _ in range(2)]
        cnt = [0]

        def out_cb(i, hb, rstd):
            y = yb[i % 2]
            for c in range(NCH):
                P.op('dve', lambda e, c=c: e.scalar_tensor_tensor(
                    out=y[:, c, :], in0=hb[:, c, :], scalar=self.g3[:, c:c + 1], in1=rstd, op0=ALU.mult, op1=ALU.mult),
                    rd=[('hb', i % 2), ('rstd',)], wr=[('yb', i % 2, c)])
            for tt in range(2):
                ok = cnt[0] % 2
                cnt[0] += 1
                for cg4 in range(4):
                    b = self.bank()
                    for cc in range(4):
                        c = cg4 * 4 + cc
                        P.op('pe', lambda e, b=b, cc=cc, c=c, tt=tt: e.transpose(
                            out=ps[:, b, cc * 128:(cc + 1) * 128], in_=y[:, c, tt * 128:(tt + 1) * 128], identity=self.identF),
                            rd=[('yb', i % 2, c)], wr=[('ps', b)])
                    if cg4 % 2 == 0:
                        P.op('act', lambda e, b=b, cg4=cg4, ok=ok: e.copy(out=ostg[ok][:, cg4 * 512:(cg4 + 1) * 512], in_=ps[:, b, :]),
                             rd=[('ps', b)], wr=[('ostg', ok, cg4)])
                    else:
                        P.op('dve', lambda e, b=b, cg4=cg4, ok=ok: e.tensor_copy(out=ostg[ok][:, cg4 * 512:(cg4 + 1) * 512], in_=ps[:, b, :]),
                             rd=[('ps', b)], wr=[('ostg', ok, cg4)])
                r0 = i * 256 + tt * 128
                P.dma('sp', self.out[s, r0:r0 + 128, :], ostg[ok], rd=[('ostg', ok, q) for q in range(4)], wr=[('out', s, r0)])
        self.rmsnorm_to_uT(lambda j: ('none',), self.hT[s], self.g3, 0, None, out_cb=out_cb)
        P.barrier()
        sb.release(m)

CST_IDENT = 0
CST_NM0 = 128
CST_W4 = 256
CST_OH = 384
CST_INVSC = 640
CST_OV = 641
CST_FM = 673
CST_E = 1185
CST_W = 3233


def t5_bucket_np(n):
    n = np.maximum(n, 0)
    nf = np.maximum(n, 1).astype(np.float32)
    large = 16 + (np.log(nf / np.float32(16)) / np.float32(math.log(128 / 16)) * np.float32(16)).astype(np.int32)
    large = np.minimum(large, 31)
    return np.where(n < 16, n, large)


def make_cst():
    c = np.zeros((128, CST_W), np.float32)
    c[:, CST_IDENT:CST_IDENT + 128] = np.eye(128, dtype=np.float32)
    p = np.arange(128)[:, None]
    q = np.arange(128)[None, :]
    c[:, CST_NM0:CST_NM0 + 128] = np.where(p > q, -BIG, 0.0)
    c[:, CST_W4:CST_W4 + 128] = np.where(q >= p, -BIG, 0.0)
    bk = t5_bucket_np(np.arange(256))
    oh = np.zeros((32, 256), np.float32)
    oh[bk, np.arange(256)] = 1.0
    oh[31, :] -= 1.0
    c[0:32, CST_OH:CST_OH + 256] = oh[:, ::-1]
    c[0:4, CST_INVSC] = 8.0
    c[4:10, CST_INVSC] = math.sqrt(128.0)
    cs_ = np.arange(NCMP) * 16
    ss_ = np.arange(NSEL) * 64
    ov = np.clip(np.minimum(cs_[:, None] + 32, ss_[None, :] + 64) - np.maximum(cs_[:, None], ss_[None, :]), 0, None) / 32.0
    c[0:NCMP, CST_OV:CST_OV + 32] = ov
    t = np.arange(S)
    blk = t // 64
    j = np.arange(NSEL)
    valid = j[None, :] <= blk[:, None]
    forced = (j[None, :] == 0) | (j[None, :] == blk[:, None]) | (j[None, :] == blk[:, None] - 1)
    fm = np.where(valid, np.where(forced, 1e4, 0.0), -1e30).astype(np.float32)
    c[:, CST_FM:CST_FM + 512] = fm.reshape(16, 128, 32).transpose(1, 0, 2).reshape(128, 512)
    k = np.arange(S)
    e = np.zeros((32, S), np.float32)
    e[k // 64, k] = 1.0
    c[0:32, CST_E:CST_E + S] = e
    return c


def make_in_map(inputs, seqs, L):
    f = lambda a: np.ascontiguousarray(np.asarray(a, dtype=np.float32))
    m = {
        "x": f(inputs["x"][seqs]),
        "attn_norm_g": f(inputs["attn_norm_g"][:L]),
        "w_in": f(inputs["w_in"][:L]),
        "fox_f_bias": f(inputs["fox_f_bias"][:L]),
        "diff_lam4": f(np.stack([inputs["diff_lq1"][:L], inputs["diff_lk1"][:L],
                                 inputs["diff_lq2"][:L], inputs["diff_lk2"][:L]], axis=1)),
        "diff_subln_g": f(inputs["diff_subln_g"][:L]),
        "nsa_cmp_pos": f(inputs["nsa_cmp_pos"][:L]),
        "nsa_cmp_w1": f(np.stack([inputs["nsa_cmp_wk1"][:L], inputs["nsa_cmp_wv1"][:L]], axis=1)),
        "nsa_cmp_w2": f(np.stack([inputs["nsa_cmp_wk2"][:L], inputs["nsa_cmp_wv2"][:L]], axis=1)),
        "w_out": f(inputs["w_out"][:L]),
        "ffn_norm_g": f(inputs["ffn_norm_g"][:L]),
        "ffn_w_up": f(inputs["ffn_w_up"][:L]),
        "ffn_conv_w": f(inputs["ffn_conv_w"][:L]),
        "ffn_conv_b": f(inputs["ffn_conv_b"][:L]),
        "ffn_w_down": f(inputs["ffn_w_down"][:L]),
        "rel_bias": f(inputs["rel_bias"]),
        "final_norm_g": f(inputs["final_norm_g"]),
        "cst": make_cst(),
    }
    return m


def kernel(**inputs):
    L, NSEQ, NCORE = 4, 2, 8
    b = Builder(L, NSEQ)
    nc = b.build(phases=("x0", "A", "Bf", "Bd", "Bn", "C", "Z"))
    in_maps = [make_in_map(inputs, slice(NSEQ * i, NSEQ * (i + 1)), L) for i in range(NCORE)]
    res = run_bass_kernel_spmd(nc, in_maps, core_ids=list(range(NCORE)))
    return np.concatenate([np.asarray(r["out"]) for r in res.results], axis=0).astype(np.float32)
```

```python
import math
from contextlib import ExitStack
import numpy as np
import concourse.bass as bass
import concourse.mybir as mybir
from concourse.bass_utils import run_bass_kernel_spmd

F32 = mybir.dt.float32
BF16 = mybir.dt.bfloat16
AF = mybir.ActivationFunctionType
ALU = mybir.AluOpType
AX = mybir.AxisListType

S = 2048
D = 2048
NCH = 16
DFF = 5632
NFF = 44
EPS = 1e-6
BIG = 30000.0
N_IN = 6168
NCMP = 127
NSEL = 32
ENGS = ['pe', 'act', 'dve', 'pool', 'sp']
ENG_ATTR = {'pe': 'tensor', 'act': 'scalar', 'dve': 'vector', 'pool': 'gpsimd', 'sp': 'sync'}
SAME_ENG_SYNC = True
NW = 45056

C_FQ, C_FK, C_FV, C_FF = 0, 768, 1536, 2304
C_DQ, C_DK, C_DV = 2310, 2822, 3334
C_NQ = 3846
C_NKC, C_NVC, C_NKS, C_NVS, C_NKW, C_NVW, C_NG = 4614, 4870, 5126, 5382, 5638, 5894, 6150
PT_FQ, PT_FK, PT_DQ, PT_DK, PT_NQ, PT_NKC, PT_NVC, PT_NKS, PT_NKW = 0, 6, 12, 16, 20, 26, 28, 30, 32
NPT = 34
VT_FV, VT_DV, VT_NVS, VT_NVW = 0, 768, 1280, 1536
VTC = 1792


class Prog:
    def __init__(self, nc, nslots=8):
        self.nc = nc
        self.streams = {e: [] for e in ENGS}
        self.cnt = {e: 0 for e in ENGS}
        self.known = {e: {} for e in ENGS}
        self.last_w = {}
        self.readers = {}
        self.slots = {q: [[f"{q}_d{i}", 0] for i in range(nslots)] for q in ('sp', 'pool', 'act')}
        self.rr = {q: 0 for q in self.slots}
        self.nops = 0

    def op(self, eng, fn, rd=(), wr=(), dma=False):
        wdeps = {}
        rdeps = {}

        def add(dd, ev):
            if ev is None:
                return
            k, v = ev
            if dd.get(k, 0) < v:
                dd[k] = v
        for t in rd:
            add(wdeps, self.last_w.get(t))
            if t[0] == 'ps':
                for k, v in self.readers.get(t, {}).items():
                    add(rdeps, (k, v))
        for t in wr:
            add(wdeps, self.last_w.get(t))
            for k, v in self.readers.get(t, {}).items():
                add(rdeps, (k, v))
        if dma:
            sl = self.slots[eng][self.rr[eng]]
            self.rr[eng] = (self.rr[eng] + 1) % len(self.slots[eng])
            add(wdeps, (sl[0], sl[1]))
            sl[1] += 16
            ev = (sl[0], sl[1])
            inc = (sl[0], 16)
        else:
            self.cnt[eng] += 1
            ev = (eng, self.cnt[eng])
            inc = (eng, 1)
        waits = []
        kn = self.known[eng]
        for dd, isw in ((wdeps, True), (rdeps, False)):
            for k, v in dd.items():
                if v <= 0:
                    continue
                if k == eng:
                    if (not isw) or eng == 'pe' or not SAME_ENG_SYNC:
                        continue
                if kn.get(k, 0) >= v:
                    continue
                kn[k] = v
                waits.append((k, v))
        self.streams[eng].append((waits, fn, inc))
        for t in rd:
            r = self.readers.setdefault(t, {})
            if r.get(ev[0], 0) < ev[1]:
                r[ev[0]] = ev[1]
        for t in wr:
            self.last_w[t] = ev
            self.readers[t] = {}
        self.nops += 1
        return ev

    def dma(self, q, out, in_, rd=(), wr=(), slow=False):
        if slow:
            return self.op(q, lambda e, o=out, i=in_: e.dma_start(out=o, in_=i, allow_slow_non_contiguous=True),
                           rd=rd, wr=wr, dma=True)
        return self.op(q, lambda e, o=out, i=in_: e.dma_start(out=o, in_=i), rd=rd, wr=wr, dma=True)

    def all_events(self):
        evs = [(e, self.cnt[e]) for e in ENGS]
        for q in self.slots:
            for nm, v in self.slots[q]:
                evs.append((nm, v))
        return evs

    def barrier(self):
        evs = self.all_events()
        for eng in ENGS:
            waits = []
            kn = self.known[eng]
            for k, v in evs:
                if k == eng or v <= 0 or kn.get(k, 0) >= v:
                    continue
                kn[k] = v
                waits.append((k, v))
            if waits:
                self.streams[eng].append((waits, None, None))
        self.last_w = {}
        self.readers = {}

    def emit(self):
        nc = self.nc
        keys = list(ENGS)
        for q in self.slots:
            keys += [s[0] for s in self.slots[q]]
        with ExitStack() as st:
            sems = {k: st.enter_context(nc.semaphore(k)) for k in keys}
            block = st.enter_context(nc.Block())
            for eng in ENGS:
                stream = self.streams[eng]

                def body(e, stream=stream):
                    for waits, fn, inc in stream:
                        for k, v in waits:
                            e.wait_ge(sems[k], v)
                        if fn is not None:
                            ins = fn(e)
                            ins.then_inc(sems[inc[0]], inc[1])
                getattr(block, ENG_ATTR[eng])(body)


class Arena:
    def __init__(self, big):
        self.big = big
        self.off = 0

    def take(self, nelem, dtype=F32):
        nb = nelem * (4 if dtype == F32 else 2)
        nw = (nb + 3) // 4
        a = self.off
        self.off += (nw + 7) // 8 * 8
        assert self.off <= NW, f"SBUF arena overflow {self.off}"
        ap = self.big[:, a:a + nw]
        if dtype != F32:
            ap = ap.bitcast(dtype)[:, :nelem]
        return ap

    def mark(self):
        return self.off

    def release(self, m):
        self.off = m


def fm_slabs():
    tiles = []
    for h in range(6):
        tiles.append((C_FQ + 128 * h, PT_FQ + h))
    for h in range(6):
        tiles.append((C_FK + 128 * h, PT_FK + h))
    for h in range(4):
        tiles.append((C_DQ + 128 * h, PT_DQ + h))
    for h in range(4):
        tiles.append((C_DK + 128 * h, PT_DK + h))
    for h in range(6):
        tiles.append((C_NQ + 128 * h, PT_NQ + h))
    for g in range(2):
        tiles.append((C_NKC + 128 * g, PT_NKC + g))
    for g in range(2):
        tiles.append((C_NVC + 128 * g, PT_NVC + g))
    for g in range(2):
        tiles.append((C_NKS + 128 * g, PT_NKS + g))
    for g in range(2):
        tiles.append((C_NKW + 128 * g, PT_NKW + g))
    slabs = []
    for i in range(0, len(tiles), 4):
        grp = tiles[i:i + 4]
        segs = []
        for c0, _ in grp:
            if segs and segs[-1][0] + segs[-1][1] == c0:
                segs[-1] = (segs[-1][0], segs[-1][1] + 128)
            else:
                segs.append((c0, 128))
        slabs.append((segs, [t[1] for t in grp]))
    return slabs


def tok_slabs():
    return [
        ([(C_FV, 512)], 0, 512, False),
        ([(C_FV + 512, 256), (C_DV, 256)], 512, 512, False),
        ([(C_DV + 256, 256), (C_NVS, 256)], 1024, 512, False),
        ([(C_NVW, 256), (C_NG, 18)], 1536, 256, True),
    ]


class Builder:
    def __init__(self, L, NSEQ, debug=False):
        self.L, self.NSEQ, self.debug = L, NSEQ, debug
        nc = self.nc = bass.Bass("TRN2", target_bir_lowering=False)
        self.P = Prog(nc)
        dk = "ExternalOutput" if debug else "Internal"
        din = lambda n, sh: nc.dram_tensor(n, sh, F32, kind="ExternalInput").ap()
        self.x = din("x", [NSEQ, S, D])
        self.attn_norm_g = din("attn_norm_g", [L, D])
        self.w_in = din("w_in", [L, D, N_IN])
        self.fox_f_bias = din("fox_f_bias", [L, 6])
        self.lam4 = din("diff_lam4", [L, 4, 64])
        self.diff_subln_g = din("diff_subln_g", [L, 128])
        self.nsa_cmp_pos = din("nsa_cmp_pos", [L, 32, 128])
        self.nsa_cmp_w1 = din("nsa_cmp_w1", [L, 2, 4096, 128])
        self.nsa_cmp_w2 = din("nsa_cmp_w2", [L, 2, 128, 128])
        self.w_out = din("w_out", [L, D, D])
        self.ffn_norm_g = din("ffn_norm_g", [L, D])
        self.ffn_w_up = din("ffn_w_up", [L, D, 2 * DFF])
        self.ffn_conv_w = din("ffn_conv_w", [L, 3, 2 * DFF])
        self.ffn_conv_b = din("ffn_conv_b", [L, 2 * DFF])
        self.ffn_w_down = din("ffn_w_down", [L, DFF, D])
        self.rel_bias = din("rel_bias", [32, 10])
        self.final_norm_g = din("final_norm_g", [D])
        self.cst = din("cst", [128, CST_W])
        self.out = nc.dram_tensor("out", [NSEQ, S, D], F32, kind="ExternalOutput").ap()
        self.hT = nc.dram_tensor("hT", [NSEQ, 128, NCH, S], F32, kind=dk).ap()
        self.PT = nc.dram_tensor("PT", [NPT, 128, S], BF16, kind=dk).ap()
        self.VT = nc.dram_tensor("VT", [16, 128, VTC], BF16, kind=dk).ap()
        self.GT = nc.dram_tensor("GT", [16, 128, 18], F32, kind=dk).ap()
        self.FAR = nc.dram_tensor("FAR", [6, 6, S], BF16, kind=dk).ap()
        self.FAL = nc.dram_tensor("FAL", [6, 6, S], BF16, kind=dk).ap()
        self.TCr_h = nc.dram_tensor("TCr", [10, 4352], F32, kind=dk)
        self.TCr = self.TCr_h.ap()
        self.MT = nc.dram_tensor("MT", [16, 128, S], BF16, kind=dk).ap()
        self.AT = nc.dram_tensor("AT", [NFF, 128, S], BF16, kind="Internal").ap()

    def build(self, phases=("x0", "A")):
        nc, P = self.nc, self.P
        with ExitStack() as st:
            big = st.enter_context(nc.sbuf_tensor("big", [128, NW], F32))
            self.ps = st.enter_context(nc.psum_tensor("ps", [128, 8, 512], F32))
            self.sb = Arena(big)
            self.bank_rr = 0
            self.setup_consts()
            for s in range(self.NSEQ):
                if "x0" in phases:
                    self.phase_x0(s)
            for l in range(self.L):
                for s in range(self.NSEQ):
                    if "A" in phases:
                        self.phase_A(l, s)
                    if "Bf" in phases:
                        self.phase_B_fox(l, s)
                    if "Bd" in phases:
                        self.phase_B_diff(l, s)
                    if "Bn" in phases:
                        self.phase_B_nsa(l, s)
                    if "C" in phases:
                        self.phase_C(l, s)
            for s in range(self.NSEQ):
                if "Z" in phases:
                    self.phase_Z(s)
            P.barrier()
            P.emit()
        return nc

    def bank(self):
        b = self.bank_rr
        self.bank_rr = (self.bank_rr + 1) % 8
        return b

    def setup_consts(self):
        P, sb, nc, ps = self.P, self.sb, self.nc, self.ps
        L = self.L
        cst = self.cst
        self.identF = sb.take(128, F32)
        self.NM0F = sb.take(128, F32)
        self.W4F = sb.take(128, F32)
        self.FM = sb.take(512, F32).rearrange("p (a m) -> p a m", a=16)
        P.dma('sp', self.identF, cst[:, CST_IDENT:CST_IDENT + 128], wr=[('identF',)])
        P.dma('sp', self.NM0F, cst[:, CST_NM0:CST_NM0 + 128], wr=[('NM0F',)])
        P.dma('sp', self.W4F, cst[:, CST_W4:CST_W4 + 128], wr=[('W4F',)])
        P.dma('sp', self.FM, cst[:, CST_FM:CST_FM + 512].rearrange("p (a m) -> p a m", a=16), wr=[('FM',)])
        self.identB = sb.take(128, BF16)
        P.op('dve', lambda e: e.tensor_copy(out=self.identB, in_=self.identF), rd=[('identF',)], wr=[('identB',)])
        self.onesB = sb.take(128, BF16)
        P.op('dve', lambda e: e.memset(self.onesB, 1.0), wr=[('onesB',)])
        self.OVb = sb.take(32, BF16)
        P.dma('pool', self.OVb[0:NCMP, :], cst[0:NCMP, CST_OV:CST_OV + 32], wr=[('OVb',)])
        self.Epad = sb.take(S, BF16)
        P.op('pool', lambda e: e.memset(self.Epad, 0.0), wr=[('Epad',)])
        P.dma('pool', self.Epad[0:32, :], cst[0:32, CST_E:CST_E + S], wr=[('Epad',)])
        self.BT = sb.take(10 * 2 * 128, F32).rearrange("p (h k c) -> p h k c", h=10, k=2)
        self.CB = sb.take(10, F32)
        P.dma('sp', self.CB, self.rel_bias[31:32, :].partition_broadcast(128), wr=[('CB',)])
        self.g1 = sb.take(L * NCH, F32)
        self.g2 = sb.take(L * NCH, F32)
        self.g3 = sb.take(NCH, F32)
        P.dma('sp', self.g1.rearrange("p (l c) -> p l c", l=L),
              self.attn_norm_g.rearrange("l (c p) -> p l c", p=128), wr=[('g1',)], slow=True)
        P.dma('sp', self.g2.rearrange("p (l c) -> p l c", l=L),
              self.ffn_norm_g.rearrange("l (c p) -> p l c", p=128), wr=[('g2',)], slow=True)
        P.dma('sp', self.g3, self.final_norm_g.rearrange("(c p) -> p c", p=128), wr=[('g3',)], slow=True)
        self.nfb = sb.take(L, F32)
        P.dma('sp', self.nfb[0:6, :], self.fox_f_bias.rearrange("l h -> h l"), wr=[('nfb',)], slow=True)
        P.op('dve', lambda e: e.tensor_scalar(out=self.nfb[0:6, :], in0=self.nfb[0:6, :], scalar1=-1.0, scalar2=None,
                                              op0=ALU.mult), rd=[('nfb',)], wr=[('nfb',)])
        m = sb.mark()
        onesrow = sb.take(S, BF16)
        P.op('pool', lambda e: e.memset(onesrow[0:6, :], 1.0), wr=[('onesrow',)])
        for r in range(3):
            P.dma('sp', self.FAR[:, 3 + r, :], onesrow[0:6, :], rd=[('onesrow',)], wr=[('FAR1', r)])
            P.dma('sp', self.FAL[:, r, :], onesrow[0:6, :], rd=[('onesrow',)], wr=[('FAL1', r)])
        ohF = sb.take(256, F32)
        invsc = sb.take(1, F32)
        relS = sb.take(10, F32)
        tt = sb.take(256, F32)
        fill = sb.take(2048, F32)
        P.dma('sp', ohF[0:32, :], cst[0:32, CST_OH:CST_OH + 256], wr=[('ohF',)])
        P.dma('sp', invsc[0:10, :], cst[0:10, CST_INVSC:CST_INVSC + 1], wr=[('invsc',)], slow=True)
        P.dma('sp', relS[0:32, :], self.rel_bias[:, :], wr=[('relS',)])
        P.op('pe', lambda e: e.matmul(ps[0:10, 0, 0:256], relS[0:32, 0:10], ohF[0:32, 0:256], start=True, stop=True),
             rd=[('ohF',), ('relS',)], wr=[('ps', 0)])
        P.op('dve', lambda e: e.tensor_scalar(out=tt[0:10, :], in0=ps[0:10, 0, 0:256], scalar1=invsc[0:10, 0:1],
                                              scalar2=None, op0=ALU.mult), rd=[('ps', 0), ('invsc',)], wr=[('tt',)])
        P.dma('sp', self.TCr[:, 2048:2304], tt[0:10, :], rd=[('tt',)], wr=[('TCr', 1)])
        P.op('pool', lambda e: e.memset(fill[0:10, :], 0.0), wr=[('fill',)])
        P.dma('sp', self.TCr[:, 0:2048], fill[0:10, :], rd=[('fill',)], wr=[('TCr', 0)])
        P.op('pool', lambda e: e.memset(fill[0:10, :], -BIG), rd=[], wr=[('fill',)])
        P.dma('sp', self.TCr[:, 2304:4352], fill[0:10, :], rd=[('fill',)], wr=[('TCr', 2)])
        P.barrier()
        Y = [sb.take(128, F32) for _ in range(2)]
        n = 0
        for h in range(10):
            for k, off in enumerate((0, 128)):
                y = Y[n % 2]
                src = bass.AP(tensor=self.TCr_h, offset=h * 4352 + 2176 - off, ap=[[1, 128], [1, 128]])
                P.dma('sp', y, src, wr=[('Y', n % 2)])
                P.op('dve', lambda e, y=y, h=h, k=k: e.tensor_copy(out=self.BT[:, h, k, :], in_=y[:, ::-1]),
                     rd=[('Y', n % 2)], wr=[('BT', h, k)])
                n += 1
        P.barrier()
        sb.release(m)

    def phase_x0(self, s):
        P, sb, ps = self.P, self.sb, self.ps
        m = sb.mark()
        xin = sb.take(4 * D, F32).rearrange("p (a c) -> p a c", a=4)
        stg = sb.take(NCH * 512, F32).rearrange("p (c t) -> p c t", c=NCH)
        for j in range(4):
            P.dma('sp', xin, self.x[s, j * 512:(j + 1) * 512, :].rearrange("(a p) c -> p a c", p=128),
                  wr=[('xin',)])
            for cc in range(NCH):
                b = self.bank()
                for a in range(4):
                    P.op('pe', lambda e, b=b, a=a, cc=cc: e.transpose(
                        out=ps[:, b, a * 128:(a + 1) * 128], in_=xin[:, a, cc * 128:(cc + 1) * 128],
                        identity=self.identF), rd=[('xin',), ('cst',)], wr=[('ps', b)])
                if cc % 2 == 0:
                    P.op('act', lambda e, b=b, cc=cc: e.copy(out=stg[:, cc, :], in_=ps[:, b, :]),
                         rd=[('ps', b)], wr=[('stg', cc)])
                else:
                    P.op('dve', lambda e, b=b, cc=cc: e.tensor_copy(out=stg[:, cc, :], in_=ps[:, b, :]),
                         rd=[('ps', b)], wr=[('stg', cc)])
            P.dma('sp', self.hT[s, :, :, j * 512:(j + 1) * 512], stg,
                  rd=[('stg', c) for c in range(NCH)], wr=[('hT', s, j)])
        P.barrier()
        sb.release(m)

    def rmsnorm_to_uT(self, src_tok, hT_s, gain, goff, uT, out_cb=None):
        P, sb, ps = self.P, self.sb, self.ps
        hbuf = [sb.take(NCH * 256, F32).rearrange("p (c t) -> p c t", c=NCH) for _ in range(2)]
        sq = sb.take(NCH * 256, BF16).rearrange("p (c t) -> p c t", c=NCH)
        t1 = sb.take(256, F32)
        t2 = sb.take(256, F32)
        rstd = sb.take(256, F32)
        for i in range(8):
            hb = hbuf[i % 2]
            P.dma('sp', hb, hT_s[:, :, i * 256:(i + 1) * 256], rd=[src_tok(i // 2)], wr=[('hb', i % 2)])
            P.op('act', lambda e, hb=hb: e.activation(out=sq, in_=hb, func=AF.Square),
                 rd=[('hb', i % 2)], wr=[('sq',)])
            b = self.bank()
            for c in range(NCH):
                P.op('pe', lambda e, b=b, c=c: e.matmul(ps[:, b, 0:256], self.onesB, sq[:, c, :],
                                                       start=(c == 0), stop=(c == NCH - 1)),
                     rd=[('sq',), ('onesB',)], wr=[('ps', b)])
            P.op('dve', lambda e, b=b: e.tensor_scalar(out=t1, in0=ps[:, b, 0:256], scalar1=EPS * D, scalar2=1.0 / D,
                                                      op0=ALU.add, op1=ALU.mult), rd=[('ps', b)], wr=[('t1',)])
            P.op('act', lambda e: e.activation(out=t2, in_=t1, func=AF.Sqrt), rd=[('t1',)], wr=[('t2',)])
            P.op('dve', lambda e: e.reciprocal(out=rstd, in_=t2), rd=[('t2',)], wr=[('rstd',)])
            if out_cb is not None:
                out_cb(i, hb, rstd)
                continue
            for c in range(NCH):
                P.op('dve', lambda e, hb=hb, c=c, i=i: e.scalar_tensor_tensor(
                    out=uT[:, c, i * 256:(i + 1) * 256], in0=hb[:, c, :], scalar=gain[:, goff + c:goff + c + 1],
                    in1=rstd, op0=ALU.mult, op1=ALU.mult),
                    rd=[('hb', i % 2), ('rstd',)], wr=[('uT', c, i)])

    def load_w_slab(self, wb, key, wsrc, segs):
        P = self.P
        off = 0
        toks = []
        for si, (c0, n) in enumerate(segs):
            tok = ('wb', key, si)
            P.dma('pool', wb[:, :, off:off + n], wsrc[:, c0:c0 + n].rearrange("(cc p) f -> p cc f", p=128),
                  wr=[tok])
            toks.append(tok)
            off += n
        return toks

    def phase_A(self, l, s):
        P, sb, ps, nc = self.P, self.sb, self.ps, self.nc
        m = sb.mark()
        uT = sb.take(NCH * S, BF16).rearrange("p (c t) -> p c t", c=NCH)
        wbuf = [sb.take(NCH * 512, BF16).rearrange("p (c f) -> p c f", c=NCH) for _ in range(2)]
        wsrc = self.w_in[l]
        pre_slabs = fm_slabs()[:2]
        pre_toks = [self.load_w_slab(wbuf[k], k, wsrc, pre_slabs[k][0]) for k in range(2)]
        m2 = sb.mark()
        self.rmsnorm_to_uT(lambda j: ('hT', s, j), self.hT[s], self.g1, l * NCH, uT)
        stgF = [sb.take(S, BF16) for _ in range(2)]
        stgT = [sb.take(512, BF16) for _ in range(2)]
        stgG = [sb.take(18, F32) for _ in range(2)]
        wsrc = self.w_in[l]
        utoks = lambda c, t0, t1: [('uT', c, i) for i in range(t0 // 256, (t1 + 255) // 256)]
        nslab = 0
        nst = 0
        for segs, tiles in fm_slabs():
            k = nslab % 2
            nslab += 1
            wb = wbuf[k]
            if nslab <= 2:
                wtoks = pre_toks[k]
            else:
                wtoks = self.load_w_slab(wb, k, wsrc, segs)
            for ft, pti in enumerate(tiles):
                sk = nst % 2
                nst += 1
                for j in range(4):
                    b = self.bank()
                    for c in range(NCH):
                        P.op('pe', lambda e, b=b, c=c, wb=wb, ft=ft, j=j: e.matmul(
                            ps[:, b, :], wb[:, c, ft * 128:(ft + 1) * 128], uT[:, c, j * 512:(j + 1) * 512],
                            start=(c == 0), stop=(c == NCH - 1)),
                            rd=wtoks + utoks(c, j * 512, (j + 1) * 512), wr=[('ps', b)])
                    if j % 2 == 0:
                        P.op('act', lambda e, b=b, sk=sk, j=j: e.copy(out=stgF[sk][:, j * 512:(j + 1) * 512],
                                                                     in_=ps[:, b, :]),
                             rd=[('ps', b)], wr=[('stgF', sk, j)])
                    else:
                        P.op('dve', lambda e, b=b, sk=sk, j=j: e.tensor_copy(out=stgF[sk][:, j * 512:(j + 1) * 512],
                                                                            in_=ps[:, b, :]),
                             rd=[('ps', b)], wr=[('stgF', sk, j)])
                P.dma('sp', self.PT[pti], stgF[sk], rd=[('stgF', sk, j) for j in range(4)], wr=[('PT', pti)])
        ntt = 0
        for segs, voff, nv, gates in tok_slabs():
            k = nslab % 2
            nslab += 1
            wb = wbuf[k]
            wtoks = self.load_w_slab(wb, k, wsrc, segs)
            ncol = sum(n for _, n in segs)
            for tt in range(16):
                sk = ntt % 2
                ntt += 1
                b = self.bank()
                for c in range(NCH):
                    P.op('pe', lambda e, b=b, c=c, wb=wb, tt=tt, ncol=ncol: e.matmul(
                        ps[:, b, 0:ncol], uT[:, c, tt * 128:(tt + 1) * 128], wb[:, c, 0:ncol],
                        start=(c == 0), stop=(c == NCH - 1)),
                        rd=wtoks + utoks(c, tt * 128, (tt + 1) * 128), wr=[('ps', b)])
                if tt % 2 == 0:
                    P.op('act', lambda e, b=b, sk=sk, nv=nv: e.copy(out=stgT[sk][:, 0:nv], in_=ps[:, b, 0:nv]),
                         rd=[('ps', b)], wr=[('stgT', sk)])
                else:
                    P.op('dve', lambda e, b=b, sk=sk, nv=nv: e.tensor_copy(out=stgT[sk][:, 0:nv], in_=ps[:, b, 0:nv]),
                         rd=[('ps', b)], wr=[('stgT', sk)])
                P.dma('sp', self.VT[tt, :, voff:voff + nv], stgT[sk][:, 0:nv], rd=[('stgT', sk)],
                      wr=[('VT', tt, voff)])
                if gates:
                    P.op('act', lambda e, b=b, sk=sk, nv=nv: e.activation(out=stgG[sk], in_=ps[:, b, nv:nv + 18],
                                                                         func=AF.Sigmoid),
                         rd=[('ps', b)], wr=[('stgG', sk)])
                    P.dma('sp', self.GT[tt], stgG[sk], rd=[('stgG', sk)], wr=[('GT', tt)])
        P.barrier()
        sb.release(m2)
        wff = sb.take(NCH * 6, BF16).rearrange("p (c f) -> p c f", c=NCH)
        P.dma('pool', wff, wsrc[:, C_FF:C_FF + 6].rearrange("(cc p) f -> p cc f", p=128), wr=[('wff',)])
        e1 = sb.take(S, F32)
        onesF = sb.take(S, F32)
        cs = sb.take(S, F32)
        P.op('pool', lambda e: e.memset(onesF[0:6, :], 1.0), wr=[('onesF',)])
        for j in range(4):
            b = self.bank()
            for c in range(NCH):
                P.op('pe', lambda e, b=b, c=c, j=j: e.matmul(ps[0:6, b, :], wff[:, c, :], uT[:, c, j * 512:(j + 1) * 512],
                                                            start=(c == 0), stop=(c == NCH - 1)),
                     rd=[('wff',)] + utoks(c, j * 512, (j + 1) * 512), wr=[('ps', b)])
            P.op('act', lambda e, b=b, j=j: e.activation(out=e1[0:6, j * 512:(j + 1) * 512], in_=ps[0:6, b, :],
                                                        func=AF.Exp, scale=-1.0, bias=self.nfb[0:6, l:l + 1]),
                 rd=[('ps', b), ('nfb',)], wr=[('e1', j)])
        e1t = [('e1', j) for j in range(4)]
        P.op('dve', lambda e: e.tensor_scalar(out=e1[0:6, :], in0=e1[0:6, :], scalar1=1.0, scalar2=None, op0=ALU.add),
             rd=e1t, wr=e1t)
        P.op('act', lambda e: e.activation(out=e1[0:6, :], in_=e1[0:6, :], func=AF.Ln), rd=e1t, wr=e1t)
        P.op('dve', lambda e: e.tensor_tensor_scan(out=cs[0:6, :], data0=onesF[0:6, :], data1=e1[0:6, :], initial=0.0,
                                                  op0=ALU.mult, op1=ALU.add), rd=e1t + [('onesF',)], wr=[('cs',)])
        sq128 = math.sqrt(128.0)
        P.op('dve', lambda e: e.tensor_scalar(out=cs[0:6, :], in0=cs[0:6, :], scalar1=-sq128, scalar2=None,
                                              op0=ALU.mult), rd=[('cs',)], wr=[('cs',)])
        pcs = [sb.take(S, BF16) for _ in range(3)]
        ncs = [sb.take(S, BF16) for _ in range(3)]
        for r in range(3):
            P.op('dve', lambda e, r=r: e.tensor_copy(out=pcs[r][0:6, :], in_=cs[0:6, :]), rd=[('cs',)], wr=[('pcs', r)])
            if r < 2:
                P.op('dve', lambda e, r=r: e.tensor_tensor(out=cs[0:6, :], in0=cs[0:6, :], in1=pcs[r][0:6, :],
                                                          op=ALU.subtract), rd=[('cs',), ('pcs', r)], wr=[('cs',)])
            P.op('dve', lambda e, r=r: e.tensor_scalar(out=ncs[r][0:6, :], in0=pcs[r][0:6, :], scalar1=-1.0,
                                                      scalar2=None, op0=ALU.mult), rd=[('pcs', r)], wr=[('ncs', r)])
            P.dma('sp', self.FAR[:, r, :], pcs[r][0:6, :], rd=[('pcs', r)], wr=[('FAR', r)])
            P.dma('sp', self.FAL[:, 3 + r, :], ncs[r][0:6, :], rd=[('ncs', r)], wr=[('FAL', r)])
        P.barrier()
        sb.release(m)


    class Stream:
        pass

    def emit_S(self, st, qc, buf, bufid):
        P, ps = self.P, self.ps
        if st.mode == 'causal':
            kts = range(0, 4 * qc + 4)
        else:
            kts = range(max(0, 4 * qc - 4), 4 * qc + 4)
        ktmin = kts[0] if st.mode == 'window' else 0
        for kt in kts:
            qi0 = max(kt, 4 * qc)
            qi1 = 4 * qc + 3 if st.mode == 'causal' else min(kt + 4, 4 * qc + 3)
            q0, q1 = qi0 * 128, (qi1 + 1) * 128
            n = q1 - q0
            b = self.bank()
            mms = [(st.lhs(kt), st.QT[:, q0:q1])] + st.extra(kt, q0, q1)
            for idx, (lh, rh) in enumerate(mms):
                P.op('pe', lambda e, b=b, n=n, lh=lh, rh=rh, idx=idx, last=len(mms) - 1: e.matmul(
                    ps[:, b, 0:n], lh, rh, start=(idx == 0), stop=(idx == last)),
                    rd=st.rtoks, wr=[('ps', b)])
            for qi in range(qi0, qi1 + 1):
                tab = st.table(qi - kt)
                if tab is not None:
                    a = (qi - qi0) * 128
                    P.op('dve', lambda e, b=b, a=a, tab=tab: e.tensor_tensor(
                        out=ps[:, b, a:a + 128], in0=ps[:, b, a:a + 128], in1=tab, op=ALU.add),
                        rd=[('ps', b)], wr=[('ps', b)])
            off = q0 - qc * 512
            slot = kt - ktmin
            if st.cb is not None:
                fn = lambda e, b=b, n=n, off=off, slot=slot: e.activation(
                    out=buf[:, slot, off:off + n], in_=ps[:, b, 0:n], func=AF.Exp, scale=st.scale, bias=st.cb)
            else:
                fn = lambda e, b=b, n=n, off=off, slot=slot: e.activation(
                    out=buf[:, slot, off:off + n], in_=ps[:, b, 0:n], func=AF.Exp, scale=st.scale)
            P.op('act', fn, rd=[('ps', b)], wr=[('pt', bufid, slot)])

    def emit_PV(self, st, qc, buf, bufid, ji):
        P, ps = self.P, self.ps
        ktmin = max(0, 4 * qc - 4) if st.mode == 'window' else 0
        for qi in range(4 * qc, 4 * qc + 4):
            kts = range(0, qi + 1) if st.mode == 'causal' else range(max(0, qi - 4), qi + 1)
            b = self.bank()
            a = (qi - 4 * qc) * 128
            for idx, kt in enumerate(kts):
                slot = kt - ktmin
                P.op('pe', lambda e, b=b, a=a, slot=slot, kt=kt, idx=idx, last=len(kts) - 1: e.matmul(
                    ps[:, b, 0:st.ncols], buf[:, slot, a:a + 128], st.V(kt), start=(idx == 0), stop=(idx == last)),
                    rd=[('pt', bufid, slot)] + st.vtoks, wr=[('ps', b)])
            st.epi(qi, b, ji)

    def run_jobs(self, jobs, ring, per_head=None, after_head=None):
        n = len(jobs)

        def hook(pi):
            if per_head is not None and (pi + 1) % per_head == 0:
                after_head(pi // per_head)
        self.emit_S(jobs[0][0], jobs[0][1], ring[0], 0)
        for i in range(n):
            st, qc = jobs[i]
            if i + 1 < n:
                self.emit_S(jobs[i + 1][0], jobs[i + 1][1], ring[(i + 1) % 2], (i + 1) % 2)
            if i >= 1:
                pst, pqc = jobs[i - 1]
                pst.post_a(pqc, i - 1)
            self.emit_PV(st, qc, ring[i % 2], i % 2, i)
            if i >= 1:
                pst.post_b(pqc, i - 1)
                hook(i - 1)
        pst, pqc = jobs[n - 1]
        pst.post_a(pqc, n - 1)
        pst.post_b(pqc, n - 1)
        hook(n - 1)

    def post_transpose(self, obf, par, mixstg, k, qc, head, evac_eng):
        P, ps = self.P, self.ps
        bt = self.bank()
        psb = ps[:, bt, 0:256].bitcast(BF16)
        for j in range(4):
            P.op('pe', lambda e, j=j: e.transpose(out=psb[:, j * 128:(j + 1) * 128], in_=obf[par][:, j, :],
                                                  identity=self.identB),
                 rd=[('obf', par, j)], wr=[('ps', bt)])
        if evac_eng == 'act':
            P.op('act', lambda e: e.copy(out=mixstg[k][:, qc * 512:(qc + 1) * 512], in_=psb),
                 rd=[('ps', bt)], wr=[('mix', k, qc)])
        else:
            P.op('dve', lambda e: e.tensor_copy(out=mixstg[k][:, qc * 512:(qc + 1) * 512], in_=psb),
                 rd=[('ps', bt)], wr=[('mix', k, qc)])
        if qc == 3:
            P.dma('sp', self.MT[head], mixstg[k], rd=[('mix', k, q) for q in range(4)], wr=[('MT', head)])

    def load_V(self, Vt, k, col):
        self.P.dma('sp', Vt[:, :, 0:128], self.VT[:, :, col:col + 128].rearrange("k p d -> p k d"), wr=[('V', k)])

    def phase_B_fox(self, l, s):
        P, sb, ps = self.P, self.sb, self.ps
        m = sb.mark()
        ring = [sb.take(16 * 512, BF16).rearrange("p (k q) -> p k q", k=16) for _ in range(2)]
        KT = [sb.take(S, BF16) for _ in range(2)]
        QT = [sb.take(S, BF16) for _ in range(2)]
        AL = [sb.take(S, BF16) for _ in range(2)]
        AR = [sb.take(S, BF16) for _ in range(2)]
        V = [sb.take(16 * 130, BF16).rearrange("p (k d) -> p k d", k=16) for _ in range(2)]
        mixstg = [sb.take(S, BF16) for _ in range(2)]
        obf = [sb.take(4 * 128, BF16).rearrange("p (j d) -> p j d", j=4) for _ in range(2)]
        raw = [sb.take(4 * 130, F32).rearrange("p (j d) -> p j d", j=4) for _ in range(2)]
        rr = [sb.take(4, F32) for _ in range(2)]
        for k in range(2):
            P.op('pool', lambda e, k=k: e.memset(AL[k], 0.0), wr=[('AL', k)])
            P.op('pool', lambda e, k=k: e.memset(AR[k], 0.0), wr=[('AR', k)])
            P.op('pool', lambda e, k=k: e.memset(V[k][:, :, 128:129], 1.0), wr=[('V', k)])
            P.op('pool', lambda e, k=k: e.memset(V[k][:, :, 129:130], 0.0), wr=[('V', k)])
        jobs = []
        sc = 128.0 ** -0.5
        for h in range(6):
            k = h % 2
            st = self.Stream()
            st.mode = 'causal'
            st.head = h
            st.k = k
            st.lhs = lambda kt, k=k: KT[k][:, kt * 128:(kt + 1) * 128]
            st.QT = QT[k]
            st.extra = lambda kt, q0, q1, k=k: [(AL[k][:, kt * 128:(kt + 1) * 128], AR[k][:, q0:q1])]
            st.table = lambda mm: self.NM0F if mm == 0 else None
            st.cb = None
            st.scale = sc
            st.V = lambda kt, k=k: V[k][:, kt, 0:129]
            st.ncols = 129
            st.rtoks = [('KT', k), ('QT', k), ('AL', k), ('AR', k)]
            st.vtoks = [('V', k)]
            st.loaded = False

            def epi(qi, b, ji):
                j = qi % 4
                par = ji % 2
                P.op('dve', lambda e: e.tensor_copy(out=raw[par][:, j, 0:129], in_=ps[:, b, 0:129]),
                     rd=[('ps', b)], wr=[('raw', par, j)])
            st.epi = epi

            def post(qc, ji, h=h, k=k):
                par = ji % 2
                rt = [('raw', par, j) for j in range(4)]
                P.op('dve', lambda e: e.reciprocal(out=rr[par].unsqueeze(2), in_=raw[par][:, :, 128:129]),
                     rd=rt, wr=[('rr', par)])
                P.op('dve', lambda e: e.tensor_tensor(out=obf[par], in0=raw[par][:, :, 0:128],
                                                      in1=rr[par].unsqueeze(2).to_broadcast([128, 4, 128]), op=ALU.mult),
                     rd=rt + [('rr', par)], wr=[('obf', par, j) for j in range(4)])
            st.post_a = post
            st.post_b = lambda qc, ji, h=h, k=k: self.post_transpose(obf, ji % 2, mixstg, k, qc, h, 'act')
            for qc in range(4):
                jobs.append((st, qc))
        def load(h):
            k = h % 2
            P.dma('sp', KT[k], self.PT[PT_FK + h], wr=[('KT', k)])
            P.dma('sp', QT[k], self.PT[PT_FQ + h], wr=[('QT', k)])
            self.load_V(V[k], k, VT_FV + 128 * h)
            P.dma('sp', AL[k][0:6, :], self.FAL[h], wr=[('AL', k)])
            P.dma('sp', AR[k][0:6, :], self.FAR[h], wr=[('AR', k)])
        load(0)
        load(1)
        self.run_jobs(jobs, ring, 4, lambda hi: load(hi + 2) if hi + 2 < 6 else None)
        P.barrier()
        sb.release(m)

    def phase_B_diff(self, l, s):
        P, sb, ps = self.P, self.sb, self.ps
        m = sb.mark()
        ring = [sb.take(16 * 512, BF16).rearrange("p (k q) -> p k q", k=16) for _ in range(2)]
        KT0 = [sb.take(S, BF16) for _ in range(2)]
        KT1 = [sb.take(S, BF16) for _ in range(2)]
        QT = [sb.take(S, BF16) for _ in range(2)]
        V = [sb.take(16 * 130, BF16).rearrange("p (k d) -> p k d", k=16) for _ in range(2)]
        mixstg = [sb.take(S, BF16) for _ in range(2)]
        obf = [sb.take(4 * 128, BF16).rearrange("p (j d) -> p j d", j=4) for _ in range(2)]
        t0 = [sb.take(4 * 128, F32).rearrange("p (j d) -> p j d", j=4) for _ in range(2)]
        od = sb.take(4 * 128, F32).rearrange("p (j d) -> p j d", j=4)
        sqd = sb.take(4 * 128, F32).rearrange("p (j d) -> p j d", j=4)
        raw = [sb.take(4 * 130, F32).rearrange("p (j d) -> p j d", j=4) for _ in range(2)]
        rr = [sb.take(4, F32) for _ in range(2)]
        ssq = sb.take(4, F32)
        mhalf = sb.take(4, F32)
        P.op('pool', lambda e: e.memset(mhalf, -0.5), wr=[('mhalf',)])
        gsub = sb.take(128, F32)
        lamb = sb.take(256, F32)
        pr = sb.take(128, F32)
        s12 = sb.take(2, F32)
        nlam = sb.take(1, F32)
        lam_init = 0.8 - 0.6 * math.exp(-0.3 * l)
        P.dma('sp', lamb, self.lam4[l:l + 1].rearrange("o a d -> o (a d)").partition_broadcast(128), wr=[('lamb',)])
        P.dma('sp', gsub, self.diff_subln_g[l:l + 1, :].partition_broadcast(128), wr=[('gsub',)])
        P.op('dve', lambda e: e.tensor_scalar(out=gsub, in0=gsub, scalar1=1.0 - lam_init, scalar2=None, op0=ALU.mult),
             rd=[('gsub',)], wr=[('gsub',)])
        P.op('dve', lambda e: e.tensor_tensor(out=pr[:, 0:64], in0=lamb[:, 0:64], in1=lamb[:, 64:128], op=ALU.mult),
             rd=[('lamb',)], wr=[('pr', 0)])
        P.op('dve', lambda e: e.tensor_tensor(out=pr[:, 64:128], in0=lamb[:, 128:192], in1=lamb[:, 192:256], op=ALU.mult),
             rd=[('lamb',)], wr=[('pr', 1)])
        P.op('dve', lambda e: e.tensor_reduce(out=s12, in_=pr.rearrange("p (a d) -> p a d", a=2), axis=AX.X, op=ALU.add),
             rd=[('pr', 0), ('pr', 1)], wr=[('s12',)])
        P.op('act', lambda e: e.activation(out=s12, in_=s12, func=AF.Exp), rd=[('s12',)], wr=[('s12',)])
        P.op('dve', lambda e: e.tensor_tensor(out=nlam, in0=s12[:, 1:2], in1=s12[:, 0:1], op=ALU.subtract),
             rd=[('s12',)], wr=[('nlam',)])
        P.op('dve', lambda e: e.tensor_scalar(out=nlam, in0=nlam, scalar1=-lam_init, scalar2=None, op0=ALU.add),
             rd=[('nlam',)], wr=[('nlam',)])
        for k in range(2):
            P.op('pool', lambda e, k=k: e.memset(KT0[k][64:128, :], 0.0), wr=[('KT0z', k)])
            P.op('pool', lambda e, k=k: e.memset(KT1[k][0:64, :], 0.0), wr=[('KT1z', k)])
            P.op('pool', lambda e, k=k: e.memset(V[k][:, :, 128:129], 1.0), wr=[('V', k)])
            P.op('pool', lambda e, k=k: e.memset(V[k][:, :, 129:130], 0.0), wr=[('V', k)])
        jobs = []
        for h in range(4):
            k = h % 2
            for mp in range(2):
                st = self.Stream()
                st.mode = 'causal'
                KTm = KT0 if mp == 0 else KT1
                st.lhs = lambda kt, k=k, KTm=KTm: KTm[k][:, kt * 128:(kt + 1) * 128]
                st.QT = QT[k]
                st.extra = lambda kt, q0, q1: []
                st.table = lambda mm, h=h: self.BT[:, h, 0, :] if mm == 0 else (self.BT[:, h, 1, :] if mm == 1 else None)
                st.cb = self.CB[:, h:h + 1]
                st.scale = 0.125
                st.V = lambda kt, k=k: V[k][:, kt, 0:129]
                st.ncols = 129
                st.rtoks = [('KT0', k), ('KT1', k), ('KT0z', k), ('KT1z', k), ('QT', k)]
                st.vtoks = [('V', k)]
                def epi(qi, b, ji, mp=mp):
                    j = qi % 4
                    P.op('dve', lambda e: e.tensor_copy(out=raw[mp][:, j, 0:129], in_=ps[:, b, 0:129]),
                         rd=[('ps', b)], wr=[('raw', mp, j)])
                st.epi = epi
                if mp == 0:
                    def post(qc, ji):
                        par = (ji // 2) % 2
                        rt = [('raw', 0, j) for j in range(4)]
                        P.op('dve', lambda e: e.reciprocal(out=rr[0].unsqueeze(2), in_=raw[0][:, :, 128:129]),
                             rd=rt, wr=[('rr', 0)])
                        P.op('dve', lambda e: e.tensor_tensor(out=t0[par], in0=raw[0][:, :, 0:128],
                                                              in1=rr[0].unsqueeze(2).to_broadcast([128, 4, 128]), op=ALU.mult),
                             rd=rt + [('rr', 0)], wr=[('t0', par)])
                    st.post_a = post
                    st.post_b = lambda qc, ji: None
                else:
                    def post(qc, ji, h=h, k=k):
                        par = (ji // 2) % 2
                        rt = [('raw', 1, j) for j in range(4)]
                        P.op('dve', lambda e: e.reciprocal(out=rr[1].unsqueeze(2), in_=raw[1][:, :, 128:129]),
                             rd=rt, wr=[('rr', 1)])
                        P.op('dve', lambda e: e.tensor_scalar(out=rr[1], in0=rr[1], scalar1=nlam[:, 0:1], scalar2=None,
                                                              op0=ALU.mult), rd=[('rr', 1), ('nlam',)], wr=[('rr', 1)])
                        P.op('dve', lambda e: e.tensor_tensor(out=od, in0=raw[1][:, :, 0:128],
                                                              in1=rr[1].unsqueeze(2).to_broadcast([128, 4, 128]), op=ALU.mult),
                             rd=rt + [('rr', 1)], wr=[('od',)])
                        P.op('pool', lambda e: e.tensor_tensor(out=od, in0=od, in1=t0[par], op=ALU.add),
                             rd=[('od',), ('t0', par)], wr=[('od',)])
                        P.op('pool', lambda e: e.tensor_tensor(out=sqd, in0=od, in1=od, op=ALU.mult),
                             rd=[('od',)], wr=[('sqd',)])
                        P.op('dve', lambda e: e.tensor_reduce(out=ssq, in_=sqd, axis=AX.X, op=ALU.add),
                             rd=[('sqd',)], wr=[('ssq',)])
                        P.op('dve', lambda e: e.tensor_scalar(out=ssq, in0=ssq, scalar1=EPS * 128.0, scalar2=1.0 / 128.0,
                                                              op0=ALU.add, op1=ALU.mult), rd=[('ssq',)], wr=[('ssq',)])
                        P.op('pool', lambda e: e.tensor_tensor(out=ssq, in0=ssq, in1=mhalf, op=ALU.pow),
                             rd=[('ssq',), ('mhalf',)], wr=[('ssq',)])
                        P.op('dve', lambda e: e.tensor_tensor(out=od, in0=od, in1=ssq.unsqueeze(2).to_broadcast([128, 4, 128]),
                                                              op=ALU.mult), rd=[('od',), ('ssq',)], wr=[('od',)])
                        P.op('pool', lambda e: e.tensor_tensor(out=obf[par], in0=od,
                                                               in1=gsub.unsqueeze(1).to_broadcast([128, 4, 128]), op=ALU.mult),
                             rd=[('od',), ('gsub',)], wr=[('obf', par, j) for j in range(4)])
                    st.post_a = post
                    st.post_b = lambda qc, ji, h=h, k=k: self.post_transpose(obf, (ji // 2) % 2, mixstg, k, qc, 6 + h, 'act')
                st.mp = mp
                jobs.append(st)
        joblist = []
        for h in range(4):
            for qc in range(4):
                joblist.append((jobs[2 * h], qc))
                joblist.append((jobs[2 * h + 1], qc))

        def load(h):
            k = h % 2
            P.dma('sp', KT0[k][0:64, :], self.PT[PT_DK + h, 0:64, :], wr=[('KT0', k)])
            P.dma('sp', KT1[k][64:128, :], self.PT[PT_DK + h, 64:128, :], wr=[('KT1', k)])
            P.dma('sp', QT[k], self.PT[PT_DQ + h], wr=[('QT', k)])
            self.load_V(V[k], k, VT_DV + 128 * h)
        load(0)
        load(1)
        self.run_jobs(joblist, ring, 8, lambda hi: load(hi + 2) if hi + 2 < 4 else None)
        P.barrier()
        sb.release(m)

    def phase_B_nsa(self, l, s):
        P, sb, ps = self.P, self.sb, self.ps
        m = sb.mark()
        sc = 128.0 ** -0.5
        NB = 4096.0
        kcmpT = [sb.take(128, BF16) for _ in range(2)]
        Rt = [sb.take(162, BF16) for _ in range(2)]
        gates = sb.take(16 * 18, F32).rearrange("p (a c) -> p a c", a=16)
        P.dma('sp', gates, self.GT.rearrange("a p c -> p a c"), wr=[('gates',)])
        for g in range(2):
            P.op('pool', lambda e, g=g: e.memset(Rt[g][:, 128:129], 1.0), wr=[('Rt1', g)])
            P.op('dve', lambda e, g=g: e.tensor_copy(out=Rt[g][0:NCMP, 129:161], in_=self.OVb[0:NCMP, :]), wr=[('Rt2', g)])
        m1 = sb.mark()
        w1 = [sb.take(32 * 128, BF16).rearrange("p (l j) -> p l j", l=32) for _ in range(2)]
        w2 = [sb.take(128, BF16) for _ in range(2)]
        posT = sb.take(32, BF16)
        xcT = [sb.take(S, BF16) for _ in range(2)]
        pb = sb.take(2, F32)
        xs = sb.take(128, F32)
        x2 = sb.take(128, F32)
        yy = sb.take(128, F32)
        gT = sb.take(128, BF16)
        for kv in range(2):
            P.dma('pool', w1[kv], self.nsa_cmp_w1[l, kv].rearrange("(l d) j -> d l j", d=128), wr=[('w1', kv)])
            P.dma('pool', w2[kv], self.nsa_cmp_w2[l, kv], wr=[('w2', kv)])
        posF = sb.take(128, F32)
        P.dma('sp', posF[0:32, :], self.nsa_cmp_pos[l], wr=[('posF',)])
        bp = self.bank()
        P.op('pe', lambda e: e.transpose(out=ps[:, bp, 0:32], in_=posF[0:32, :], identity=self.identF[0:32, 0:32]),
             rd=[('posF',)], wr=[('ps', bp)])
        P.op('dve', lambda e: e.tensor_copy(out=posT, in_=ps[:, bp, 0:32]), rd=[('ps', bp)], wr=[('posT',)])
        for kv in range(2):
            b2 = self.bank()
            for li in range(32):
                P.op('pe', lambda e, b2=b2, kv=kv, li=li: e.matmul(ps[:, b2, 0:1], w1[kv][:, li, :], posT[:, li:li + 1],
                                                                  start=(li == 0), stop=(li == 31)),
                     rd=[('w1', kv), ('posT',)], wr=[('ps', b2)])
            P.op('dve', lambda e, b2=b2, kv=kv: e.tensor_copy(out=pb[:, kv:kv + 1], in_=ps[:, b2, 0:1]),
                 rd=[('ps', b2)], wr=[('pb', kv)])
        n = 0
        for g in range(2):
            for kv in range(2):
                xc = xcT[n % 2]
                xk = n % 2
                n += 1
                P.dma('sp', xc, self.PT[(PT_NKC if kv == 0 else PT_NVC) + g], wr=[('xc', xk)])
                b = self.bank()
                for li in range(32):
                    P.op('pe', lambda e, b=b, kv=kv, li=li, xc=xc: e.matmul(
                        ps[:, b, 0:NCMP], w1[kv][:, li, :], xc[:, li:li + 16 * (NCMP - 1) + 1:16],
                        start=(li == 0), stop=(li == 31)), rd=[('w1', kv), ('xc', xk)], wr=[('ps', b)])
                P.op('act', lambda e, b=b, kv=kv: e.activation(out=xs[:, 0:NCMP], in_=ps[:, b, 0:NCMP], func=AF.Identity,
                                                              bias=pb[:, kv:kv + 1]),
                     rd=[('ps', b), ('pb', kv)], wr=[('xs',)])
                P.op('dve', lambda e: e.tensor_tensor(out=x2[:, 0:NCMP], in0=xs[:, 0:NCMP], in1=xs[:, 0:NCMP], op=ALU.mult),
                     rd=[('xs',)], wr=[('x2',)])
                P.op('dve', lambda e: e.tensor_scalar(out=x2[:, 0:NCMP], in0=x2[:, 0:NCMP], scalar1=0.044715, scalar2=1.0,
                                                      op0=ALU.mult, op1=ALU.add), rd=[('x2',)], wr=[('x2',)])
                P.op('dve', lambda e: e.tensor_tensor(out=yy[:, 0:NCMP], in0=x2[:, 0:NCMP], in1=xs[:, 0:NCMP], op=ALU.mult),
                     rd=[('x2',), ('xs',)], wr=[('yy',)])
                P.op('act', lambda e: e.activation(out=yy[:, 0:NCMP], in_=yy[:, 0:NCMP], func=AF.Sigmoid,
                                                   scale=1.5957691216057308), rd=[('yy',)], wr=[('yy',)])
                P.op('dve', lambda e: e.tensor_tensor(out=gT[:, 0:NCMP], in0=xs[:, 0:NCMP], in1=yy[:, 0:NCMP], op=ALU.mult),
                     rd=[('xs',), ('yy',)], wr=[('gT',)])
                b3 = self.bank()
                if kv == 0:
                    P.op('pe', lambda e, b3=b3: e.matmul(ps[:, b3, 0:NCMP], w2[0], gT[:, 0:NCMP], start=True, stop=True),
                         rd=[('w2', 0), ('gT',)], wr=[('ps', b3)])
                    P.op('dve', lambda e, b3=b3, g=g: e.tensor_copy(out=kcmpT[g][:, 0:NCMP], in_=ps[:, b3, 0:NCMP]),
                         rd=[('ps', b3)], wr=[('kcmpT', g)])
                else:
                    P.op('pe', lambda e, b3=b3: e.matmul(ps[0:NCMP, b3, 0:128], gT[:, 0:NCMP], w2[1], start=True, stop=True),
                         rd=[('w2', 1), ('gT',)], wr=[('ps', b3)])
                    P.op('dve', lambda e, b3=b3, g=g: e.tensor_copy(out=Rt[g][0:NCMP, 0:128], in_=ps[0:NCMP, b3, 0:128]),
                         rd=[('ps', b3)], wr=[('Rt0', g)])
        P.barrier()
        sb.release(m1)
        ring = [sb.take(16 * 512, BF16).rearrange("p (k q) -> p k q", k=16) for _ in range(2)]
        QT3 = [sb.take(S, BF16) for _ in range(3)]
        KS = sb.take(S, BF16)
        KW = sb.take(S, BF16)
        VS = sb.take(16 * 130, BF16).rearrange("p (k d) -> p k d", k=16)
        VW = sb.take(16 * 130, BF16).rearrange("p (k d) -> p k d", k=16)
        ET = sb.take(S, BF16)
        Ycs = [sb.take(S, F32) for _ in range(2)]
        negselT = sb.take(S, BF16)
        imp = sb.take(16 * 32, F32).rearrange("p (a m) -> p a m", a=16)
        ocmp = [sb.take(16 * 128, F32).rearrange("p (a d) -> p a d", a=16) for _ in range(3)]
        mixstg = [sb.take(S, BF16) for _ in range(2)]
        acc = [sb.take(4 * 128, F32).rearrange("p (j d) -> p j d", j=4) for _ in range(2)]
        obf = [sb.take(4 * 128, BF16).rearrange("p (j d) -> p j d", j=4) for _ in range(2)]
        rc16 = sb.take(16, F32)
        rg16 = sb.take(16, F32)
        craw = sb.take(16 * 162, F32).rearrange("p (a d) -> p a d", a=16)
        imp2 = sb.take(16 * 32, F32).rearrange("p (a m) -> p a m", a=16)
        scb = sb.take(16 * 32, F32).rearrange("p (a m) -> p a m", a=16)
        selm = sb.take(16 * 32, F32).rearrange("p (a m) -> p a m", a=16)
        mx = sb.take(16 * 8, F32).rearrange("p (a m) -> p a m", a=16)
        nsb = sb.take(16 * 32, BF16).rearrange("p (a m) -> p a m", a=16)
        raw = [sb.take(4 * 130, F32).rearrange("p (j d) -> p j d", j=4) for _ in range(2)]
        rr = [sb.take(4, F32) for _ in range(2)]
        tmpw = sb.take(4 * 128, F32).rearrange("p (j d) -> p j d", j=4)
        P.op('pool', lambda e: e.memset(negselT, 0.0), wr=[('nsT', q) for q in range(4)])
        for Vt, nm in ((VS, 'VS'), (VW, 'VW')):
            P.op('pool', lambda e, Vt=Vt: e.memset(Vt[:, :, 128:129], 1.0), wr=[(nm,)])
            P.op('pool', lambda e, Vt=Vt: e.memset(Vt[:, :, 129:130], 0.0), wr=[(nm,)])
        for g in range(2):
            P.dma('sp', KS, self.PT[PT_NKS + g], wr=[('KS',)])
            P.dma('sp', KW, self.PT[PT_NKW + g], wr=[('KW',)])
            P.dma('sp', VS[:, :, 0:128], self.VT[:, :, VT_NVS + 128 * g:VT_NVS + 128 * (g + 1)].rearrange("k p d -> p k d"),
                  wr=[('VS',)])
            P.dma('sp', VW[:, :, 0:128], self.VT[:, :, VT_NVW + 128 * g:VT_NVW + 128 * (g + 1)].rearrange("k p d -> p k d"),
                  wr=[('VW',)])
            cnt = 0
            for hh in range(3):
                h = 3 * g + hh
                yk = hh % 2
                Ycr = Ycs[yk][:, ::-1]
                if hh == 0:
                    for h2 in range(3):
                        P.dma('sp', QT3[h2], self.PT[PT_NQ + 3 * g + h2], wr=[('QT3', h2)])
                    for h2 in range(2):
                        P.dma('sp', Ycs[h2][0:NCMP, :], bass.AP(tensor=self.TCr_h, offset=(4 + 3 * g + h2) * 4352 + 287,
                                                                ap=[[16, NCMP], [1, S]]), wr=[('Yc', h2)])
                if hh == 2:
                    P.dma('sp', Ycs[0][0:NCMP, :], bass.AP(tensor=self.TCr_h, offset=(4 + h) * 4352 + 287,
                                                           ap=[[16, NCMP], [1, S]]), wr=[('Yc', 0)])
                for qc in range(4):
                    b = self.bank()
                    q0, q1 = qc * 512, (qc + 1) * 512
                    P.op('pe', lambda e, b=b, hh=hh, q0=q0, q1=q1, g=g: e.matmul(
                        ps[0:NCMP, b, :], kcmpT[g][:, 0:NCMP], QT3[hh][:, q0:q1], start=True, stop=True),
                        rd=[('kcmpT', g), ('QT3', hh)], wr=[('ps', b)])
                    P.op('dve', lambda e, b=b, q0=q0, q1=q1, Ycr=Ycr: e.tensor_tensor(
                        out=ps[0:NCMP, b, :], in0=ps[0:NCMP, b, :], in1=Ycr[0:NCMP, q0:q1], op=ALU.add),
                        rd=[('ps', b), ('Yc', yk)], wr=[('ps', b)])
                    P.op('act', lambda e, b=b, q0=q0, q1=q1, h=h: e.activation(
                        out=ET[0:NCMP, q0:q1], in_=ps[0:NCMP, b, :], func=AF.Exp, scale=sc,
                        bias=self.CB[0:NCMP, 4 + h:5 + h]), rd=[('ps', b)], wr=[('ET', qc)])
                for qi in range(16):
                    b = self.bank()
                    P.op('pe', lambda e, b=b, qi=qi, g=g: e.matmul(
                        ps[:, b, 0:161], ET[0:NCMP, qi * 128:(qi + 1) * 128], Rt[g][0:NCMP, 0:161], start=True, stop=True),
                        rd=[('ET', qi // 4), ('Rt0', g), ('Rt1', g), ('Rt2', g)], wr=[('ps', b)])
                    if qi % 2 == 0:
                        P.op('act', lambda e, b=b, qi=qi: e.copy(out=craw[:, qi, 0:161], in_=ps[:, b, 0:161]),
                             rd=[('ps', b)], wr=[('craw', qi)])
                    else:
                        P.op('dve', lambda e, b=b, qi=qi: e.tensor_copy(out=craw[:, qi, 0:161], in_=ps[:, b, 0:161]),
                             rd=[('ps', b)], wr=[('craw', qi)])
                ct = [('craw', qi) for qi in range(16)]
                P.op('dve', lambda e: e.tensor_scalar(out=rc16.unsqueeze(2), in0=craw[:, :, 128:129], scalar1=1e-30,
                                                      scalar2=None, op0=ALU.max), rd=ct, wr=[('rc16',)])
                P.op('dve', lambda e: e.reciprocal(out=rc16, in_=rc16), rd=[('rc16',)], wr=[('rc16',)])
                P.op('dve', lambda e, h=h: e.tensor_tensor(out=rg16.unsqueeze(2), in0=rc16.unsqueeze(2),
                                                           in1=gates[:, :, 3 * h:3 * h + 1], op=ALU.mult),
                     rd=[('rc16',), ('gates',)], wr=[('rg16',)])
                P.op('pool', lambda e, hh=hh: e.tensor_tensor(out=ocmp[hh], in0=craw[:, :, 0:128],
                                                              in1=rg16.unsqueeze(2).to_broadcast([128, 16, 128]), op=ALU.mult),
                     rd=ct + [('rg16',)], wr=[('ocmp', hh, qi) for qi in range(16)])
                if hh == 0:
                    P.op('dve', lambda e: e.tensor_tensor(out=imp, in0=craw[:, :, 129:161],
                                                          in1=rc16.unsqueeze(2).to_broadcast([128, 16, 32]), op=ALU.mult),
                         rd=ct + [('rc16',)], wr=[('imp',)])
                else:
                    P.op('dve', lambda e: e.tensor_tensor(out=imp2, in0=craw[:, :, 129:161],
                                                          in1=rc16.unsqueeze(2).to_broadcast([128, 16, 32]), op=ALU.mult),
                         rd=ct + [('rc16',)], wr=[('imp2',)])
                    P.op('dve', lambda e: e.tensor_tensor(out=imp, in0=imp, in1=imp2, op=ALU.add),
                         rd=[('imp',), ('imp2',)], wr=[('imp',)])
            P.op('dve', lambda e: e.tensor_tensor(out=scb, in0=imp, in1=self.FM, op=ALU.add), rd=[('imp',)], wr=[('scb',)])
            for qi in range(16):
                P.op('dve', lambda e, qi=qi: e.max(out=mx[:, qi, :], in_=scb[:, qi, :]), rd=[('scb',)], wr=[('mx', qi)])
            P.op('dve', lambda e: e.tensor_tensor(out=selm, in0=scb, in1=mx[:, :, 7:8].to_broadcast([128, 16, 32]),
                                                  op=ALU.is_ge), rd=[('scb',)] + [('mx', qi) for qi in range(16)], wr=[('selm',)])
            P.op('dve', lambda e: e.tensor_scalar(out=nsb, in0=selm, scalar1=-1.0, scalar2=NB, op0=ALU.add, op1=ALU.mult),
                 rd=[('selm',)], wr=[('nsb',)])
            for qc in range(4):
                bt = self.bank()
                psb = ps[0:32, bt, 0:256].bitcast(BF16)
                for j in range(4):
                    qi = 4 * qc + j
                    P.op('pe', lambda e, j=j, qi=qi, psb=psb: e.transpose(out=psb[:, j * 128:(j + 1) * 128], in_=nsb[:, qi, :],
                                                                          identity=self.identB),
                         rd=[('nsb',)], wr=[('ps', bt)])
                P.op('act', lambda e, psb=psb, qc=qc: e.copy(out=negselT[0:32, qc * 512:(qc + 1) * 512], in_=psb),
                     rd=[('ps', bt)], wr=[('nsT', qc)])
            joblist = []
            for hh in range(3):
                h = 3 * g + hh
                tab = lambda mm, h=h: (self.BT[:, 4 + h, 0, :] if mm == 0 else (self.BT[:, 4 + h, 1, :] if mm == 1 else None))
                ss = self.Stream()
                ss.mode = 'causal'
                ss.lhs = lambda kt: KS[:, kt * 128:(kt + 1) * 128]
                ss.QT = QT3[hh]
                ss.extra = lambda kt, q0, q1: [(self.Epad[:, kt * 128:(kt + 1) * 128], negselT[:, q0:q1])]
                ss.table = tab
                ss.cb = self.CB[:, 4 + h:5 + h]
                ss.scale = sc
                ss.V = lambda kt: VS[:, kt, 0:129]
                ss.ncols = 129
                ss.rtoks = [('KS',), ('QT3', hh)] + [('nsT', q) for q in range(4)]
                ss.vtoks = [('VS',)]

                def epi_s(qi, b, ji):
                    j = qi % 4
                    P.op('dve', lambda e: e.tensor_copy(out=raw[0][:, j, 0:129], in_=ps[:, b, 0:129]),
                         rd=[('ps', b)], wr=[('raw', 0, j)])

                def post_s(qc, ji, h=h, hh=hh):
                    par = (ji // 2) % 2
                    rt = [('raw', 0, j) for j in range(4)]
                    P.op('dve', lambda e: e.reciprocal(out=rr[0].unsqueeze(2), in_=raw[0][:, :, 128:129]),
                         rd=rt, wr=[('rr', 0)])
                    P.op('dve', lambda e: e.tensor_tensor(out=rr[0].unsqueeze(2), in0=rr[0].unsqueeze(2),
                                                          in1=gates[:, 4 * qc:4 * qc + 4, 3 * h + 1:3 * h + 2], op=ALU.mult),
                         rd=[('rr', 0), ('gates',)], wr=[('rr', 0)])
                    P.op('dve', lambda e: e.tensor_tensor(out=acc[par], in0=raw[0][:, :, 0:128],
                                                          in1=rr[0].unsqueeze(2).to_broadcast([128, 4, 128]), op=ALU.mult),
                         rd=rt + [('rr', 0)], wr=[('acc', par)])
                    P.op('pool', lambda e: e.tensor_tensor(out=acc[par], in0=acc[par], in1=ocmp[hh][:, 4 * qc:4 * qc + 4, :],
                                                           op=ALU.add),
                         rd=[('acc', par)] + [('ocmp', hh, q) for q in range(4 * qc, 4 * qc + 4)], wr=[('acc', par)])
                ss.epi = epi_s
                ss.post_a = post_s
                ss.post_b = lambda qc, ji: None
                sw = self.Stream()
                sw.mode = 'window'
                sw.lhs = lambda kt: KW[:, kt * 128:(kt + 1) * 128]
                sw.QT = QT3[hh]
                sw.extra = lambda kt, q0, q1: []
                sw.table = lambda mm, tab=tab: (self.W4F if mm == 4 else tab(mm))
                sw.cb = self.CB[:, 4 + h:5 + h]
                sw.scale = sc
                sw.V = lambda kt: VW[:, kt, 0:129]
                sw.ncols = 129
                sw.rtoks = [('KW',), ('QT3', hh)]
                sw.vtoks = [('VW',)]

                def epi_w(qi, b, ji):
                    j = qi % 4
                    P.op('dve', lambda e: e.tensor_copy(out=raw[1][:, j, 0:129], in_=ps[:, b, 0:129]),
                         rd=[('ps', b)], wr=[('raw', 1, j)])

                def post_w(qc, ji, h=h, hh=hh):
                    par = (ji // 2) % 2
                    rt = [('raw', 1, j) for j in range(4)]
                    P.op('dve', lambda e: e.reciprocal(out=rr[1].unsqueeze(2), in_=raw[1][:, :, 128:129]),
                         rd=rt, wr=[('rr', 1)])
                    P.op('dve', lambda e: e.tensor_tensor(out=rr[1].unsqueeze(2), in0=rr[1].unsqueeze(2),
                                                          in1=gates[:, 4 * qc:4 * qc + 4, 3 * h + 2:3 * h + 3], op=ALU.mult),
                         rd=[('rr', 1), ('gates',)], wr=[('rr', 1)])
                    P.op('dve', lambda e: e.tensor_tensor(out=tmpw, in0=raw[1][:, :, 0:128],
                                                          in1=rr[1].unsqueeze(2).to_broadcast([128, 4, 128]), op=ALU.mult),
                         rd=rt + [('rr', 1)], wr=[('tmpw',)])
                    P.op('pool', lambda e: e.tensor_tensor(out=obf[par], in0=tmpw, in1=acc[par], op=ALU.add),
                         rd=[('tmpw',), ('acc', par)], wr=[('obf', par, j) for j in range(4)])
                sw.epi = epi_w
                sw.post_a = post_w
                sw.post_b = lambda qc, ji, h=h, hh=hh: self.post_transpose(obf, (ji // 2) % 2, mixstg, hh % 2, qc, 10 + h, 'act')
                for qc in range(4):
                    joblist.append((ss, qc))
                    joblist.append((sw, qc))
            self.run_jobs(joblist, ring)
        P.barrier()
        sb.release(m)

    def phase_C(self, l, s):
        P, sb, ps = self.P, self.sb, self.ps
        m = sb.mark()
        csub = "1234"
        mixT = sb.take(NCH * S, BF16).rearrange("p (c t) -> p c t", c=NCH)
        for hd in range(16 if "1" in csub else 0):
            P.dma('sp', mixT[:, hd, :], self.MT[hd], wr=[('mixT', hd)])
        wbuf = [sb.take(NCH * 512, BF16).rearrange("p (c f) -> p c f", c=NCH) for _ in range(2)]
        hold = [sb.take(S, F32) for _ in range(2)]
        hnew = [sb.take(S, F32) for _ in range(2)]
        n = 0
        for i in range(4 if "1" in csub else 0):
            wb = wbuf[i % 2]
            wtoks = self.load_w_slab(wb, i % 2, self.w_out[l], [(512 * i, 512)])
            for ft in range(4):
                co = 4 * i + ft
                k2 = n % 2
                n += 1
                P.dma('sp', hold[k2], self.hT[s, :, co, :], rd=[('hTc', co)], wr=[('hold', k2)])
                for j in range(4):
                    b = self.bank()
                    for c in range(NCH):
                        P.op('pe', lambda e, b=b, c=c, wb=wb, ft=ft, j=j: e.matmul(
                            ps[:, b, :], wb[:, c, ft * 128:(ft + 1) * 128], mixT[:, c, j * 512:(j + 1) * 512],
                            start=(c == 0), stop=(c == NCH - 1)), rd=wtoks + [('mixT', c)], wr=[('ps', b)])
                    P.op('dve', lambda e, b=b, k2=k2, j=j: e.tensor_tensor(
                        out=hnew[k2][:, j * 512:(j + 1) * 512], in0=ps[:, b, :], in1=hold[k2][:, j * 512:(j + 1) * 512],
                        op=ALU.add), rd=[('ps', b), ('hold', k2)], wr=[('hnew', k2, j)])
                P.dma('sp', self.hT[s, :, co, :], hnew[k2], rd=[('hnew', k2, j) for j in range(4)], wr=[('hTc', co)])
        P.barrier()
        sb.release(m)
        u2T = sb.take(NCH * S, BF16).rearrange("p (c t) -> p c t", c=NCH)
        cw = sb.take(3 * 88, F32).rearrange("p (t f) -> p t f", t=3)
        cbv = sb.take(88, F32)
        wbuf = [sb.take(NCH * 512, BF16).rearrange("p (c f) -> p c f", c=NCH) for _ in range(2)]
        up_segs = lambda i: [(256 * i, 256), (DFF + 256 * i, 256)]
        pre_toks = [self.load_w_slab(wbuf[k], k, self.ffn_w_up[l], up_segs(k)) for k in range(2)]
        m2 = sb.mark()
        cwA = sb.take(4 * 128, F32).rearrange("p (t q) -> p t q", t=4)
        P.dma('sp', cwA[0:88, 0:3, :], self.ffn_conv_w[l].rearrange("t (f p) -> f t p", p=128), wr=[('cwA', 0)])
        P.dma('sp', cwA[0:88, 3, :], self.ffn_conv_b[l].rearrange("(f p) -> f p", p=128), wr=[('cwA', 1)])
        for t in range(4):
            b = self.bank()
            P.op('pe', lambda e, b=b, t=t: e.transpose(out=ps[:, b, 0:88], in_=cwA[0:88, t, :], identity=self.identF[0:88, 0:88]),
                 rd=[('cwA', 0), ('cwA', 1)], wr=[('ps', b)])
            if t < 3:
                P.op('dve', lambda e, b=b, t=t: e.tensor_copy(out=cw[:, t, :], in_=ps[:, b, 0:88]), rd=[('ps', b)], wr=[('cw',)])
            else:
                P.op('dve', lambda e, b=b: e.tensor_copy(out=cbv, in_=ps[:, b, 0:88]), rd=[('ps', b)], wr=[('cw',)])
        self.rmsnorm_to_uT(lambda j: ('none',), self.hT[s], self.g2, l * NCH, u2T)
        P.barrier()
        sb.release(m2)
        hgu = [sb.take(S + 2, F32) for _ in range(4)]
        cgu = [sb.take(S, F32) for _ in range(2)]
        aT = [sb.take(S, BF16) for _ in range(2)]
        for k in range(4):
            P.op('pool', lambda e, k=k: e.memset(hgu[k][:, 0:2], 0.0), wr=[('hgz', k)])
        utoks = lambda c, t0, t1: [('uT', c, i) for i in range(t0 // 256, (t1 + 255) // 256)]
        nt = 0
        for i in range(22 if "3" in csub else 0):
            wb = wbuf[i % 2]
            wtoks = pre_toks[i] if i < 2 else self.load_w_slab(wb, i % 2, self.ffn_w_up[l], up_segs(i))
            for ft in range(2):
                f = 2 * i + ft
                par = nt % 2
                nt += 1
                for br in range(2):
                    hb = hgu[2 * par + br]
                    hk = 2 * par + br
                    wcol = br * 256 + ft * 128
                    fi = f + 44 * br
                    for j in range(4):
                        b = self.bank()
                        for c in range(NCH):
                            P.op('pe', lambda e, b=b, c=c, wb=wb, wcol=wcol, j=j: e.matmul(
                                ps[:, b, :], wb[:, c, wcol:wcol + 128], u2T[:, c, j * 512:(j + 1) * 512],
                                start=(c == 0), stop=(c == NCH - 1)), rd=wtoks, wr=[('ps', b)])
                        P.op('act', lambda e, b=b, hb=hb, j=j: e.copy(out=hb[:, 2 + j * 512:2 + (j + 1) * 512], in_=ps[:, b, :]),
                             rd=[('ps', b), ('hgz', hk)], wr=[('hgu', hk, j)])
                    cg = cgu[br]
                    htok = [('hgu', hk, j) for j in range(4)]
                    P.op('dve', lambda e, hb=hb, cg=cg, fi=fi: e.tensor_scalar(
                        out=cg, in0=hb[:, 2:S + 2], scalar1=cw[:, 2, fi:fi + 1], scalar2=cbv[:, fi:fi + 1],
                        op0=ALU.mult, op1=ALU.add), rd=htok, wr=[('cgu', br)])
                    P.op('dve', lambda e, hb=hb, cg=cg, fi=fi: e.scalar_tensor_tensor(
                        out=cg, in0=hb[:, 1:S + 1], scalar=cw[:, 1, fi:fi + 1], in1=cg, op0=ALU.mult, op1=ALU.add),
                        rd=htok + [('cgu', br)], wr=[('cgu', br)])
                    P.op('dve', lambda e, hb=hb, cg=cg, fi=fi: e.scalar_tensor_tensor(
                        out=cg, in0=hb[:, 0:S], scalar=cw[:, 0, fi:fi + 1], in1=cg, op0=ALU.mult, op1=ALU.add),
                        rd=htok + [('cgu', br)], wr=[('cgu', br)])
                P.op('act', lambda e: e.activation(out=cgu[0], in_=cgu[0], func=AF.Silu), rd=[('cgu', 0)], wr=[('cgu', 0)])
                P.op('dve', lambda e, par=par: e.tensor_tensor(out=aT[par], in0=cgu[0], in1=cgu[1], op=ALU.mult),
                     rd=[('cgu', 0), ('cgu', 1)], wr=[('aT', par)])
                P.dma('sp', self.AT[f], aT[par], rd=[('aT', par)], wr=[('AT', f)])
        P.barrier()
        sb.release(m)
        m = sb.mark()
        actH = sb.take(NFF * 1024, BF16).rearrange("p (f t) -> p f t", f=NFF)
        wd = [sb.take(NFF * 256, BF16).rearrange("p (f c) -> p f c", f=NFF) for _ in range(2)]
        hold4 = [sb.take(1024, F32) for _ in range(2)]
        hnew4 = [sb.take(1024, F32) for _ in range(2)]
        nw = 0
        n = 0
        for half in range(2 if "4" in csub else 0):
            t0 = half * 1024
            for f in range(NFF):
                P.dma('sp', actH[:, f, :], self.AT[f, :, t0:t0 + 1024], wr=[('actH', f)])
            for i in range(8):
                wk = nw % 2
                nw += 1
                P.dma('pool', wd[wk], self.ffn_w_down[l, :, 256 * i:256 * (i + 1)].rearrange("(fc p) c -> p fc c", p=128),
                      wr=[('wd', wk)])
                for ft in range(2):
                    co = 2 * i + ft
                    k2 = n % 2
                    n += 1
                    P.dma('sp', hold4[k2], self.hT[s, :, co, t0:t0 + 1024], rd=[('hTd', co, half)], wr=[('hold4', k2)])
                    for j in range(2):
                        b = self.bank()
                        for fc in range(NFF):
                            P.op('pe', lambda e, b=b, fc=fc, wk=wk, ft=ft, j=j: e.matmul(
                                ps[:, b, :], wd[wk][:, fc, ft * 128:(ft + 1) * 128], actH[:, fc, j * 512:(j + 1) * 512],
                                start=(fc == 0), stop=(fc == NFF - 1)), rd=[('wd', wk), ('actH', fc)], wr=[('ps', b)])
                        P.op('dve', lambda e, b=b, k2=k2, j=j: e.tensor_tensor(
                            out=hnew4[k2][:, j * 512:(j + 1) * 512], in0=ps[:, b, :], in1=hold4[k2][:, j * 512:(j + 1) * 512],
                            op=ALU.add), rd=[('ps', b), ('hold4', k2)], wr=[('hnew4', k2, j)])
                    P.dma('sp', self.hT[s, :, co, t0:t0 + 1024], hnew4[k2], rd=[('hnew4', k2, j) for j in range(2)],
                          wr=[('hTd', co, half)])
        P.barrier()
        sb.release(m)

    def phase_Z(self, s):
        P, sb, ps = self.P, self.sb, self.ps
        m = sb.mark()
        yb = [sb.take(NCH * 256, F32).rearrange("p (c t) -> p c t", c=NCH) for _ in range(2)]
        ostg = [sb.take(D, F32) for _ in range(2)]
        cnt = [0]

        def out_cb(i, hb, rstd):
            y = yb[i % 2]
            for c in range(NCH):
                P.op('dve', lambda e, c=c: e.scalar_tensor_tensor(
                    out=y[:, c, :], in0=hb[:, c, :], scalar=self.g3[:, c:c + 1], in1=rstd, op0=ALU.mult, op1=ALU.mult),
                    rd=[('hb', i % 2), ('rstd',)], wr=[('yb', i % 2, c)])
            for tt in range(2):
                ok = cnt[0] % 2
                cnt[0] += 1
                for cg4 in range(4):
                    b = self.bank()
                    for cc in range(4):
                        c = cg4 * 4 + cc
                        P.op('pe', lambda e, b=b, cc=cc, c=c, tt=tt: e.transpose(
                            out=ps[:, b, cc * 128:(cc + 1) * 128], in_=y[:, c, tt * 128:(tt + 1) * 128], identity=self.identF),
                            rd=[('yb', i % 2, c)], wr=[('ps', b)])
                    if cg4 % 2 == 0:
                        P.op('act', lambda e, b=b, cg4=cg4, ok=ok: e.copy(out=ostg[ok][:, cg4 * 512:(cg4 + 1) * 512], in_=ps[:, b, :]),
                             rd=[('ps', b)], wr=[('ostg', ok, cg4)])
                    else:
                        P.op('dve', lambda e, b=b, cg4=cg4, ok=ok: e.tensor_copy(out=ostg[ok][:, cg4 * 512:(cg4 + 1) * 512], in_=ps[:, b, :]),
                             rd=[('ps', b)], wr=[('ostg', ok, cg4)])
                r0 = i * 256 + tt * 128
                P.dma('sp', self.out[s, r0:r0 + 128, :], ostg[ok], rd=[('ostg', ok, q) for q in range(4)], wr=[('out', s, r0)])
        self.rmsnorm_to_uT(lambda j: ('none',), self.hT[s], self.g3, 0, None, out_cb=out_cb)
        P.barrier()
        sb.release(m)

CST_IDENT = 0
CST_NM0 = 128
CST_W4 = 256
CST_OH = 384
CST_INVSC = 640
CST_OV = 641
CST_FM = 673
CST_E = 1185
CST_W = 3233


def t5_bucket_np(n):
    n = np.maximum(n, 0)
    nf = np.maximum(n, 1).astype(np.float32)
    large = 16 + (np.log(nf / np.float32(16)) / np.float32(math.log(128 / 16)) * np.float32(16)).astype(np.int32)
    large = np.minimum(large, 31)
    return np.where(n < 16, n, large)


def make_cst():
    c = np.zeros((128, CST_W), np.float32)
    c[:, CST_IDENT:CST_IDENT + 128] = np.eye(128, dtype=np.float32)
    p = np.arange(128)[:, None]
    q = np.arange(128)[None, :]
    c[:, CST_NM0:CST_NM0 + 128] = np.where(p > q, -BIG, 0.0)
    c[:, CST_W4:CST_W4 + 128] = np.where(q >= p, -BIG, 0.0)
    bk = t5_bucket_np(np.arange(256))
    oh = np.zeros((32, 256), np.float32)
    oh[bk, np.arange(256)] = 1.0
    oh[31, :] -= 1.0
    c[0:32, CST_OH:CST_OH + 256] = oh[:, ::-1]
    c[0:4, CST_INVSC] = 8.0
    c[4:10, CST_INVSC] = math.sqrt(128.0)
    cs_ = np.arange(NCMP) * 16
    ss_ = np.arange(NSEL) * 64
    ov = np.clip(np.minimum(cs_[:, None] + 32, ss_[None, :] + 64) - np.maximum(cs_[:, None], ss_[None, :]), 0, None) / 32.0
    c[0:NCMP, CST_OV:CST_OV + 32] = ov
    t = np.arange(S)
    blk = t // 64
    j = np.arange(NSEL)
    valid = j[None, :] <= blk[:, None]
    forced = (j[None, :] == 0) | (j[None, :] == blk[:, None]) | (j[None, :] == blk[:, None] - 1)
    fm = np.where(valid, np.where(forced, 1e4, 0.0), -1e30).astype(np.float32)
    c[:, CST_FM:CST_FM + 512] = fm.reshape(16, 128, 32).transpose(1, 0, 2).reshape(128, 512)
    k = np.arange(S)
    e = np.zeros((32, S), np.float32)
    e[k // 64, k] = 1.0
    c[0:32, CST_E:CST_E + S] = e
    return c


def make_in_map(inputs, seqs, L):
    f = lambda a: np.ascontiguousarray(np.asarray(a, dtype=np.float32))
    m = {
        "x": f(inputs["x"][seqs]),
        "attn_norm_g": f(inputs["attn_norm_g"][:L]),
        "w_in": f(inputs["w_in"][:L]),
        "fox_f_bias": f(inputs["fox_f_bias"][:L]),
        "diff_lam4": f(np.stack([inputs["diff_lq1"][:L], inputs["diff_lk1"][:L],
                                 inputs["diff_lq2"][:L], inputs["diff_lk2"][:L]], axis=1)),
        "diff_subln_g": f(inputs["diff_subln_g"][:L]),
        "nsa_cmp_pos": f(inputs["nsa_cmp_pos"][:L]),
        "nsa_cmp_w1": f(np.stack([inputs["nsa_cmp_wk1"][:L], inputs["nsa_cmp_wv1"][:L]], axis=1)),
        "nsa_cmp_w2": f(np.stack([inputs["nsa_cmp_wk2"][:L], inputs["nsa_cmp_wv2"][:L]], axis=1)),
        "w_out": f(inputs["w_out"][:L]),
        "ffn_norm_g": f(inputs["ffn_norm_g"][:L]),
        "ffn_w_up": f(inputs["ffn_w_up"][:L]),
        "ffn_conv_w": f(inputs["ffn_conv_w"][:L]),
        "ffn_conv_b": f(inputs["ffn_conv_b"][:L]),
        "ffn_w_down": f(inputs["ffn_w_down"][:L]),
        "rel_bias": f(inputs["rel_bias"]),
        "final_norm_g": f(inputs["final_norm_g"]),
        "cst": make_cst(),
    }
    return m


def kernel(**inputs):
    L, NSEQ, NCORE = 4, 2, 8
    b = Builder(L, NSEQ)
    nc = b.build(phases=("x0", "A", "Bf", "Bd", "Bn", "C", "Z"))
    in_maps = [make_in_map(inputs, slice(NSEQ * i, NSEQ * (i + 1)), L) for i in range(NCORE)]
    res = run_bass_kernel_spmd(nc, in_maps, core_ids=list(range(NCORE)))
    return np.concatenate([np.asarray(r["out"]) for r in res.results], axis=0).astype(np.float32)
```

```python
import math
from contextlib import ExitStack
import numpy as np
import concourse.bass as bass
import concourse.mybir as mybir
from concourse.bass_utils import run_bass_kernel_spmd

F32 = mybir.dt.float32
BF16 = mybir.dt.bfloat16
AF = mybir.ActivationFunctionType
ALU = mybir.AluOpType
AX = mybir.AxisListType

S = 2048
D = 2048
NCH = 16
DFF = 5632
NFF = 44
EPS = 1e-6
BIG = 30000.0
N_IN = 6168
NCMP = 127
NSEL = 32
ENGS = ['pe', 'act', 'dve', 'pool', 'sp']
ENG_ATTR = {'pe': 'tensor', 'act': 'scalar', 'dve': 'vector', 'pool': 'gpsimd', 'sp': 'sync'}
SAME_ENG_SYNC = True
NW = 45056

C_FQ, C_FK, C_FV, C_FF = 0, 768, 1536, 2304
C_DQ, C_DK, C_DV = 2310, 2822, 3334
C_NQ = 3846
C_NKC, C_NVC, C_NKS, C_NVS, C_NKW, C_NVW, C_NG = 4614, 4870, 5126, 5382, 5638, 5894, 6150
PT_FQ, PT_FK, PT_DQ, PT_DK, PT_NQ, PT_NKC, PT_NVC, PT_NKS, PT_NKW = 0, 6, 12, 16, 20, 26, 28, 30, 32
NPT = 34
VT_FV, VT_DV, VT_NVS, VT_NVW = 0, 768, 1280, 1536
VTC = 1792


class Prog:
    def __init__(self, nc, nslots=8):
        self.nc = nc
        self.streams = {e: [] for e in ENGS}
        self.cnt = {e: 0 for e in ENGS}
        self.known = {e: {} for e in ENGS}
        self.last_w = {}
        self.readers = {}
        self.slots = {q: [[f"{q}_d{i}", 0] for i in range(nslots)] for q in ('sp', 'pool', 'act')}
        self.rr = {q: 0 for q in self.slots}
        self.nops = 0

    def op(self, eng, fn, rd=(), wr=(), dma=False):
        wdeps = {}
        rdeps = {}

        def add(dd, ev):
            if ev is None:
                return
            k, v = ev
            if dd.get(k, 0) < v:
                dd[k] = v
        for t in rd:
            add(wdeps, self.last_w.get(t))
            if t[0] == 'ps':
                for k, v in self.readers.get(t, {}).items():
                    add(rdeps, (k, v))
        for t in wr:
            add(wdeps, self.last_w.get(t))
            for k, v in self.readers.get(t, {}).items():
                add(rdeps, (k, v))
        if dma:
            sl = self.slots[eng][self.rr[eng]]
            self.rr[eng] = (self.rr[eng] + 1) % len(self.slots[eng])
            add(wdeps, (sl[0], sl[1]))
            sl[1] += 16
            ev = (sl[0], sl[1])
            inc = (sl[0], 16)
        else:
            self.cnt[eng] += 1
            ev = (eng, self.cnt[eng])
            inc = (eng, 1)
        waits = []
        kn = self.known[eng]
        for dd, isw in ((wdeps, True), (rdeps, False)):
            for k, v in dd.items():
                if v <= 0:
                    continue
                if k == eng:
                    if (not isw) or eng == 'pe' or not SAME_ENG_SYNC:
                        continue
                if kn.get(k, 0) >= v:
                    continue
                kn[k] = v
                waits.append((k, v))
        self.streams[eng].append((waits, fn, inc))
        for t in rd:
            r = self.readers.setdefault(t, {})
            if r.get(ev[0], 0) < ev[1]:
                r[ev[0]] = ev[1]
        for t in wr:
            self.last_w[t] = ev
            self.readers[t] = {}
        self.nops += 1
        return ev

    def dma(self, q, out, in_, rd=(), wr=(), slow=False):
        if slow:
            return self.op(q, lambda e, o=out, i=in_: e.dma_start(out=o, in_=i, allow_slow_non_contiguous=True),
                           rd=rd, wr=wr, dma=True)
        return self.op(q, lambda e, o=out, i=in_: e.dma_start(out=o, in_=i), rd=rd, wr=wr, dma=True)

    def all_events(self):
        evs = [(e, self.cnt[e]) for e in ENGS]
        for q in self.slots:
            for nm, v in self.slots[q]:
                evs.append((nm, v))
        return evs

    def barrier(self):
        evs = self.all_events()
        for eng in ENGS:
            waits = []
            kn = self.known[eng]
            for k, v in evs:
                if k == eng or v <= 0 or kn.get(k, 0) >= v:
                    continue
                kn[k] = v
                waits.append((k, v))
            if waits:
                self.streams[eng].append((waits, None, None))
        self.last_w = {}
        self.readers = {}

    def emit(self):
        nc = self.nc
        keys = list(ENGS)
        for q in self.slots:
            keys += [s[0] for s in self.slots[q]]
        with ExitStack() as st:
            sems = {k: st.enter_context(nc.semaphore(k)) for k in keys}
            block = st.enter_context(nc.Block())
            for eng in ENGS:
                stream = self.streams[eng]

                def body(e, stream=stream):
                    for waits, fn, inc in stream:
                        for k, v in waits:
                            e.wait_ge(sems[k], v)
                        if fn is not None:
                            ins = fn(e)
                            ins.then_inc(sems[inc[0]], inc[1])
                getattr(block, ENG_ATTR[eng])(body)


class Arena:
    def __init__(self, big):
        self.big = big
        self.off = 0

    def take(self, nelem, dtype=F32):
        nb = nelem * (4 if dtype == F32 else 2)
        nw = (nb + 3) // 4
        a = self.off
        self.off += (nw + 7) // 8 * 8
        assert self.off <= NW, f"SBUF arena overflow {self.off}"
        ap = self.big[:, a:a + nw]
        if dtype != F32:
            ap = ap.bitcast(dtype)[:, :nelem]
        return ap

    def mark(self):
        return self.off

    def release(self, m):
        self.off = m


def fm_slabs():
    tiles = []
    for h in range(6):
        tiles.append((C_FQ + 128 * h, PT_FQ + h))
    for h in range(6):
        tiles.append((C_FK + 128 * h, PT_FK + h))
    for h in range(4):
        tiles.append((C_DQ + 128 * h, PT_DQ + h))
    for h in range(4):
        tiles.append((C_DK + 128 * h, PT_DK + h))
    for h in range(6):
        tiles.append((C_NQ + 128 * h, PT_NQ + h))
    for g in range(2):
        tiles.append((C_NKC + 128 * g, PT_NKC + g))
    for g in range(2):
        tiles.append((C_NVC + 128 * g, PT_NVC + g))
    for g in range(2):
        tiles.append((C_NKS + 128 * g, PT_NKS + g))
    for g in range(2):
        tiles.append((C_NKW + 128 * g, PT_NKW + g))
    slabs = []
    for i in range(0, len(tiles), 4):
        grp = tiles[i:i + 4]
        segs = []
        for c0, _ in grp:
            if segs and segs[-1][0] + segs[-1][1] == c0:
                segs[-1] = (segs[-1][0], segs[-1][1] + 128)
            else:
                segs.append((c0, 128))
        slabs.append((segs, [t[1] for t in grp]))
    return slabs


def tok_slabs():
    return [
        ([(C_FV, 512)], 0, 512, False),
        ([(C_FV + 512, 256), (C_DV, 256)], 512, 512, False),
        ([(C_DV + 256, 256), (C_NVS, 256)], 1024, 512, False),
        ([(C_NVW, 256), (C_NG, 18)], 1536, 256, True),
    ]


class Builder:
    def __init__(self, L, NSEQ, debug=False):
        self.L, self.NSEQ, self.debug = L, NSEQ, debug
        nc = self.nc = bass.Bass("TRN2", target_bir_lowering=False)
        self.P = Prog(nc)
        dk = "ExternalOutput" if debug else "Internal"
        din = lambda n, sh: nc.dram_tensor(n, sh, F32, kind="ExternalInput").ap()
        self.x = din("x", [NSEQ, S, D])
        self.attn_norm_g = din("attn_norm_g", [L, D])
        self.w_in = din("w_in", [L, D, N_IN])
        self.fox_f_bias = din("fox_f_bias", [L, 6])
        self.lam4 = din("diff_lam4", [L, 4, 64])
        self.diff_subln_g = din("diff_subln_g", [L, 128])
        self.nsa_cmp_pos = din("nsa_cmp_pos", [L, 32, 128])
        self.nsa_cmp_w1 = din("nsa_cmp_w1", [L, 2, 4096, 128])
        self.nsa_cmp_w2 = din("nsa_cmp_w2", [L, 2, 128, 128])
        self.w_out = din("w_out", [L, D, D])
        self.ffn_norm_g = din("ffn_norm_g", [L, D])
        self.ffn_w_up = din("ffn_w_up", [L, D, 2 * DFF])
        self.ffn_conv_w = din("ffn_conv_w", [L, 3, 2 * DFF])
        self.ffn_conv_b = din("ffn_conv_b", [L, 2 * DFF])
        self.ffn_w_down = din("ffn_w_down", [L, DFF, D])
        self.rel_bias = din("rel_bias", [32, 10])
        self.final_norm_g = din("final_norm_g", [D])
        self.cst = din("cst", [128, CST_W])
        self.out = nc.dram_tensor("out", [NSEQ, S, D], F32, kind="ExternalOutput").ap()
        self.hT = nc.dram_tensor("hT", [NSEQ, 128, NCH, S], F32, kind=dk).ap()
        self.PT = nc.dram_tensor("PT", [NPT, 128, S], BF16, kind=dk).ap()
        self.VT = nc.dram_tensor("VT", [128, VTC // 128, 16, 128], BF16, kind=dk).ap()
        self.GT = nc.dram_tensor("GT", [16, 128, 18], F32, kind=dk).ap()
        self.FAR = nc.dram_tensor("FAR", [6, 6, S], BF16, kind=dk).ap()
        self.FAL = nc.dram_tensor("FAL", [6, 6, S], BF16, kind=dk).ap()
        self.TCr_h = nc.dram_tensor("TCr", [11, 4352], F32, kind=dk)
        self.TCr = self.TCr_h.ap()[0:10]
        self.MT = nc.dram_tensor("MT", [16, 128, S], BF16, kind=dk).ap()
        self.AT = nc.dram_tensor("AT", [NFF, 128, S], BF16, kind="Internal").ap()

    def build(self, phases=("x0", "A")):
        nc, P = self.nc, self.P
        with ExitStack() as st:
            big = st.enter_context(nc.sbuf_tensor("big", [128, NW], F32))
            self.ps = st.enter_context(nc.psum_tensor("ps", [128, 8, 512], F32))
            self.sb = Arena(big)
            self.bank_rr = 0
            self.setup_consts()
            for s in range(self.NSEQ):
                if "x0" in phases:
                    self.phase_x0(s)
            for l in range(self.L):
                for s in range(self.NSEQ):
                    if "A" in phases:
                        self.phase_A(l, s)
                    if "Bf" in phases:
                        self.phase_B_fox(l, s)
                    if "Bd" in phases:
                        self.phase_B_diff(l, s)
                    if "Bn" in phases:
                        self.phase_B_nsa(l, s)
                    if "C" in phases:
                        self.phase_C(l, s)
            for s in range(self.NSEQ):
                if "Z" in phases:
                    self.phase_Z(s)
            P.barrier()
            P.emit()
        return nc

    def bank(self):
        b = self.bank_rr
        self.bank_rr = (self.bank_rr + 1) % 8
        return b

    def setup_consts(self):
        P, sb, nc, ps = self.P, self.sb, self.nc, self.ps
        L = self.L
        cst = self.cst
        self.identF = sb.take(128, F32)
        self.NM0F = sb.take(128, F32)
        self.W4F = sb.take(128, F32)
        self.FM = sb.take(512, F32).rearrange("p (a m) -> p a m", a=16)
        P.dma('sp', self.identF, cst[:, CST_IDENT:CST_IDENT + 128], wr=[('identF',)])
        P.dma('sp', self.NM0F, cst[:, CST_NM0:CST_NM0 + 128], wr=[('NM0F',)])
        P.dma('sp', self.W4F, cst[:, CST_W4:CST_W4 + 128], wr=[('W4F',)])
        P.dma('sp', self.FM, cst[:, CST_FM:CST_FM + 512].rearrange("p (a m) -> p a m", a=16), wr=[('FM',)])
        self.identB = sb.take(128, BF16)
        P.op('dve', lambda e: e.tensor_copy(out=self.identB, in_=self.identF), rd=[('identF',)], wr=[('identB',)])
        self.onesB = sb.take(128, BF16)
        P.op('dve', lambda e: e.memset(self.onesB, 1.0), wr=[('onesB',)])
        self.OVb = sb.take(32, BF16)
        P.dma('pool', self.OVb[0:NCMP, :], cst[0:NCMP, CST_OV:CST_OV + 32], wr=[('OVb',)])
        self.Epad = sb.take(S, BF16)
        P.op('pool', lambda e: e.memset(self.Epad, 0.0), wr=[('Epad',)])
        P.dma('pool', self.Epad[0:32, :], cst[0:32, CST_E:CST_E + S], wr=[('Epad',)])
        self.BT = sb.take(10 * 2 * 128, F32).rearrange("p (h k c) -> p h k c", h=10, k=2)
        self.CB = sb.take(10, F32)
        P.dma('sp', self.CB, self.rel_bias[31:32, :].partition_broadcast(128), wr=[('CB',)])
        self.g1 = sb.take(L * NCH, F32)
        self.g2 = sb.take(L * NCH, F32)
        self.g3 = sb.take(NCH, F32)
        P.dma('sp', self.g1.rearrange("p (l c) -> p l c", l=L),
              self.attn_norm_g.rearrange("l (c p) -> p l c", p=128), wr=[('g1',)], slow=True)
        P.dma('sp', self.g2.rearrange("p (l c) -> p l c", l=L),
              self.ffn_norm_g.rearrange("l (c p) -> p l c", p=128), wr=[('g2',)], slow=True)
        P.dma('sp', self.g3, self.final_norm_g.rearrange("(c p) -> p c", p=128), wr=[('g3',)], slow=True)
        self.nfb = sb.take(L, F32)
        P.dma('sp', self.nfb[0:6, :], self.fox_f_bias.rearrange("l h -> h l"), wr=[('nfb',)], slow=True)
        P.op('dve', lambda e: e.tensor_scalar(out=self.nfb[0:6, :], in0=self.nfb[0:6, :], scalar1=-1.0, scalar2=None,
                                              op0=ALU.mult), rd=[('nfb',)], wr=[('nfb',)])
        m = sb.mark()
        onesrow = sb.take(S, BF16)
        P.op('pool', lambda e: e.memset(onesrow[0:6, :], 1.0), wr=[('onesrow',)])
        for r in range(3):
            P.dma('sp', self.FAR[:, 3 + r, :], onesrow[0:6, :], rd=[('onesrow',)], wr=[('FAR1', r)])
            P.dma('sp', self.FAL[:, r, :], onesrow[0:6, :], rd=[('onesrow',)], wr=[('FAL1', r)])
        ohF = sb.take(256, F32)
        invsc = sb.take(1, F32)
        relS = sb.take(10, F32)
        tt = sb.take(256, F32)
        fill = sb.take(2048, F32)
        P.dma('sp', ohF[0:32, :], cst[0:32, CST_OH:CST_OH + 256], wr=[('ohF',)])
        P.dma('sp', invsc[0:10, :], cst[0:10, CST_INVSC:CST_INVSC + 1], wr=[('invsc',)], slow=True)
        P.dma('sp', relS[0:32, :], self.rel_bias[:, :], wr=[('relS',)])
        P.op('pe', lambda e: e.matmul(ps[0:10, 0, 0:256], relS[0:32, 0:10], ohF[0:32, 0:256], start=True, stop=True),
             rd=[('ohF',), ('relS',)], wr=[('ps', 0)])
        P.op('dve', lambda e: e.tensor_scalar(out=tt[0:10, :], in0=ps[0:10, 0, 0:256], scalar1=invsc[0:10, 0:1],
                                              scalar2=None, op0=ALU.mult), rd=[('ps', 0), ('invsc',)], wr=[('tt',)])
        P.dma('sp', self.TCr[:, 2048:2304], tt[0:10, :], rd=[('tt',)], wr=[('TCr', 1)])
        P.op('pool', lambda e: e.memset(fill[0:10, :], 0.0), wr=[('fill',)])
        P.dma('sp', self.TCr[:, 0:2048], fill[0:10, :], rd=[('fill',)], wr=[('TCr', 0)])
        P.op('pool', lambda e: e.memset(fill[0:10, :], -BIG), rd=[], wr=[('fill',)])
        P.dma('sp', self.TCr[:, 2304:4352], fill[0:10, :], rd=[('fill',)], wr=[('TCr', 2)])
        P.barrier()
        Y = [sb.take(128, F32) for _ in range(2)]
        n = 0
        for h in range(10):
            for k, off in enumerate((0, 128)):
                y = Y[n % 2]
                src = bass.AP(tensor=self.TCr_h, offset=h * 4352 + 2176 - off, ap=[[1, 128], [1, 128]])
                P.dma('sp', y, src, wr=[('Y', n % 2)])
                P.op('dve', lambda e, y=y, h=h, k=k: e.tensor_copy(out=self.BT[:, h, k, :], in_=y[:, ::-1]),
                     rd=[('Y', n % 2)], wr=[('BT', h, k)])
                n += 1
        P.barrier()
        sb.release(m)

    def phase_x0(self, s):
        P, sb, ps = self.P, self.sb, self.ps
        m = sb.mark()
        xins = [sb.take(4 * D, F32).rearrange("p (a c) -> p a c", a=4) for _ in range(2)]
        stgs = [sb.take(NCH * 512, F32).rearrange("p (c t) -> p c t", c=NCH) for _ in range(2)]
        for j in range(4):
            xin = xins[j % 2]
            stg = stgs[j % 2]
            P.dma('sp', xin, self.x[s, j * 512:(j + 1) * 512, :].rearrange("(a p) c -> p a c", p=128),
                  wr=[('xin', j % 2)])
            for cc in range(NCH):
                b = self.bank()
                for a in range(4):
                    P.op('pe', lambda e, b=b, a=a, cc=cc, xin=xin: e.transpose(
                        out=ps[:, b, a * 128:(a + 1) * 128], in_=xin[:, a, cc * 128:(cc + 1) * 128],
                        identity=self.identF), rd=[('xin', j % 2)], wr=[('ps', b)])
                if cc % 2 == 0:
                    P.op('act', lambda e, b=b, cc=cc, stg=stg: e.copy(out=stg[:, cc, :], in_=ps[:, b, :]),
                         rd=[('ps', b)], wr=[('stg', j % 2, cc)])
                else:
                    P.op('dve', lambda e, b=b, cc=cc, stg=stg: e.tensor_copy(out=stg[:, cc, :], in_=ps[:, b, :]),
                         rd=[('ps', b)], wr=[('stg', j % 2, cc)])
            P.dma('sp', self.hT[s, :, :, j * 512:(j + 1) * 512], stg,
                  rd=[('stg', j % 2, c) for c in range(NCH)], wr=[('hT', s, j)])
        P.barrier()
        sb.release(m)

    def rmsnorm_to_uT(self, src_tok, hT_s, gain, goff, uT, out_cb=None):
        P, sb, ps = self.P, self.sb, self.ps
        hbuf = [sb.take(NCH * 256, F32).rearrange("p (c t) -> p c t", c=NCH) for _ in range(2)]
        sq = sb.take(NCH * 256, BF16).rearrange("p (c t) -> p c t", c=NCH)
        t1 = sb.take(256, F32)
        t2 = sb.take(256, F32)
        rstd = sb.take(256, F32)
        for i in range(8):
            hb = hbuf[i % 2]
            P.dma('sp', hb, hT_s[:, :, i * 256:(i + 1) * 256], rd=[src_tok(i // 2)], wr=[('hb', i % 2)])
            P.op('act', lambda e, hb=hb: e.activation(out=sq, in_=hb, func=AF.Square),
                 rd=[('hb', i % 2)], wr=[('sq',)])
            b = self.bank()
            for c in range(NCH):
                P.op('pe', lambda e, b=b, c=c: e.matmul(ps[:, b, 0:256], self.onesB, sq[:, c, :],
                                                       start=(c == 0), stop=(c == NCH - 1)),
                     rd=[('sq',), ('onesB',)], wr=[('ps', b)])
            P.op('dve', lambda e, b=b: e.tensor_scalar(out=t1, in0=ps[:, b, 0:256], scalar1=EPS * D, scalar2=1.0 / D,
                                                      op0=ALU.add, op1=ALU.mult), rd=[('ps', b)], wr=[('t1',)])
            P.op('act', lambda e: e.activation(out=t2, in_=t1, func=AF.Sqrt), rd=[('t1',)], wr=[('t2',)])
            P.op('dve', lambda e: e.reciprocal(out=rstd, in_=t2), rd=[('t2',)], wr=[('rstd',)])
            if out_cb is not None:
                out_cb(i, hb, rstd)
                continue
            for c in range(NCH):
                P.op('dve', lambda e, hb=hb, c=c, i=i: e.scalar_tensor_tensor(
                    out=uT[:, c, i * 256:(i + 1) * 256], in0=hb[:, c, :], scalar=gain[:, goff + c:goff + c + 1],
                    in1=rstd, op0=ALU.mult, op1=ALU.mult),
                    rd=[('hb', i % 2), ('rstd',)], wr=[('uT', c, i)])

    def load_w_slab(self, wb, key, wsrc, segs):
        P = self.P
        off = 0
        toks = []
        for si, (c0, n) in enumerate(segs):
            tok = ('wb', key, si)
            P.dma('pool', wb[:, :, off:off + n], wsrc[:, c0:c0 + n].rearrange("(cc p) f -> p cc f", p=128),
                  wr=[tok])
            toks.append(tok)
            off += n
        return toks

    def phase_A(self, l, s):
        P, sb, ps, nc = self.P, self.sb, self.ps, self.nc
        m = sb.mark()
        uT = sb.take(NCH * S, BF16).rearrange("p (c t) -> p c t", c=NCH)
        wbuf = [sb.take(NCH * 512, BF16).rearrange("p (c f) -> p c f", c=NCH) for _ in range(2)]
        wsrc = self.w_in[l]
        pre_slabs = fm_slabs()[:2]
        pre_toks = [self.load_w_slab(wbuf[k], k, wsrc, pre_slabs[k][0]) for k in range(2)]
        m2 = sb.mark()
        self.rmsnorm_to_uT(lambda j: ('hT', s, j), self.hT[s], self.g1, l * NCH, uT)
        stgF = [sb.take(S, BF16) for _ in range(2)]
        stgT = [sb.take(512, BF16) for _ in range(2)]
        stgG = [sb.take(18, F32) for _ in range(2)]
        wsrc = self.w_in[l]
        utoks = lambda c, t0, t1: [('uT', c, i) for i in range(t0 // 256, (t1 + 255) // 256)]
        nslab = 0
        nst = 0
        for segs, tiles in fm_slabs():
            k = nslab % 2
            nslab += 1
            wb = wbuf[k]
            if nslab <= 2:
                wtoks = pre_toks[k]
            else:
                wtoks = self.load_w_slab(wb, k, wsrc, segs)
            for ft, pti in enumerate(tiles):
                sk = nst % 2
                nst += 1
                for j in range(4):
                    b = self.bank()
                    for c in range(NCH):
                        P.op('pe', lambda e, b=b, c=c, wb=wb, ft=ft, j=j: e.matmul(
                            ps[:, b, :], wb[:, c, ft * 128:(ft + 1) * 128], uT[:, c, j * 512:(j + 1) * 512],
                            start=(c == 0), stop=(c == NCH - 1)),
                            rd=wtoks + utoks(c, j * 512, (j + 1) * 512), wr=[('ps', b)])
                    if j % 2 == 0:
                        P.op('act', lambda e, b=b, sk=sk, j=j: e.copy(out=stgF[sk][:, j * 512:(j + 1) * 512],
                                                                     in_=ps[:, b, :]),
                             rd=[('ps', b)], wr=[('stgF', sk, j)])
                    else:
                        P.op('dve', lambda e, b=b, sk=sk, j=j: e.tensor_copy(out=stgF[sk][:, j * 512:(j + 1) * 512],
                                                                            in_=ps[:, b, :]),
                             rd=[('ps', b)], wr=[('stgF', sk, j)])
                P.dma('sp', self.PT[pti], stgF[sk], rd=[('stgF', sk, j) for j in range(4)], wr=[('PT', pti)])
        ntt = 0
        for segs, voff, nv, gates in tok_slabs():
            k = nslab % 2
            nslab += 1
            wb = wbuf[k]
            wtoks = self.load_w_slab(wb, k, wsrc, segs)
            ncol = sum(n for _, n in segs)
            for tt in range(16):
                sk = ntt % 2
                ntt += 1
                b = self.bank()
                for c in range(NCH):
                    P.op('pe', lambda e, b=b, c=c, wb=wb, tt=tt, ncol=ncol: e.matmul(
                        ps[:, b, 0:ncol], uT[:, c, tt * 128:(tt + 1) * 128], wb[:, c, 0:ncol],
                        start=(c == 0), stop=(c == NCH - 1)),
                        rd=wtoks + utoks(c, tt * 128, (tt + 1) * 128), wr=[('ps', b)])
                if tt % 2 == 0:
                    P.op('act', lambda e, b=b, sk=sk, nv=nv: e.copy(out=stgT[sk][:, 0:nv], in_=ps[:, b, 0:nv]),
                         rd=[('ps', b)], wr=[('stgT', sk)])
                else:
                    P.op('dve', lambda e, b=b, sk=sk, nv=nv: e.tensor_copy(out=stgT[sk][:, 0:nv], in_=ps[:, b, 0:nv]),
                         rd=[('ps', b)], wr=[('stgT', sk)])
                P.dma('sp', self.VT[:, voff // 128:(voff + nv) // 128, tt, :],
                      stgT[sk][:, 0:nv].rearrange("p (h d) -> p h d", d=128), rd=[('stgT', sk)],
                      wr=[('VT', tt, voff)])
                if gates:
                    P.op('act', lambda e, b=b, sk=sk, nv=nv: e.activation(out=stgG[sk], in_=ps[:, b, nv:nv + 18],
                                                                         func=AF.Sigmoid),
                         rd=[('ps', b)], wr=[('stgG', sk)])
                    P.dma('sp', self.GT[tt], stgG[sk], rd=[('stgG', sk)], wr=[('GT', tt)])
        P.barrier()
        sb.release(m2)
        wff = sb.take(NCH * 6, BF16).rearrange("p (c f) -> p c f", c=NCH)
        P.dma('pool', wff, wsrc[:, C_FF:C_FF + 6].rearrange("(cc p) f -> p cc f", p=128), wr=[('wff',)])
        e1 = sb.take(S, F32)
        onesF = sb.take(S, F32)
        cs = sb.take(S, F32)
        P.op('pool', lambda e: e.memset(onesF[0:6, :], 1.0), wr=[('onesF',)])
        for j in range(4):
            b = self.bank()
            for c in range(NCH):
                P.op('pe', lambda e, b=b, c=c, j=j: e.matmul(ps[0:6, b, :], wff[:, c, :], uT[:, c, j * 512:(j + 1) * 512],
                                                            start=(c == 0), stop=(c == NCH - 1)),
                     rd=[('wff',)] + utoks(c, j * 512, (j + 1) * 512), wr=[('ps', b)])
            P.op('act', lambda e, b=b, j=j: e.activation(out=e1[0:6, j * 512:(j + 1) * 512], in_=ps[0:6, b, :],
                                                        func=AF.Exp, scale=-1.0, bias=self.nfb[0:6, l:l + 1]),
                 rd=[('ps', b), ('nfb',)], wr=[('e1', j)])
        e1t = [('e1', j) for j in range(4)]
        P.op('dve', lambda e: e.tensor_scalar(out=e1[0:6, :], in0=e1[0:6, :], scalar1=1.0, scalar2=None, op0=ALU.add),
             rd=e1t, wr=e1t)
        P.op('act', lambda e: e.activation(out=e1[0:6, :], in_=e1[0:6, :], func=AF.Ln), rd=e1t, wr=e1t)
        P.op('dve', lambda e: e.tensor_tensor_scan(out=cs[0:6, :], data0=onesF[0:6, :], data1=e1[0:6, :], initial=0.0,
                                                  op0=ALU.mult, op1=ALU.add), rd=e1t + [('onesF',)], wr=[('cs',)])
        sq128 = math.sqrt(128.0)
        P.op('dve', lambda e: e.tensor_scalar(out=cs[0:6, :], in0=cs[0:6, :], scalar1=-sq128, scalar2=None,
                                              op0=ALU.mult), rd=[('cs',)], wr=[('cs',)])
        pcs = [sb.take(S, BF16) for _ in range(3)]
        ncs = [sb.take(S, BF16) for _ in range(3)]
        for r in range(3):
            P.op('dve', lambda e, r=r: e.tensor_copy(out=pcs[r][0:6, :], in_=cs[0:6, :]), rd=[('cs',)], wr=[('pcs', r)])
            if r < 2:
                P.op('dve', lambda e, r=r: e.tensor_tensor(out=cs[0:6, :], in0=cs[0:6, :], in1=pcs[r][0:6, :],
                                                          op=ALU.subtract), rd=[('cs',), ('pcs', r)], wr=[('cs',)])
            P.op('dve', lambda e, r=r: e.tensor_scalar(out=ncs[r][0:6, :], in0=pcs[r][0:6, :], scalar1=-1.0,
                                                      scalar2=None, op0=ALU.mult), rd=[('pcs', r)], wr=[('ncs', r)])
            P.dma('sp', self.FAR[:, r, :], pcs[r][0:6, :], rd=[('pcs', r)], wr=[('FAR', r)])
            P.dma('sp', self.FAL[:, 3 + r, :], ncs[r][0:6, :], rd=[('ncs', r)], wr=[('FAL', r)])
        P.barrier()
        sb.release(m)


    class Stream:
        pass

    def emit_S(self, st, qc, buf, bufid):
        P, ps = self.P, self.ps
        if st.mode == 'causal':
            kts = range(0, 4 * qc + 4)
        else:
            kts = range(max(0, 4 * qc - 4), 4 * qc + 4)
        ktmin = kts[0] if st.mode == 'window' else 0
        for kt in kts:
            qi0 = max(kt, 4 * qc)
            qi1 = 4 * qc + 3 if st.mode == 'causal' else min(kt + 4, 4 * qc + 3)
            q0, q1 = qi0 * 128, (qi1 + 1) * 128
            n = q1 - q0
            b = self.bank()
            mms = [(st.lhs(kt), st.QT[:, q0:q1])] + st.extra(kt, q0, q1)
            for idx, (lh, rh) in enumerate(mms):
                P.op('pe', lambda e, b=b, n=n, lh=lh, rh=rh, idx=idx, last=len(mms) - 1: e.matmul(
                    ps[:, b, 0:n], lh, rh, start=(idx == 0), stop=(idx == last)),
                    rd=st.rtoks, wr=[('ps', b)])
            for qi in range(qi0, qi1 + 1):
                tab = st.table(qi - kt)
                if tab is not None:
                    a = (qi - qi0) * 128
                    P.op('dve', lambda e, b=b, a=a, tab=tab: e.tensor_tensor(
                        out=ps[:, b, a:a + 128], in0=ps[:, b, a:a + 128], in1=tab, op=ALU.add),
                        rd=[('ps', b)], wr=[('ps', b)])
            off = q0 - qc * 512
            slot = kt - ktmin
            if st.cb is not None:
                fn = lambda e, b=b, n=n, off=off, slot=slot: e.activation(
                    out=buf[:, slot, off:off + n], in_=ps[:, b, 0:n], func=AF.Exp, scale=st.scale, bias=st.cb)
            else:
                fn = lambda e, b=b, n=n, off=off, slot=slot: e.activation(
                    out=buf[:, slot, off:off + n], in_=ps[:, b, 0:n], func=AF.Exp, scale=st.scale)
            P.op('act', fn, rd=[('ps', b)], wr=[('pt', bufid, slot)])

    def emit_PV(self, st, qc, buf, bufid, ji):
        P, ps = self.P, self.ps
        ktmin = max(0, 4 * qc - 4) if st.mode == 'window' else 0
        for qi in range(4 * qc, 4 * qc + 4):
            kts = range(0, qi + 1) if st.mode == 'causal' else range(max(0, qi - 4), qi + 1)
            b = self.bank()
            a = (qi - 4 * qc) * 128
            for idx, kt in enumerate(kts):
                slot = kt - ktmin
                P.op('pe', lambda e, b=b, a=a, slot=slot, kt=kt, idx=idx, last=len(kts) - 1: e.matmul(
                    ps[:, b, 0:st.ncols], buf[:, slot, a:a + 128], st.V(kt), start=(idx == 0), stop=(idx == last)),
                    rd=[('pt', bufid, slot)] + st.vtoks, wr=[('ps', b)])
            st.epi(qi, b, ji)

    def run_jobs(self, jobs, ring, per_head=None, after_head=None):
        n = len(jobs)

        def hook(pi):
            if per_head is not None and (pi + 1) % per_head == 0:
                after_head(pi // per_head)
        self.emit_S(jobs[0][0], jobs[0][1], ring[0], 0)
        for i in range(n):
            st, qc = jobs[i]
            if i + 1 < n:
                self.emit_S(jobs[i + 1][0], jobs[i + 1][1], ring[(i + 1) % 2], (i + 1) % 2)
            if i >= 1:
                pst, pqc = jobs[i - 1]
                pst.post_a(pqc, i - 1)
            self.emit_PV(st, qc, ring[i % 2], i % 2, i)
            if i >= 1:
                pst.post_b(pqc, i - 1)
                hook(i - 1)
        pst, pqc = jobs[n - 1]
        pst.post_a(pqc, n - 1)
        pst.post_b(pqc, n - 1)
        hook(n - 1)

    def post_transpose(self, obf, par, mixstg, k, qc, head, evac_eng):
        P, ps = self.P, self.ps
        bt = self.bank()
        psb = ps[:, bt, 0:256].bitcast(BF16)
        for j in range(4):
            P.op('pe', lambda e, j=j: e.transpose(out=psb[:, j * 128:(j + 1) * 128], in_=obf[par][:, j, :],
                                                  identity=self.identB),
                 rd=[('obf', par, j)], wr=[('ps', bt)])
        if evac_eng == 'act':
            P.op('act', lambda e: e.copy(out=mixstg[k][:, qc * 512:(qc + 1) * 512], in_=psb),
                 rd=[('ps', bt)], wr=[('mix', k, qc)])
        else:
            P.op('dve', lambda e: e.tensor_copy(out=mixstg[k][:, qc * 512:(qc + 1) * 512], in_=psb),
                 rd=[('ps', bt)], wr=[('mix', k, qc)])
        if qc == 3:
            P.dma('sp', self.MT[head], mixstg[k], rd=[('mix', k, q) for q in range(4)], wr=[('MT', head)])

    def load_V(self, Vt, k, col):
        self.P.dma('sp', Vt[:, :, 0:128], self.VT[:, col // 128, :, :], wr=[('V', k)])

    def phase_B_fox(self, l, s):
        P, sb, ps = self.P, self.sb, self.ps
        m = sb.mark()
        ring = [sb.take(16 * 512, BF16).rearrange("p (k q) -> p k q", k=16) for _ in range(2)]
        KT = [sb.take(S, BF16) for _ in range(2)]
        QT = [sb.take(S, BF16) for _ in range(2)]
        AL = [sb.take(S, BF16) for _ in range(2)]
        AR = [sb.take(S, BF16) for _ in range(2)]
        V = [sb.take(16 * 130, BF16).rearrange("p (k d) -> p k d", k=16) for _ in range(2)]
        mixstg = [sb.take(S, BF16) for _ in range(2)]
        obf = [sb.take(4 * 128, BF16).rearrange("p (j d) -> p j d", j=4) for _ in range(2)]
        raw = [sb.take(4 * 130, F32).rearrange("p (j d) -> p j d", j=4) for _ in range(2)]
        rr = [sb.take(4, F32) for _ in range(2)]
        for k in range(2):
            P.op('pool', lambda e, k=k: e.memset(AL[k], 0.0), wr=[('AL', k)])
            P.op('pool', lambda e, k=k: e.memset(AR[k], 0.0), wr=[('AR', k)])
            P.op('pool', lambda e, k=k: e.memset(V[k][:, :, 128:129], 1.0), wr=[('V', k)])
            P.op('pool', lambda e, k=k: e.memset(V[k][:, :, 129:130], 0.0), wr=[('V', k)])
        jobs = []
        sc = 128.0 ** -0.5
        for h in range(6):
            k = h % 2
            st = self.Stream()
            st.mode = 'causal'
            st.head = h
            st.k = k
            st.lhs = lambda kt, k=k: KT[k][:, kt * 128:(kt + 1) * 128]
            st.QT = QT[k]
            st.extra = lambda kt, q0, q1, k=k: [(AL[k][:, kt * 128:(kt + 1) * 128], AR[k][:, q0:q1])]
            st.table = lambda mm: self.NM0F if mm == 0 else None
            st.cb = None
            st.scale = sc
            st.V = lambda kt, k=k: V[k][:, kt, 0:129]
            st.ncols = 129
            st.rtoks = [('KT', k), ('QT', k), ('AL', k), ('AR', k)]
            st.vtoks = [('V', k)]
            st.loaded = False

            def epi(qi, b, ji):
                j = qi % 4
                par = ji % 2
                P.op('dve', lambda e: e.tensor_copy(out=raw[par][:, j, 0:129], in_=ps[:, b, 0:129]),
                     rd=[('ps', b)], wr=[('raw', par, j)])
            st.epi = epi

            def post(qc, ji, h=h, k=k):
                par = ji % 2
                rt = [('raw', par, j) for j in range(4)]
                P.op('dve', lambda e: e.reciprocal(out=rr[par].unsqueeze(2), in_=raw[par][:, :, 128:129]),
                     rd=rt, wr=[('rr', par)])
                P.op('dve', lambda e: e.tensor_tensor(out=obf[par], in0=raw[par][:, :, 0:128],
                                                      in1=rr[par].unsqueeze(2).to_broadcast([128, 4, 128]), op=ALU.mult),
                     rd=rt + [('rr', par)], wr=[('obf', par, j) for j in range(4)])
            st.post_a = post
            st.post_b = lambda qc, ji, h=h, k=k: self.post_transpose(obf, ji % 2, mixstg, k, qc, h, 'act')
            for qc in range(4):
                jobs.append((st, qc))
        def load(h):
            k = h % 2
            P.dma('sp', KT[k], self.PT[PT_FK + h], wr=[('KT', k)])
            P.dma('sp', QT[k], self.PT[PT_FQ + h], wr=[('QT', k)])
            self.load_V(V[k], k, VT_FV + 128 * h)
            P.dma('sp', AL[k][0:6, :], self.FAL[h], wr=[('AL', k)])
            P.dma('sp', AR[k][0:6, :], self.FAR[h], wr=[('AR', k)])
        load(0)
        load(1)
        self.run_jobs(jobs, ring, 4, lambda hi: load(hi + 2) if hi + 2 < 6 else None)
        P.barrier()
        sb.release(m)

    def phase_B_diff(self, l, s):
        P, sb, ps = self.P, self.sb, self.ps
        m = sb.mark()
        ring = [sb.take(16 * 512, BF16).rearrange("p (k q) -> p k q", k=16) for _ in range(2)]
        KT0 = [sb.take(S, BF16) for _ in range(2)]
        KT1 = [sb.take(S, BF16) for _ in range(2)]
        QT = [sb.take(S, BF16) for _ in range(2)]
        V = [sb.take(16 * 130, BF16).rearrange("p (k d) -> p k d", k=16) for _ in range(2)]
        mixstg = [sb.take(S, BF16) for _ in range(2)]
        obf = [sb.take(4 * 128, BF16).rearrange("p (j d) -> p j d", j=4) for _ in range(2)]
        t0 = [sb.take(4 * 128, F32).rearrange("p (j d) -> p j d", j=4) for _ in range(2)]
        od = sb.take(4 * 128, F32).rearrange("p (j d) -> p j d", j=4)
        sqd = sb.take(4 * 128, F32).rearrange("p (j d) -> p j d", j=4)
        raw = [sb.take(4 * 130, F32).rearrange("p (j d) -> p j d", j=4) for _ in range(2)]
        rr = [sb.take(4, F32) for _ in range(2)]
        ssq = sb.take(4, F32)
        mhalf = sb.take(4, F32)
        P.op('pool', lambda e: e.memset(mhalf, -0.5), wr=[('mhalf',)])
        gsub = sb.take(128, F32)
        lamb = sb.take(256, F32)
        pr = sb.take(128, F32)
        s12 = sb.take(2, F32)
        nlam = sb.take(1, F32)
        lam_init = 0.8 - 0.6 * math.exp(-0.3 * l)
        P.dma('sp', lamb, self.lam4[l:l + 1].rearrange("o a d -> o (a d)").partition_broadcast(128), wr=[('lamb',)])
        P.dma('sp', gsub, self.diff_subln_g[l:l + 1, :].partition_broadcast(128), wr=[('gsub',)])
        P.op('dve', lambda e: e.tensor_scalar(out=gsub, in0=gsub, scalar1=1.0 - lam_init, scalar2=None, op0=ALU.mult),
             rd=[('gsub',)], wr=[('gsub',)])
        P.op('dve', lambda e: e.tensor_tensor(out=pr[:, 0:64], in0=lamb[:, 0:64], in1=lamb[:, 64:128], op=ALU.mult),
             rd=[('lamb',)], wr=[('pr', 0)])
        P.op('dve', lambda e: e.tensor_tensor(out=pr[:, 64:128], in0=lamb[:, 128:192], in1=lamb[:, 192:256], op=ALU.mult),
             rd=[('lamb',)], wr=[('pr', 1)])
        P.op('dve', lambda e: e.tensor_reduce(out=s12, in_=pr.rearrange("p (a d) -> p a d", a=2), axis=AX.X, op=ALU.add),
             rd=[('pr', 0), ('pr', 1)], wr=[('s12',)])
        P.op('act', lambda e: e.activation(out=s12, in_=s12, func=AF.Exp), rd=[('s12',)], wr=[('s12',)])
        P.op('dve', lambda e: e.tensor_tensor(out=nlam, in0=s12[:, 1:2], in1=s12[:, 0:1], op=ALU.subtract),
             rd=[('s12',)], wr=[('nlam',)])
        P.op('dve', lambda e: e.tensor_scalar(out=nlam, in0=nlam, scalar1=-lam_init, scalar2=None, op0=ALU.add),
             rd=[('nlam',)], wr=[('nlam',)])
        for k in range(2):
            P.op('pool', lambda e, k=k: e.memset(KT0[k][64:128, :], 0.0), wr=[('KT0z', k)])
            P.op('pool', lambda e, k=k: e.memset(KT1[k][0:64, :], 0.0), wr=[('KT1z', k)])
            P.op('pool', lambda e, k=k: e.memset(V[k][:, :, 128:129], 1.0), wr=[('V', k)])
            P.op('pool', lambda e, k=k: e.memset(V[k][:, :, 129:130], 0.0), wr=[('V', k)])
        jobs = []
        for h in range(4):
            k = h % 2
            for mp in range(2):
                st = self.Stream()
                st.mode = 'causal'
                KTm = KT0 if mp == 0 else KT1
                st.lhs = lambda kt, k=k, KTm=KTm: KTm[k][:, kt * 128:(kt + 1) * 128]
                st.QT = QT[k]
                st.extra = lambda kt, q0, q1: []
                st.table = lambda mm, h=h: self.BT[:, h, 0, :] if mm == 0 else (self.BT[:, h, 1, :] if mm == 1 else None)
                st.cb = self.CB[:, h:h + 1]
                st.scale = 0.125
                st.V = lambda kt, k=k: V[k][:, kt, 0:129]
                st.ncols = 129
                st.rtoks = [('KT0', k), ('KT1', k), ('KT0z', k), ('KT1z', k), ('QT', k)]
                st.vtoks = [('V', k)]
                def epi(qi, b, ji, mp=mp):
                    j = qi % 4
                    P.op('dve', lambda e: e.tensor_copy(out=raw[mp][:, j, 0:129], in_=ps[:, b, 0:129]),
                         rd=[('ps', b)], wr=[('raw', mp, j)])
                st.epi = epi
                if mp == 0:
                    def post(qc, ji):
                        par = (ji // 2) % 2
                        rt = [('raw', 0, j) for j in range(4)]
                        P.op('dve', lambda e: e.reciprocal(out=rr[0].unsqueeze(2), in_=raw[0][:, :, 128:129]),
                             rd=rt, wr=[('rr', 0)])
                        P.op('dve', lambda e: e.tensor_tensor(out=t0[par], in0=raw[0][:, :, 0:128],
                                                              in1=rr[0].unsqueeze(2).to_broadcast([128, 4, 128]), op=ALU.mult),
                             rd=rt + [('rr', 0)], wr=[('t0', par)])
                    st.post_a = post
                    st.post_b = lambda qc, ji: None
                else:
                    def post(qc, ji, h=h, k=k):
                        par = (ji // 2) % 2
                        rt = [('raw', 1, j) for j in range(4)]
                        P.op('dve', lambda e: e.reciprocal(out=rr[1].unsqueeze(2), in_=raw[1][:, :, 128:129]),
                             rd=rt, wr=[('rr', 1)])
                        P.op('dve', lambda e: e.tensor_scalar(out=rr[1], in0=rr[1], scalar1=nlam[:, 0:1], scalar2=None,
                                                              op0=ALU.mult), rd=[('rr', 1), ('nlam',)], wr=[('rr', 1)])
                        P.op('dve', lambda e: e.tensor_tensor(out=od, in0=raw[1][:, :, 0:128],
                                                              in1=rr[1].unsqueeze(2).to_broadcast([128, 4, 128]), op=ALU.mult),
                             rd=rt + [('rr', 1)], wr=[('od',)])
                        P.op('pool', lambda e: e.tensor_tensor(out=od, in0=od, in1=t0[par], op=ALU.add),
                             rd=[('od',), ('t0', par)], wr=[('od',)])
                        P.op('pool', lambda e: e.tensor_tensor(out=sqd, in0=od, in1=od, op=ALU.mult),
                             rd=[('od',)], wr=[('sqd',)])
                        P.op('dve', lambda e: e.tensor_reduce(out=ssq, in_=sqd, axis=AX.X, op=ALU.add),
                             rd=[('sqd',)], wr=[('ssq',)])
                        P.op('dve', lambda e: e.tensor_scalar(out=ssq, in0=ssq, scalar1=EPS * 128.0, scalar2=1.0 / 128.0,
                                                              op0=ALU.add, op1=ALU.mult), rd=[('ssq',)], wr=[('ssq',)])
                        P.op('pool', lambda e: e.tensor_tensor(out=ssq, in0=ssq, in1=mhalf, op=ALU.pow),
                             rd=[('ssq',), ('mhalf',)], wr=[('ssq',)])
                        P.op('dve', lambda e: e.tensor_tensor(out=od, in0=od, in1=ssq.unsqueeze(2).to_broadcast([128, 4, 128]),
                                                              op=ALU.mult), rd=[('od',), ('ssq',)], wr=[('od',)])
                        P.op('pool', lambda e: e.tensor_tensor(out=obf[par], in0=od,
                                                               in1=gsub.unsqueeze(1).to_broadcast([128, 4, 128]), op=ALU.mult),
                             rd=[('od',), ('gsub',)], wr=[('obf', par, j) for j in range(4)])
                    st.post_a = post
                    st.post_b = lambda qc, ji, h=h, k=k: self.post_transpose(obf, (ji // 2) % 2, mixstg, k, qc, 6 + h, 'act')
                st.mp = mp
                jobs.append(st)
        joblist = []
        for h in range(4):
            for qc in range(4):
                joblist.append((jobs[2 * h], qc))
                joblist.append((jobs[2 * h + 1], qc))

        def load(h):
            k = h % 2
            P.dma('sp', KT0[k][0:64, :], self.PT[PT_DK + h, 0:64, :], wr=[('KT0', k)])
            P.dma('sp', KT1[k][64:128, :], self.PT[PT_DK + h, 64:128, :], wr=[('KT1', k)])
            P.dma('sp', QT[k], self.PT[PT_DQ + h], wr=[('QT', k)])
            self.load_V(V[k], k, VT_DV + 128 * h)
        load(0)
        load(1)
        self.run_jobs(joblist, ring, 8, lambda hi: load(hi + 2) if hi + 2 < 4 else None)
        P.barrier()
        sb.release(m)

    def phase_B_nsa(self, l, s):
        P, sb, ps = self.P, self.sb, self.ps
        m = sb.mark()
        sc = 128.0 ** -0.5
        NB = 4096.0
        kcmpT = [sb.take(128, BF16) for _ in range(2)]
        Rt = [sb.take(162, BF16) for _ in range(2)]
        gates = sb.take(16 * 18, F32).rearrange("p (a c) -> p a c", a=16)
        P.dma('sp', gates, self.GT.rearrange("a p c -> p a c"), wr=[('gates',)])
        for g in range(2):
            P.op('pool', lambda e, g=g: e.memset(Rt[g][:, 128:129], 1.0), wr=[('Rt1', g)])
            P.op('dve', lambda e, g=g: e.tensor_copy(out=Rt[g][0:NCMP, 129:161], in_=self.OVb[0:NCMP, :]), wr=[('Rt2', g)])
        m1 = sb.mark()
        w1 = [sb.take(32 * 128, BF16).rearrange("p (l j) -> p l j", l=32) for _ in range(2)]
        w2 = [sb.take(128, BF16) for _ in range(2)]
        posT = sb.take(32, BF16)
        xcT = [sb.take(S, BF16) for _ in range(2)]
        pb = sb.take(2, F32)
        xs = sb.take(128, F32)
        x2 = sb.take(128, F32)
        yy = sb.take(128, F32)
        gT = sb.take(128, BF16)
        for kv in range(2):
            P.dma('pool', w1[kv], self.nsa_cmp_w1[l, kv].rearrange("(l d) j -> d l j", d=128), wr=[('w1', kv)])
            P.dma('pool', w2[kv], self.nsa_cmp_w2[l, kv], wr=[('w2', kv)])
        posF = sb.take(128, F32)
        P.dma('sp', posF[0:32, :], self.nsa_cmp_pos[l], wr=[('posF',)])
        bp = self.bank()
        P.op('pe', lambda e: e.transpose(out=ps[:, bp, 0:32], in_=posF[0:32, :], identity=self.identF[0:32, 0:32]),
             rd=[('posF',)], wr=[('ps', bp)])
        P.op('dve', lambda e: e.tensor_copy(out=posT, in_=ps[:, bp, 0:32]), rd=[('ps', bp)], wr=[('posT',)])
        for kv in range(2):
            b2 = self.bank()
            for li in range(32):
                P.op('pe', lambda e, b2=b2, kv=kv, li=li: e.matmul(ps[:, b2, 0:1], w1[kv][:, li, :], posT[:, li:li + 1],
                                                                  start=(li == 0), stop=(li == 31)),
                     rd=[('w1', kv), ('posT',)], wr=[('ps', b2)])
            P.op('dve', lambda e, b2=b2, kv=kv: e.tensor_copy(out=pb[:, kv:kv + 1], in_=ps[:, b2, 0:1]),
                 rd=[('ps', b2)], wr=[('pb', kv)])
        n = 0
        for g in range(2):
            for kv in range(2):
                xc = xcT[n % 2]
                xk = n % 2
                n += 1
                P.dma('sp', xc, self.PT[(PT_NKC if kv == 0 else PT_NVC) + g], wr=[('xc', xk)])
                b = self.bank()
                for li in range(32):
                    P.op('pe', lambda e, b=b, kv=kv, li=li, xc=xc: e.matmul(
                        ps[:, b, 0:NCMP], w1[kv][:, li, :], xc[:, li:li + 16 * (NCMP - 1) + 1:16],
                        start=(li == 0), stop=(li == 31)), rd=[('w1', kv), ('xc', xk)], wr=[('ps', b)])
                P.op('act', lambda e, b=b, kv=kv: e.activation(out=xs[:, 0:NCMP], in_=ps[:, b, 0:NCMP], func=AF.Identity,
                                                              bias=pb[:, kv:kv + 1]),
                     rd=[('ps', b), ('pb', kv)], wr=[('xs',)])
                P.op('dve', lambda e: e.tensor_tensor(out=x2[:, 0:NCMP], in0=xs[:, 0:NCMP], in1=xs[:, 0:NCMP], op=ALU.mult),
                     rd=[('xs',)], wr=[('x2',)])
                P.op('dve', lambda e: e.tensor_scalar(out=x2[:, 0:NCMP], in0=x2[:, 0:NCMP], scalar1=0.044715, scalar2=1.0,
                                                      op0=ALU.mult, op1=ALU.add), rd=[('x2',)], wr=[('x2',)])
                P.op('dve', lambda e: e.tensor_tensor(out=yy[:, 0:NCMP], in0=x2[:, 0:NCMP], in1=xs[:, 0:NCMP], op=ALU.mult),
                     rd=[('x2',), ('xs',)], wr=[('yy',)])
                P.op('act', lambda e: e.activation(out=yy[:, 0:NCMP], in_=yy[:, 0:NCMP], func=AF.Sigmoid,
                                                   scale=1.5957691216057308), rd=[('yy',)], wr=[('yy',)])
                P.op('dve', lambda e: e.tensor_tensor(out=gT[:, 0:NCMP], in0=xs[:, 0:NCMP], in1=yy[:, 0:NCMP], op=ALU.mult),
                     rd=[('xs',), ('yy',)], wr=[('gT',)])
                b3 = self.bank()
                if kv == 0:
                    P.op('pe', lambda e, b3=b3: e.matmul(ps[:, b3, 0:NCMP], w2[0], gT[:, 0:NCMP], start=True, stop=True),
                         rd=[('w2', 0), ('gT',)], wr=[('ps', b3)])
                    P.op('dve', lambda e, b3=b3, g=g: e.tensor_copy(out=kcmpT[g][:, 0:NCMP], in_=ps[:, b3, 0:NCMP]),
                         rd=[('ps', b3)], wr=[('kcmpT', g)])
                else:
                    P.op('pe', lambda e, b3=b3: e.matmul(ps[0:NCMP, b3, 0:128], gT[:, 0:NCMP], w2[1], start=True, stop=True),
                         rd=[('w2', 1), ('gT',)], wr=[('ps', b3)])
                    P.op('dve', lambda e, b3=b3, g=g: e.tensor_copy(out=Rt[g][0:NCMP, 0:128], in_=ps[0:NCMP, b3, 0:128]),
                         rd=[('ps', b3)], wr=[('Rt0', g)])
        P.barrier()
        sb.release(m1)
        ring = [sb.take(16 * 512, BF16).rearrange("p (k q) -> p k q", k=16) for _ in range(2)]
        QT3 = [sb.take(S, BF16) for _ in range(3)]
        KS = sb.take(S, BF16)
        KW = sb.take(S, BF16)
        VS = sb.take(16 * 130, BF16).rearrange("p (k d) -> p k d", k=16)
        VW = sb.take(16 * 130, BF16).rearrange("p (k d) -> p k d", k=16)
        ETs = [sb.take(S, BF16) for _ in range(2)]
        Ycs = [sb.take(S, F32) for _ in range(2)]
        negselT = sb.take(S, BF16)
        imp = sb.take(16 * 32, F32).rearrange("p (a m) -> p a m", a=16)
        ocmp = [sb.take(16 * 128, BF16).rearrange("p (a d) -> p a d", a=16) for _ in range(3)]
        mixstg = [sb.take(S, BF16) for _ in range(2)]
        acc = [sb.take(4 * 128, F32).rearrange("p (j d) -> p j d", j=4) for _ in range(2)]
        obf = [sb.take(4 * 128, BF16).rearrange("p (j d) -> p j d", j=4) for _ in range(2)]
        rc16s = [sb.take(16, F32) for _ in range(3)]
        rg16s = [sb.take(16, F32) for _ in range(3)]
        craws = [sb.take(16 * 162, F32).rearrange("p (a d) -> p a d", a=16) for _ in range(2)]
        imp2 = sb.take(16 * 32, F32).rearrange("p (a m) -> p a m", a=16)
        scb = sb.take(16 * 32, F32).rearrange("p (a m) -> p a m", a=16)
        selm = sb.take(16 * 32, F32).rearrange("p (a m) -> p a m", a=16)
        mx = sb.take(16 * 8, F32).rearrange("p (a m) -> p a m", a=16)
        nsb = sb.take(16 * 32, BF16).rearrange("p (a m) -> p a m", a=16)
        raw = [sb.take(4 * 130, F32).rearrange("p (j d) -> p j d", j=4) for _ in range(2)]
        rr = [sb.take(4, F32) for _ in range(2)]
        tmpw = sb.take(4 * 128, F32).rearrange("p (j d) -> p j d", j=4)
        P.op('pool', lambda e: e.memset(negselT, 0.0), wr=[('nsT', q) for q in range(4)])
        for Vt, nm in ((VS, 'VS'), (VW, 'VW')):
            P.op('pool', lambda e, Vt=Vt: e.memset(Vt[:, :, 128:129], 1.0), wr=[(nm,)])
            P.op('pool', lambda e, Vt=Vt: e.memset(Vt[:, :, 129:130], 0.0), wr=[(nm,)])
        for g in range(2):
            P.dma('sp', KS, self.PT[PT_NKS + g], wr=[('KS',)])
            P.dma('sp', KW, self.PT[PT_NKW + g], wr=[('KW',)])
            P.dma('sp', VS[:, :, 0:128], self.VT[:, VT_NVS // 128 + g, :, :],
                  wr=[('VS',)])
            P.dma('sp', VW[:, :, 0:128], self.VT[:, VT_NVW // 128 + g, :, :],
                  wr=[('VW',)])
            cnt = 0
            for hh in range(3):
                h = 3 * g + hh
                yk = hh % 2
                Ycr = Ycs[yk][:, ::-1]
                if hh == 0:
                    for h2 in range(3):
                        P.dma('sp', QT3[h2], self.PT[PT_NQ + 3 * g + h2], wr=[('QT3', h2)])
                    for h2 in range(2):
                        P.dma('sp', Ycs[h2], bass.AP(tensor=self.TCr_h, offset=(4 + 3 * g + h2) * 4352 + 287,
                                                     ap=[[16, 128], [1, S]]), wr=[('Yc', h2)])
                if hh == 2:
                    P.dma('sp', Ycs[0], bass.AP(tensor=self.TCr_h, offset=(4 + h) * 4352 + 287,
                                                ap=[[16, 128], [1, S]]), wr=[('Yc', 0)])
                ek = hh % 2
                ETk = ETs[ek]
                crawk = craws[ek]
                rck = rc16s[hh]
                rgk = rg16s[hh]
                for qc in range(4):
                    b = self.bank()
                    q0, q1 = qc * 512, (qc + 1) * 512
                    P.op('pe', lambda e, b=b, hh=hh, q0=q0, q1=q1, g=g: e.matmul(
                        ps[0:NCMP, b, :], kcmpT[g][:, 0:NCMP], QT3[hh][:, q0:q1], start=True, stop=True),
                        rd=[('kcmpT', g), ('QT3', hh)], wr=[('ps', b)])
                    P.op('dve', lambda e, b=b, q0=q0, q1=q1, Ycr=Ycr: e.tensor_tensor(
                        out=ps[0:NCMP, b, :], in0=ps[0:NCMP, b, :], in1=Ycr[0:NCMP, q0:q1], op=ALU.add),
                        rd=[('ps', b), ('Yc', yk)], wr=[('ps', b)])
                    P.op('act', lambda e, b=b, q0=q0, q1=q1, h=h, ETk=ETk: e.activation(
                        out=ETk[0:NCMP, q0:q1], in_=ps[0:NCMP, b, :], func=AF.Exp, scale=sc,
                        bias=self.CB[0:NCMP, 4 + h:5 + h]), rd=[('ps', b)], wr=[('ET', ek, qc)])
                for qi in range(16):
                    b = self.bank()
                    P.op('pe', lambda e, b=b, qi=qi, g=g, ETk=ETk: e.matmul(
                        ps[:, b, 0:161], ETk[0:NCMP, qi * 128:(qi + 1) * 128], Rt[g][0:NCMP, 0:161], start=True, stop=True),
                        rd=[('ET', ek, qi // 4), ('Rt0', g), ('Rt1', g), ('Rt2', g)], wr=[('ps', b)])
                    if qi % 2 == 0:
                        P.op('act', lambda e, b=b, qi=qi, crawk=crawk: e.copy(out=crawk[:, qi, 0:161], in_=ps[:, b, 0:161]),
                             rd=[('ps', b)], wr=[('craw', ek, qi)])
                    else:
                        P.op('dve', lambda e, b=b, qi=qi, crawk=crawk: e.tensor_copy(out=crawk[:, qi, 0:161], in_=ps[:, b, 0:161]),
                             rd=[('ps', b)], wr=[('craw', ek, qi)])
                ct = [('craw', ek, qi) for qi in range(16)]
                P.op('dve', lambda e, crawk=crawk, rck=rck: e.tensor_scalar(
                    out=rck.unsqueeze(2), in0=crawk[:, :, 128:129], scalar1=1e-30, scalar2=None, op0=ALU.max),
                    rd=ct, wr=[('rc16', hh)])
                P.op('dve', lambda e, rck=rck: e.reciprocal(out=rck, in_=rck), rd=[('rc16', hh)], wr=[('rc16', hh)])
                P.op('dve', lambda e, h=h, rck=rck, rgk=rgk: e.tensor_tensor(
                    out=rgk.unsqueeze(2), in0=rck.unsqueeze(2), in1=gates[:, :, 3 * h:3 * h + 1], op=ALU.mult),
                    rd=[('rc16', hh), ('gates',)], wr=[('rg16', hh)])
                P.op('pool', lambda e, hh=hh, crawk=crawk, rgk=rgk: e.tensor_tensor(
                    out=ocmp[hh], in0=crawk[:, :, 0:128], in1=rgk.unsqueeze(2).to_broadcast([128, 16, 128]), op=ALU.mult),
                    rd=ct + [('rg16', hh)], wr=[('ocmp', hh, qi) for qi in range(16)])
                if hh == 0:
                    P.op('dve', lambda e, crawk=crawk, rck=rck: e.tensor_tensor(
                        out=imp, in0=crawk[:, :, 129:161], in1=rck.unsqueeze(2).to_broadcast([128, 16, 32]), op=ALU.mult),
                        rd=ct + [('rc16', hh)], wr=[('imp',)])
                else:
                    P.op('dve', lambda e, crawk=crawk, rck=rck: e.tensor_tensor(
                        out=imp2, in0=crawk[:, :, 129:161], in1=rck.unsqueeze(2).to_broadcast([128, 16, 32]), op=ALU.mult),
                        rd=ct + [('rc16', hh)], wr=[('imp2',)])
                    P.op('dve', lambda e: e.tensor_tensor(out=imp, in0=imp, in1=imp2, op=ALU.add),
                         rd=[('imp',), ('imp2',)], wr=[('imp',)])
            P.op('dve', lambda e: e.tensor_tensor(out=scb, in0=imp, in1=self.FM, op=ALU.add), rd=[('imp',)], wr=[('scb',)])
            for qi in range(16):
                P.op('dve', lambda e, qi=qi: e.max(out=mx[:, qi, :], in_=scb[:, qi, :]), rd=[('scb',)], wr=[('mx', qi)])
            P.op('dve', lambda e: e.tensor_tensor(out=selm, in0=scb, in1=mx[:, :, 7:8].to_broadcast([128, 16, 32]),
                                                  op=ALU.is_ge), rd=[('scb',)] + [('mx', qi) for qi in range(16)], wr=[('selm',)])
            P.op('dve', lambda e: e.tensor_scalar(out=nsb, in0=selm, scalar1=-1.0, scalar2=NB, op0=ALU.add, op1=ALU.mult),
                 rd=[('selm',)], wr=[('nsb',)])
            for qc in range(4):
                bt = self.bank()
                psb = ps[0:32, bt, 0:256].bitcast(BF16)
                for j in range(4):
                    qi = 4 * qc + j
                    P.op('pe', lambda e, j=j, qi=qi, psb=psb: e.transpose(out=psb[:, j * 128:(j + 1) * 128], in_=nsb[:, qi, :],
                                                                          identity=self.identB),
                         rd=[('nsb',)], wr=[('ps', bt)])
                P.op('act', lambda e, psb=psb, qc=qc: e.copy(out=negselT[0:32, qc * 512:(qc + 1) * 512], in_=psb),
                     rd=[('ps', bt)], wr=[('nsT', qc)])
            joblist = []
            for hh in range(3):
                h = 3 * g + hh
                tab = lambda mm, h=h: (self.BT[:, 4 + h, 0, :] if mm == 0 else (self.BT[:, 4 + h, 1, :] if mm == 1 else None))
                ss = self.Stream()
                ss.mode = 'causal'
                ss.lhs = lambda kt: KS[:, kt * 128:(kt + 1) * 128]
                ss.QT = QT3[hh]
                ss.extra = lambda kt, q0, q1: [(self.Epad[:, kt * 128:(kt + 1) * 128], negselT[:, q0:q1])]
                ss.table = tab
                ss.cb = self.CB[:, 4 + h:5 + h]
                ss.scale = sc
                ss.V = lambda kt: VS[:, kt, 0:129]
                ss.ncols = 129
                ss.rtoks = [('KS',), ('QT3', hh)] + [('nsT', q) for q in range(4)]
                ss.vtoks = [('VS',)]

                def epi_s(qi, b, ji):
                    j = qi % 4
                    P.op('dve', lambda e: e.tensor_copy(out=raw[0][:, j, 0:129], in_=ps[:, b, 0:129]),
                         rd=[('ps', b)], wr=[('raw', 0, j)])

                def post_s(qc, ji, h=h, hh=hh):
                    par = (ji // 2) % 2
                    rt = [('raw', 0, j) for j in range(4)]
                    P.op('dve', lambda e: e.reciprocal(out=rr[0].unsqueeze(2), in_=raw[0][:, :, 128:129]),
                         rd=rt, wr=[('rr', 0)])
                    P.op('dve', lambda e: e.tensor_tensor(out=rr[0].unsqueeze(2), in0=rr[0].unsqueeze(2),
                                                          in1=gates[:, 4 * qc:4 * qc + 4, 3 * h + 1:3 * h + 2], op=ALU.mult),
                         rd=[('rr', 0), ('gates',)], wr=[('rr', 0)])
                    P.op('dve', lambda e: e.tensor_tensor(out=acc[par], in0=raw[0][:, :, 0:128],
                                                          in1=rr[0].unsqueeze(2).to_broadcast([128, 4, 128]), op=ALU.mult),
                         rd=rt + [('rr', 0)], wr=[('acc', par)])
                    P.op('pool', lambda e: e.tensor_tensor(out=acc[par], in0=acc[par], in1=ocmp[hh][:, 4 * qc:4 * qc + 4, :],
                                                           op=ALU.add),
                         rd=[('acc', par)] + [('ocmp', hh, q) for q in range(4 * qc, 4 * qc + 4)], wr=[('acc', par)])
                ss.epi = epi_s
                ss.post_a = post_s
                ss.post_b = lambda qc, ji: None
                sw = self.Stream()
                sw.mode = 'window'
                sw.lhs = lambda kt: KW[:, kt * 128:(kt + 1) * 128]
                sw.QT = QT3[hh]
                sw.extra = lambda kt, q0, q1: []
                sw.table = lambda mm, tab=tab: (self.W4F if mm == 4 else tab(mm))
                sw.cb = self.CB[:, 4 + h:5 + h]
                sw.scale = sc
                sw.V = lambda kt: VW[:, kt, 0:129]
                sw.ncols = 129
                sw.rtoks = [('KW',), ('QT3', hh)]
                sw.vtoks = [('VW',)]

                def epi_w(qi, b, ji):
                    j = qi % 4
                    P.op('dve', lambda e: e.tensor_copy(out=raw[1][:, j, 0:129], in_=ps[:, b, 0:129]),
                         rd=[('ps', b)], wr=[('raw', 1, j)])

                def post_w(qc, ji, h=h, hh=hh):
                    par = (ji // 2) % 2
                    rt = [('raw', 1, j) for j in range(4)]
                    P.op('dve', lambda e: e.reciprocal(out=rr[1].unsqueeze(2), in_=raw[1][:, :, 128:129]),
                         rd=rt, wr=[('rr', 1)])
                    P.op('dve', lambda e: e.tensor_tensor(out=rr[1].unsqueeze(2), in0=rr[1].unsqueeze(2),
                                                          in1=gates[:, 4 * qc:4 * qc + 4, 3 * h + 2:3 * h + 3], op=ALU.mult),
                         rd=[('rr', 1), ('gates',)], wr=[('rr', 1)])
                    P.op('dve', lambda e: e.tensor_tensor(out=tmpw, in0=raw[1][:, :, 0:128],
                                                          in1=rr[1].unsqueeze(2).to_broadcast([128, 4, 128]), op=ALU.mult),
                         rd=rt + [('rr', 1)], wr=[('tmpw',)])
                    P.op('pool', lambda e: e.tensor_tensor(out=obf[par], in0=tmpw, in1=acc[par], op=ALU.add),
                         rd=[('tmpw',), ('acc', par)], wr=[('obf', par, j) for j in range(4)])
                sw.epi = epi_w
                sw.post_a = post_w
                sw.post_b = lambda qc, ji, h=h, hh=hh: self.post_transpose(obf, (ji // 2) % 2, mixstg, hh % 2, qc, 10 + h, 'act')
                for qc in range(4):
                    joblist.append((ss, qc))
                    joblist.append((sw, qc))
            self.run_jobs(joblist, ring)
        P.barrier()
        sb.release(m)

    def phase_C(self, l, s):
        P, sb, ps = self.P, self.sb, self.ps
        m = sb.mark()
        csub = "1234"
        mixT = sb.take(NCH * S, BF16).rearrange("p (c t) -> p c t", c=NCH)
        for hd in range(16 if "1" in csub else 0):
            P.dma('sp', mixT[:, hd, :], self.MT[hd], wr=[('mixT', hd)])
        wbuf = [sb.take(NCH * 512, BF16).rearrange("p (c f) -> p c f", c=NCH) for _ in range(2)]
        hold = [sb.take(S, F32) for _ in range(2)]
        hnew = [sb.take(S, F32) for _ in range(2)]
        n = 0
        for i in range(4 if "1" in csub else 0):
            wb = wbuf[i % 2]
            wtoks = self.load_w_slab(wb, i % 2, self.w_out[l], [(512 * i, 512)])
            for ft in range(4):
                co = 4 * i + ft
                k2 = n % 2
                n += 1
                P.dma('sp', hold[k2], self.hT[s, :, co, :], rd=[('hTc', co)], wr=[('hold', k2)])
                for j in range(4):
                    b = self.bank()
                    for c in range(NCH):
                        P.op('pe', lambda e, b=b, c=c, wb=wb, ft=ft, j=j: e.matmul(
                            ps[:, b, :], wb[:, c, ft * 128:(ft + 1) * 128], mixT[:, c, j * 512:(j + 1) * 512],
                            start=(c == 0), stop=(c == NCH - 1)), rd=wtoks + [('mixT', c)], wr=[('ps', b)])
                    P.op('dve', lambda e, b=b, k2=k2, j=j: e.tensor_tensor(
                        out=hnew[k2][:, j * 512:(j + 1) * 512], in0=ps[:, b, :], in1=hold[k2][:, j * 512:(j + 1) * 512],
                        op=ALU.add), rd=[('ps', b), ('hold', k2)], wr=[('hnew', k2, j)])
                P.dma('sp', self.hT[s, :, co, :], hnew[k2], rd=[('hnew', k2, j) for j in range(4)], wr=[('hTc', co)])
        P.barrier()
        sb.release(m)
        u2T = sb.take(NCH * S, BF16).rearrange("p (c t) -> p c t", c=NCH)
        cw = sb.take(3 * 88, F32).rearrange("p (t f) -> p t f", t=3)
        cbv = sb.take(88, F32)
        wbuf = [sb.take(NCH * 512, BF16).rearrange("p (c f) -> p c f", c=NCH) for _ in range(2)]
        up_segs = lambda i: [(256 * i, 256), (DFF + 256 * i, 256)]
        pre_toks = [self.load_w_slab(wbuf[k], k, self.ffn_w_up[l], up_segs(k)) for k in range(2)]
        m2 = sb.mark()
        cwA = sb.take(4 * 128, F32).rearrange("p (t q) -> p t q", t=4)
        P.dma('sp', cwA[0:88, 0:3, :], self.ffn_conv_w[l].rearrange("t (f p) -> f t p", p=128), wr=[('cwA', 0)])
        P.dma('sp', cwA[0:88, 3, :], self.ffn_conv_b[l].rearrange("(f p) -> f p", p=128), wr=[('cwA', 1)])
        for t in range(4):
            b = self.bank()
            P.op('pe', lambda e, b=b, t=t: e.transpose(out=ps[:, b, 0:88], in_=cwA[0:88, t, :], identity=self.identF[0:88, 0:88]),
                 rd=[('cwA', 0), ('cwA', 1)], wr=[('ps', b)])
            if t < 3:
                P.op('dve', lambda e, b=b, t=t: e.tensor_copy(out=cw[:, t, :], in_=ps[:, b, 0:88]), rd=[('ps', b)], wr=[('cw',)])
            else:
                P.op('dve', lambda e, b=b: e.tensor_copy(out=cbv, in_=ps[:, b, 0:88]), rd=[('ps', b)], wr=[('cw',)])
        self.rmsnorm_to_uT(lambda j: ('none',), self.hT[s], self.g2, l * NCH, u2T)
        P.barrier()
        sb.release(m2)
        hgu = [sb.take(S + 2, F32) for _ in range(4)]
        cgu = [sb.take(S, F32) for _ in range(2)]
        aT = [sb.take(S, BF16) for _ in range(2)]
        for k in range(4):
            P.op('pool', lambda e, k=k: e.memset(hgu[k][:, 0:2], 0.0), wr=[('hgz', k)])
        utoks = lambda c, t0, t1: [('uT', c, i) for i in range(t0 // 256, (t1 + 255) // 256)]
        nt = 0
        for i in range(22 if "3" in csub else 0):
            wb = wbuf[i % 2]
            wtoks = pre_toks[i] if i < 2 else self.load_w_slab(wb, i % 2, self.ffn_w_up[l], up_segs(i))
            for ft in range(2):
                f = 2 * i + ft
                par = nt % 2
                nt += 1
                for br in range(2):
                    hb = hgu[2 * par + br]
                    hk = 2 * par + br
                    wcol = br * 256 + ft * 128
                    fi = f + 44 * br
                    for j in range(4):
                        b = self.bank()
                        for c in range(NCH):
                            P.op('pe', lambda e, b=b, c=c, wb=wb, wcol=wcol, j=j: e.matmul(
                                ps[:, b, :], wb[:, c, wcol:wcol + 128], u2T[:, c, j * 512:(j + 1) * 512],
                                start=(c == 0), stop=(c == NCH - 1)), rd=wtoks, wr=[('ps', b)])
                        P.op('act', lambda e, b=b, hb=hb, j=j: e.copy(out=hb[:, 2 + j * 512:2 + (j + 1) * 512], in_=ps[:, b, :]),
                             rd=[('ps', b), ('hgz', hk)], wr=[('hgu', hk, j)])
                    cg = cgu[br]
                    htok = [('hgu', hk, j) for j in range(4)]
                    P.op('dve', lambda e, hb=hb, cg=cg, fi=fi: e.tensor_scalar(
                        out=cg, in0=hb[:, 2:S + 2], scalar1=cw[:, 2, fi:fi + 1], scalar2=cbv[:, fi:fi + 1],
                        op0=ALU.mult, op1=ALU.add), rd=htok, wr=[('cgu', br)])
                    P.op('dve', lambda e, hb=hb, cg=cg, fi=fi: e.scalar_tensor_tensor(
                        out=cg, in0=hb[:, 1:S + 1], scalar=cw[:, 1, fi:fi + 1], in1=cg, op0=ALU.mult, op1=ALU.add),
                        rd=htok + [('cgu', br)], wr=[('cgu', br)])
                    P.op('dve', lambda e, hb=hb, cg=cg, fi=fi: e.scalar_tensor_tensor(
                        out=cg, in0=hb[:, 0:S], scalar=cw[:, 0, fi:fi + 1], in1=cg, op0=ALU.mult, op1=ALU.add),
                        rd=htok + [('cgu', br)], wr=[('cgu', br)])
                P.op('act', lambda e: e.activation(out=cgu[0], in_=cgu[0], func=AF.Silu), rd=[('cgu', 0)], wr=[('cgu', 0)])
                P.op('dve', lambda e, par=par: e.tensor_tensor(out=aT[par], in0=cgu[0], in1=cgu[1], op=ALU.mult),
                     rd=[('cgu', 0), ('cgu', 1)], wr=[('aT', par)])
                P.dma('sp', self.AT[f], aT[par], rd=[('aT', par)], wr=[('AT', f)])
        P.barrier()
        sb.release(m)
        m = sb.mark()
        actH = sb.take(NFF * 1024, BF16).rearrange("p (f t) -> p f t", f=NFF)
        wd = [sb.take(NFF * 256, BF16).rearrange("p (f c) -> p f c", f=NFF) for _ in range(2)]
        hold4 = [sb.take(1024, F32) for _ in range(2)]
        hnew4 = [sb.take(1024, F32) for _ in range(2)]
        nw = 0
        n = 0
        for half in range(2 if "4" in csub else 0):
            t0 = half * 1024
            for f in range(NFF):
                P.dma('sp', actH[:, f, :], self.AT[f, :, t0:t0 + 1024], wr=[('actH', f)])
            for i in range(8):
                wk = nw % 2
                nw += 1
                P.dma('pool', wd[wk], self.ffn_w_down[l, :, 256 * i:256 * (i + 1)].rearrange("(fc p) c -> p fc c", p=128),
                      wr=[('wd', wk)])
                for ft in range(2):
                    co = 2 * i + ft
                    k2 = n % 2
                    n += 1
                    P.dma('sp', hold4[k2], self.hT[s, :, co, t0:t0 + 1024], rd=[('hTd', co, half)], wr=[('hold4', k2)])
                    for j in range(2):
                        b = self.bank()
                        for fc in range(NFF):
                            P.op('pe', lambda e, b=b, fc=fc, wk=wk, ft=ft, j=j: e.matmul(
                                ps[:, b, :], wd[wk][:, fc, ft * 128:(ft + 1) * 128], actH[:, fc, j * 512:(j + 1) * 512],
                                start=(fc == 0), stop=(fc == NFF - 1)), rd=[('wd', wk), ('actH', fc)], wr=[('ps', b)])
                        P.op('dve', lambda e, b=b, k2=k2, j=j: e.tensor_tensor(
                            out=hnew4[k2][:, j * 512:(j + 1) * 512], in0=ps[:, b, :], in1=hold4[k2][:, j * 512:(j + 1) * 512],
                            op=ALU.add), rd=[('ps', b), ('hold4', k2)], wr=[('hnew4', k2, j)])
                    P.dma('sp', self.hT[s, :, co, t0:t0 + 1024], hnew4[k2], rd=[('hnew4', k2, j) for j in range(2)],
                          wr=[('hTd', co, half)])
        P.barrier()
        sb.release(m)

    def phase_Z(self, s):
        P, sb, ps = self.P, self.sb, self.ps
        m = sb.mark()
        yb = [sb.take(NCH * 256, F32).rearrange("p (c t) -> p c t", c=NCH) for _ in range(2)]
        ostg = [sb.take(D, F32) for _ in range(2)]
        cnt = [0]

        def out_cb(i, hb, rstd):
            y = yb[i % 2]
            for c in range(NCH):
                P.op('dve', lambda e, c=c: e.scalar_tensor_tensor(
                    out=y[:, c, :], in0=hb[:, c, :], scalar=self.g3[:, c:c + 1], in1=rstd, op0=ALU.mult, op1=ALU.mult),
                    rd=[('hb', i % 2), ('rstd',)], wr=[('yb', i % 2, c)])
            for tt in range(2):
                ok = cnt[0] % 2
                cnt[0] += 1
                for cg4 in range(4):
                    b = self.bank()
                    for cc in range(4):
                        c = cg4 * 4 + cc
                        P.op('pe', lambda e, b=b, cc=cc, c=c, tt=tt: e.transpose(
                            out=ps[:, b, cc * 128:(cc + 1) * 128], in_=y[:, c, tt * 128:(tt + 1) * 128], identity=self.identF),
                            rd=[('yb', i % 2, c)], wr=[('ps', b)])
                    if cg4 % 2 == 0:
                        P.op('act', lambda e, b=b, cg4=cg4, ok=ok: e.copy(out=ostg[ok][:, cg4 * 512:(cg4 + 1) * 512], in_=ps[:, b, :]),
                             rd=[('ps', b)], wr=[('ostg', ok, cg4)])
                    else:
                        P.op('dve', lambda e, b=b, cg4=cg4, ok=ok: e.tensor_copy(out=ostg[ok][:, cg4 * 512:(cg4 + 1) * 512], in_=ps[:, b, :]),
                             rd=[('ps', b)], wr=[('ostg', ok, cg4)])
                r0 = i * 256 + tt * 128
                P.dma('sp', self.out[s, r0:r0 + 128, :], ostg[ok], rd=[('ostg', ok, q) for q in range(4)], wr=[('out', s, r0)])
        self.rmsnorm_to_uT(lambda j: ('none',), self.hT[s], self.g3, 0, None, out_cb=out_cb)
        P.barrier()
        sb.release(m)

CST_IDENT = 0
CST_NM0 = 128
CST_W4 = 256
CST_OH = 384
CST_INVSC = 640
CST_OV = 641
CST_FM = 673
CST_E = 1185
CST_W = 3233


def t5_bucket_np(n):
    n = np.maximum(n, 0)
    nf = np.maximum(n, 1).astype(np.float32)
    large = 16 + (np.log(nf / np.float32(16)) / np.float32(math.log(128 / 16)) * np.float32(16)).astype(np.int32)
    large = np.minimum(large, 31)
    return np.where(n < 16, n, large)


def make_cst():
    c = np.zeros((128, CST_W), np.float32)
    c[:, CST_IDENT:CST_IDENT + 128] = np.eye(128, dtype=np.float32)
    p = np.arange(128)[:, None]
    q = np.arange(128)[None, :]
    c[:, CST_NM0:CST_NM0 + 128] = np.where(p > q, -BIG, 0.0)
    c[:, CST_W4:CST_W4 + 128] = np.where(q >= p, -BIG, 0.0)
    bk = t5_bucket_np(np.arange(256))
    oh = np.zeros((32, 256), np.float32)
    oh[bk, np.arange(256)] = 1.0
    oh[31, :] -= 1.0
    c[0:32, CST_OH:CST_OH + 256] = oh[:, ::-1]
    c[0:4, CST_INVSC] = 8.0
    c[4:10, CST_INVSC] = math.sqrt(128.0)
    cs_ = np.arange(NCMP) * 16
    ss_ = np.arange(NSEL) * 64
    ov = np.clip(np.minimum(cs_[:, None] + 32, ss_[None, :] + 64) - np.maximum(cs_[:, None], ss_[None, :]), 0, None) / 32.0
    c[0:NCMP, CST_OV:CST_OV + 32] = ov
    t = np.arange(S)
    blk = t // 64
    j = np.arange(NSEL)
    valid = j[None, :] <= blk[:, None]
    forced = (j[None, :] == 0) | (j[None, :] == blk[:, None]) | (j[None, :] == blk[:, None] - 1)
    fm = np.where(valid, np.where(forced, 1e4, 0.0), -1e30).astype(np.float32)
    c[:, CST_FM:CST_FM + 512] = fm.reshape(16, 128, 32).transpose(1, 0, 2).reshape(128, 512)
    k = np.arange(S)
    e = np.zeros((32, S), np.float32)
    e[k // 64, k] = 1.0
    c[0:32, CST_E:CST_E + S] = e
    return c


def make_in_map(inputs, seqs, L):
    f = lambda a: np.ascontiguousarray(np.asarray(a, dtype=np.float32))
    m = {
        "x": f(inputs["x"][seqs]),
        "attn_norm_g": f(inputs["attn_norm_g"][:L]),
        "w_in": f(inputs["w_in"][:L]),
        "fox_f_bias": f(inputs["fox_f_bias"][:L]),
        "diff_lam4": f(np.stack([inputs["diff_lq1"][:L], inputs["diff_lk1"][:L],
                                 inputs["diff_lq2"][:L], inputs["diff_lk2"][:L]], axis=1)),
        "diff_subln_g": f(inputs["diff_subln_g"][:L]),
        "nsa_cmp_pos": f(inputs["nsa_cmp_pos"][:L]),
        "nsa_cmp_w1": f(np.stack([inputs["nsa_cmp_wk1"][:L], inputs["nsa_cmp_wv1"][:L]], axis=1)),
        "nsa_cmp_w2": f(np.stack([inputs["nsa_cmp_wk2"][:L], inputs["nsa_cmp_wv2"][:L]], axis=1)),
        "w_out": f(inputs["w_out"][:L]),
        "ffn_norm_g": f(inputs["ffn_norm_g"][:L]),
        "ffn_w_up": f(inputs["ffn_w_up"][:L]),
        "ffn_conv_w": f(inputs["ffn_conv_w"][:L]),
        "ffn_conv_b": f(inputs["ffn_conv_b"][:L]),
        "ffn_w_down": f(inputs["ffn_w_down"][:L]),
        "rel_bias": f(inputs["rel_bias"]),
        "final_norm_g": f(inputs["final_norm_g"]),
        "cst": make_cst(),
    }
    return m


def kernel(**inputs):
    L, NSEQ, NCORE = 4, 2, 8
    b = Builder(L, NSEQ)
    nc = b.build(phases=("x0", "A", "Bf", "Bd", "Bn", "C", "Z"))
    in_maps = [make_in_map(inputs, slice(NSEQ * i, NSEQ * (i + 1)), L) for i in range(NCORE)]
    res = run_bass_kernel_spmd(nc, in_maps, core_ids=list(range(NCORE)))
    return np.concatenate([np.asarray(r["out"]) for r in res.results], axis=0).astype(np.float32)
```

```python
import math
from contextlib import ExitStack
import numpy as np
import concourse.bass as bass
import concourse.mybir as mybir
from concourse.bass_utils import run_bass_kernel_spmd

F32 = mybir.dt.float32
BF16 = mybir.dt.bfloat16
AF = mybir.ActivationFunctionType
ALU = mybir.AluOpType
AX = mybir.AxisListType

S = 2048
D = 2048
NCH = 16
DFF = 5632
NFF = 44
EPS = 1e-6
BIG = 30000.0
N_IN = 6168
NCMP = 127
NSEL = 32
ENGS = ['pe', 'act', 'dve', 'pool', 'sp']
ENG_ATTR = {'pe': 'tensor', 'act': 'scalar', 'dve': 'vector', 'pool': 'gpsimd', 'sp': 'sync'}
SAME_ENG_SYNC = True
NW = 45056

C_FQ, C_FK, C_FV, C_FF = 0, 768, 1536, 2304
C_DQ, C_DK, C_DV = 2310, 2822, 3334
C_NQ = 3846
C_NKC, C_NVC, C_NKS, C_NVS, C_NKW, C_NVW, C_NG = 4614, 4870, 5126, 5382, 5638, 5894, 6150
PT_FQ, PT_FK, PT_DQ, PT_DK, PT_NQ, PT_NKC, PT_NVC, PT_NKS, PT_NKW = 0, 6, 12, 16, 20, 26, 28, 30, 32
NPT = 34
VT_FV, VT_DV, VT_NVS, VT_NVW = 0, 768, 1280, 1536
VTC = 1792


class Prog:
    def __init__(self, nc, nslots=8):
        self.nc = nc
        self.streams = {e: [] for e in ENGS}
        self.cnt = {e: 0 for e in ENGS}
        self.known = {e: {} for e in ENGS}
        self.last_w = {}
        self.readers = {}
        self.slots = {q: [[f"{q}_d{i}", 0] for i in range(nslots)] for q in ('sp', 'pool', 'act')}
        self.rr = {q: 0 for q in self.slots}
        self.nops = 0

    def op(self, eng, fn, rd=(), wr=(), dma=False):
        wdeps = {}
        rdeps = {}

        def add(dd, ev):
            if ev is None:
                return
            k, v = ev
            if dd.get(k, 0) < v:
                dd[k] = v
        for t in rd:
            add(wdeps, self.last_w.get(t))
            if t[0] == 'ps':
                for k, v in self.readers.get(t, {}).items():
                    add(rdeps, (k, v))
        for t in wr:
            add(wdeps, self.last_w.get(t))
            for k, v in self.readers.get(t, {}).items():
                add(rdeps, (k, v))
        if dma:
            sl = self.slots[eng][self.rr[eng]]
            self.rr[eng] = (self.rr[eng] + 1) % len(self.slots[eng])
            add(wdeps, (sl[0], sl[1]))
            sl[1] += 16
            ev = (sl[0], sl[1])
            inc = (sl[0], 16)
        else:
            self.cnt[eng] += 1
            ev = (eng, self.cnt[eng])
            inc = (eng, 1)
        waits = []
        kn = self.known[eng]
        for dd, isw in ((wdeps, True), (rdeps, False)):
            for k, v in dd.items():
                if v <= 0:
                    continue
                if k == eng:
                    if (not isw) or eng == 'pe' or not SAME_ENG_SYNC:
                        continue
                if kn.get(k, 0) >= v:
                    continue
                kn[k] = v
                waits.append((k, v))
        self.streams[eng].append((waits, fn, inc))
        for t in rd:
            r = self.readers.setdefault(t, {})
            if r.get(ev[0], 0) < ev[1]:
                r[ev[0]] = ev[1]
        for t in wr:
            self.last_w[t] = ev
            self.readers[t] = {}
        self.nops += 1
        return ev

    def dma(self, q, out, in_, rd=(), wr=(), slow=False):
        if slow:
            return self.op(q, lambda e, o=out, i=in_: e.dma_start(out=o, in_=i, allow_slow_non_contiguous=True),
                           rd=rd, wr=wr, dma=True)
        return self.op(q, lambda e, o=out, i=in_: e.dma_start(out=o, in_=i), rd=rd, wr=wr, dma=True)

    def all_events(self):
        evs = [(e, self.cnt[e]) for e in ENGS]
        for q in self.slots:
            for nm, v in self.slots[q]:
                evs.append((nm, v))
        return evs

    def barrier(self):
        evs = self.all_events()
        for eng in ENGS:
            waits = []
            kn = self.known[eng]
            for k, v in evs:
                if k == eng or v <= 0 or kn.get(k, 0) >= v:
                    continue
                kn[k] = v
                waits.append((k, v))
            if waits:
                self.streams[eng].append((waits, None, None))
        self.last_w = {}
        self.readers = {}

    def emit(self):
        nc = self.nc
        keys = list(ENGS)
        for q in self.slots:
            keys += [s[0] for s in self.slots[q]]
        with ExitStack() as st:
            sems = {k: st.enter_context(nc.semaphore(k)) for k in keys}
            block = st.enter_context(nc.Block())
            for eng in ENGS:
                stream = self.streams[eng]

                def body(e, stream=stream):
                    for waits, fn, inc in stream:
                        for k, v in waits:
                            e.wait_ge(sems[k], v)
                        if fn is not None:
                            ins = fn(e)
                            ins.then_inc(sems[inc[0]], inc[1])
                getattr(block, ENG_ATTR[eng])(body)


class Arena:
    def __init__(self, big):
        self.big = big
        self.off = 0

    def take(self, nelem, dtype=F32):
        nb = nelem * (4 if dtype == F32 else 2)
        nw = (nb + 3) // 4
        a = self.off
        self.off += (nw + 7) // 8 * 8
        assert self.off <= NW, f"SBUF arena overflow {self.off}"
        ap = self.big[:, a:a + nw]
        if dtype != F32:
            ap = ap.bitcast(dtype)[:, :nelem]
        return ap

    def mark(self):
        return self.off

    def release(self, m):
        self.off = m


def fm_slabs():
    tiles = []
    for h in range(6):
        tiles.append((C_FQ + 128 * h, PT_FQ + h))
    for h in range(6):
        tiles.append((C_FK + 128 * h, PT_FK + h))
    for h in range(4):
        tiles.append((C_DQ + 128 * h, PT_DQ + h))
    for h in range(4):
        tiles.append((C_DK + 128 * h, PT_DK + h))
    for h in range(6):
        tiles.append((C_NQ + 128 * h, PT_NQ + h))
    for g in range(2):
        tiles.append((C_NKC + 128 * g, PT_NKC + g))
    for g in range(2):
        tiles.append((C_NVC + 128 * g, PT_NVC + g))
    for g in range(2):
        tiles.append((C_NKS + 128 * g, PT_NKS + g))
    for g in range(2):
        tiles.append((C_NKW + 128 * g, PT_NKW + g))
    slabs = []
    for i in range(0, len(tiles), 4):
        grp = tiles[i:i + 4]
        segs = []
        for c0, _ in grp:
            if segs and segs[-1][0] + segs[-1][1] == c0:
                segs[-1] = (segs[-1][0], segs[-1][1] + 128)
            else:
                segs.append((c0, 128))
        slabs.append((segs, [t[1] for t in grp]))
    return slabs


def tok_slabs():
    return [
        ([(C_FV, 512)], 0, 512, False),
        ([(C_FV + 512, 256), (C_DV, 256)], 512, 512, False),
        ([(C_DV + 256, 256), (C_NVS, 256)], 1024, 512, False),
        ([(C_NVW, 256), (C_NG, 18)], 1536, 256, True),
    ]


class Builder:
    def __init__(self, L, NSEQ, debug=False):
        self.L, self.NSEQ, self.debug = L, NSEQ, debug
        nc = self.nc = bass.Bass("TRN2", target_bir_lowering=False)
        self.P = Prog(nc)
        dk = "ExternalOutput" if debug else "Internal"
        din = lambda n, sh: nc.dram_tensor(n, sh, F32, kind="ExternalInput").ap()
        self.x = din("x", [NSEQ, S, D])
        self.attn_norm_g = din("attn_norm_g", [L, D])
        self.w_in = din("w_in", [L, D, N_IN])
        self.fox_f_bias = din("fox_f_bias", [L, 6])
        self.lam4 = din("diff_lam4", [L, 4, 64])
        self.diff_subln_g = din("diff_subln_g", [L, 128])
        self.nsa_cmp_pos = din("nsa_cmp_pos", [L, 32, 128])
        self.nsa_cmp_w1 = din("nsa_cmp_w1", [L, 2, 4096, 128])
        self.nsa_cmp_w2 = din("nsa_cmp_w2", [L, 2, 128, 128])
        self.w_out = din("w_out", [L, D, D])
        self.ffn_norm_g = din("ffn_norm_g", [L, D])
        self.ffn_w_up = din("ffn_w_up", [L, D, 2 * DFF])
        self.ffn_conv_w = din("ffn_conv_w", [L, 3, 2 * DFF])
        self.ffn_conv_b = din("ffn_conv_b", [L, 2 * DFF])
        self.ffn_w_down = din("ffn_w_down", [L, DFF, D])
        self.rel_bias = din("rel_bias", [32, 10])
        self.final_norm_g = din("final_norm_g", [D])
        self.cst = din("cst", [128, CST_W])
        self.out = nc.dram_tensor("out", [NSEQ, S, D], F32, kind="ExternalOutput").ap()
        self.hT = nc.dram_tensor("hT", [NSEQ, 128, NCH, S], F32, kind=dk).ap()
        self.PT = nc.dram_tensor("PT", [NPT, 128, S], BF16, kind=dk).ap()
        self.VT = nc.dram_tensor("VT", [128, VTC // 128, 16, 128], BF16, kind=dk).ap()
        self.GT = nc.dram_tensor("GT", [16, 128, 18], F32, kind=dk).ap()
        self.FAR = nc.dram_tensor("FAR", [6, 6, S], BF16, kind=dk).ap()
        self.FAL = nc.dram_tensor("FAL", [6, 6, S], BF16, kind=dk).ap()
        self.TCr_h = nc.dram_tensor("TCr", [11, 4352], F32, kind=dk)
        self.TCr = self.TCr_h.ap()[0:10]
        self.MT = nc.dram_tensor("MT", [16, 128, S], BF16, kind=dk).ap()
        self.AT = nc.dram_tensor("AT", [NFF, 128, S], BF16, kind="Internal").ap()

    def build(self, phases=("x0", "A")):
        nc, P = self.nc, self.P
        with ExitStack() as st:
            big = st.enter_context(nc.sbuf_tensor("big", [128, NW], F32))
            self.ps = st.enter_context(nc.psum_tensor("ps", [128, 8, 512], F32))
            self.sb = Arena(big)
            self.bank_rr = 0
            self.setup_consts()
            for s in range(self.NSEQ):
                if "x0" in phases:
                    self.phase_x0(s)
            for l in range(self.L):
                for s in range(self.NSEQ):
                    if "A" in phases:
                        self.phase_A(l, s)
                    if "Bf" in phases:
                        self.phase_B_fox(l, s)
                    if "Bd" in phases:
                        self.phase_B_diff(l, s)
                    if "Bn" in phases:
                        self.phase_B_nsa(l, s)
                    if "C" in phases:
                        self.phase_C(l, s)
            for s in range(self.NSEQ):
                if "Z" in phases:
                    self.phase_Z(s)
            P.barrier()
            P.emit()
        return nc

    def bank(self):
        while True:
            b = self.bank_rr
            self.bank_rr = (self.bank_rr + 1) % 8
            if b not in getattr(self, 'bank_excl', ()):
                return b

    def setup_consts(self):
        P, sb, nc, ps = self.P, self.sb, self.nc, self.ps
        L = self.L
        cst = self.cst
        self.identF = sb.take(128, F32)
        self.NM0F = sb.take(128, F32)
        self.W4F = sb.take(128, F32)
        self.FM = sb.take(512, F32).rearrange("p (a m) -> p a m", a=16)
        P.dma('sp', self.identF, cst[:, CST_IDENT:CST_IDENT + 128], wr=[('identF',)])
        P.dma('sp', self.NM0F, cst[:, CST_NM0:CST_NM0 + 128], wr=[('NM0F',)])
        P.dma('sp', self.W4F, cst[:, CST_W4:CST_W4 + 128], wr=[('W4F',)])
        P.dma('sp', self.FM, cst[:, CST_FM:CST_FM + 512].rearrange("p (a m) -> p a m", a=16), wr=[('FM',)])
        self.identB = sb.take(128, BF16)
        P.op('dve', lambda e: e.tensor_copy(out=self.identB, in_=self.identF), rd=[('identF',)], wr=[('identB',)])
        self.onesB = sb.take(128, BF16)
        P.op('dve', lambda e: e.memset(self.onesB, 1.0), wr=[('onesB',)])
        self.epsT = sb.take(1, F32)
        P.op('dve', lambda e: e.memset(self.epsT, EPS), wr=[('epsT',)])
        self.OVb = sb.take(32, BF16)
        P.dma('pool', self.OVb[0:NCMP, :], cst[0:NCMP, CST_OV:CST_OV + 32], wr=[('OVb',)])
        self.Epad = sb.take(S, BF16)
        P.op('pool', lambda e: e.memset(self.Epad, 0.0), wr=[('Epad',)])
        P.dma('pool', self.Epad[0:32, :], cst[0:32, CST_E:CST_E + S], wr=[('Epad',)])
        self.BT = sb.take(10 * 2 * 128, F32).rearrange("p (h k c) -> p h k c", h=10, k=2)
        self.CB = sb.take(10, F32)
        P.dma('sp', self.CB, self.rel_bias[31:32, :].partition_broadcast(128), wr=[('CB',)])
        self.g1 = sb.take(L * NCH, F32)
        self.g2 = sb.take(L * NCH, F32)
        self.g3 = sb.take(NCH, F32)
        P.dma('sp', self.g1.rearrange("p (l c) -> p l c", l=L),
              self.attn_norm_g.rearrange("l (c p) -> p l c", p=128), wr=[('g1',)], slow=True)
        P.dma('sp', self.g2.rearrange("p (l c) -> p l c", l=L),
              self.ffn_norm_g.rearrange("l (c p) -> p l c", p=128), wr=[('g2',)], slow=True)
        P.dma('sp', self.g3, self.final_norm_g.rearrange("(c p) -> p c", p=128), wr=[('g3',)], slow=True)
        self.nfb = sb.take(L, F32)
        P.dma('sp', self.nfb[0:6, :], self.fox_f_bias.rearrange("l h -> h l"), wr=[('nfb',)], slow=True)
        P.op('dve', lambda e: e.tensor_scalar(out=self.nfb[0:6, :], in0=self.nfb[0:6, :], scalar1=-1.0, scalar2=None,
                                              op0=ALU.mult), rd=[('nfb',)], wr=[('nfb',)])
        m = sb.mark()
        onesrow = sb.take(S, BF16)
        P.op('pool', lambda e: e.memset(onesrow[0:6, :], 1.0), wr=[('onesrow',)])
        for r in range(3):
            P.dma('sp', self.FAR[:, 3 + r, :], onesrow[0:6, :], rd=[('onesrow',)], wr=[('FAR1', r)])
            P.dma('sp', self.FAL[:, r, :], onesrow[0:6, :], rd=[('onesrow',)], wr=[('FAL1', r)])
        ohF = sb.take(256, F32)
        invsc = sb.take(1, F32)
        relS = sb.take(10, F32)
        tt = sb.take(256, F32)
        fill = sb.take(2048, F32)
        P.dma('sp', ohF[0:32, :], cst[0:32, CST_OH:CST_OH + 256], wr=[('ohF',)])
        P.dma('sp', invsc[0:10, :], cst[0:10, CST_INVSC:CST_INVSC + 1], wr=[('invsc',)], slow=True)
        P.dma('sp', relS[0:32, :], self.rel_bias[:, :], wr=[('relS',)])
        P.op('pe', lambda e: e.matmul(ps[0:10, 0, 0:256], relS[0:32, 0:10], ohF[0:32, 0:256], start=True, stop=True),
             rd=[('ohF',), ('relS',)], wr=[('ps', 0)])
        P.op('dve', lambda e: e.tensor_scalar(out=tt[0:10, :], in0=ps[0:10, 0, 0:256], scalar1=invsc[0:10, 0:1],
                                              scalar2=None, op0=ALU.mult), rd=[('ps', 0), ('invsc',)], wr=[('tt',)])
        P.dma('sp', self.TCr[:, 2048:2304], tt[0:10, :], rd=[('tt',)], wr=[('TCr', 1)])
        P.op('pool', lambda e: e.memset(fill[0:10, :], 0.0), wr=[('fill',)])
        P.dma('sp', self.TCr[:, 0:2048], fill[0:10, :], rd=[('fill',)], wr=[('TCr', 0)])
        P.op('pool', lambda e: e.memset(fill[0:10, :], -BIG), rd=[], wr=[('fill',)])
        P.dma('sp', self.TCr[:, 2304:4352], fill[0:10, :], rd=[('fill',)], wr=[('TCr', 2)])
        P.barrier()
        Y = [sb.take(128, F32) for _ in range(2)]
        n = 0
        for h in range(10):
            for k, off in enumerate((0, 128)):
                y = Y[n % 2]
                src = bass.AP(tensor=self.TCr_h, offset=h * 4352 + 2176 - off, ap=[[1, 128], [1, 128]])
                P.dma('sp', y, src, wr=[('Y', n % 2)])
                P.op('dve', lambda e, y=y, h=h, k=k: e.tensor_copy(out=self.BT[:, h, k, :], in_=y[:, ::-1]),
                     rd=[('Y', n % 2)], wr=[('BT', h, k)])
                n += 1
        P.barrier()
        sb.release(m)

    def phase_x0(self, s):
        P, sb, ps = self.P, self.sb, self.ps
        m = sb.mark()
        xins = [sb.take(4 * D, F32).rearrange("p (a c) -> p a c", a=4) for _ in range(2)]
        stgs = [sb.take(NCH * 512, F32).rearrange("p (c t) -> p c t", c=NCH) for _ in range(2)]
        for j in range(4):
            xin = xins[j % 2]
            stg = stgs[j % 2]
            P.dma('sp', xin, self.x[s, j * 512:(j + 1) * 512, :].rearrange("(a p) c -> p a c", p=128),
                  wr=[('xin', j % 2)])
            for cc in range(NCH):
                b = self.bank()
                for a in range(4):
                    P.op('pe', lambda e, b=b, a=a, cc=cc, xin=xin: e.transpose(
                        out=ps[:, b, a * 128:(a + 1) * 128], in_=xin[:, a, cc * 128:(cc + 1) * 128],
                        identity=self.identF), rd=[('xin', j % 2)], wr=[('ps', b)])
                if cc % 2 == 0:
                    P.op('act', lambda e, b=b, cc=cc, stg=stg: e.copy(out=stg[:, cc, :], in_=ps[:, b, :]),
                         rd=[('ps', b)], wr=[('stg', j % 2, cc)])
                else:
                    P.op('dve', lambda e, b=b, cc=cc, stg=stg: e.tensor_copy(out=stg[:, cc, :], in_=ps[:, b, :]),
                         rd=[('ps', b)], wr=[('stg', j % 2, cc)])
            P.dma('sp', self.hT[s, :, :, j * 512:(j + 1) * 512], stg,
                  rd=[('stg', j % 2, c) for c in range(NCH)], wr=[('hT', s, j)])
        P.barrier()
        sb.release(m)

    def rmsnorm_to_uT(self, src_tok, hT_s, gain, goff, uT, out_cb=None):
        P, sb, ps = self.P, self.sb, self.ps
        hbuf = [sb.take(NCH * 256, F32).rearrange("p (c t) -> p c t", c=NCH) for _ in range(2)]
        sq = sb.take(NCH * 256, BF16).rearrange("p (c t) -> p c t", c=NCH)
        t2 = sb.take(256, F32)
        rstd = sb.take(256, F32)
        banks = {}

        def p1(i):
            hb = hbuf[i % 2]
            P.dma('sp', hb, hT_s[:, :, i * 256:(i + 1) * 256], rd=[src_tok(i // 2)], wr=[('hb', i % 2)])
            P.op('act', lambda e, hb=hb: e.activation(out=sq, in_=hb, func=AF.Square),
                 rd=[('hb', i % 2)], wr=[('sq',)])
            b = 6 + (i % 2)
            banks[i] = b
            for c in range(NCH):
                P.op('pe', lambda e, b=b, c=c: e.matmul(ps[:, b, 0:256], self.onesB, sq[:, c, :],
                                                       start=(c == 0), stop=(c == NCH - 1)),
                     rd=[('sq',), ('onesB',)], wr=[('ps', b)])

        def p2(i):
            b = banks[i]
            P.op('act', lambda e, b=b: e.activation(out=t2, in_=ps[:, b, 0:256], func=AF.Sqrt, scale=1.0 / D,
                                                   bias=self.epsT[:, 0:1]), rd=[('ps', b)], wr=[('t2',)])
            P.op('dve', lambda e: e.reciprocal(out=rstd, in_=t2), rd=[('t2',)], wr=[('rstd',)])

        def p3(i):
            hb = hbuf[i % 2]
            if out_cb is not None:
                out_cb(i, hb, rstd)
                return
            for c in range(NCH):
                P.op('dve', lambda e, hb=hb, c=c, i=i: e.scalar_tensor_tensor(
                    out=uT[:, c, i * 256:(i + 1) * 256], in0=hb[:, c, :], scalar=gain[:, goff + c:goff + c + 1],
                    in1=rstd, op0=ALU.mult, op1=ALU.mult),
                    rd=[('hb', i % 2), ('rstd',)], wr=[('uT', c, i)])
        self.bank_excl = (6, 7)
        p1(0)
        p2(0)
        for i in range(8):
            if i + 1 < 8:
                p1(i + 1)
            p3(i)
            if i + 1 < 8:
                p2(i + 1)
        self.bank_excl = ()

    def load_w_slab(self, wb, key, wsrc, segs):
        P = self.P
        off = 0
        toks = []
        for si, (c0, n) in enumerate(segs):
            tok = ('wb', key, si)
            P.dma('pool', wb[:, :, off:off + n], wsrc[:, c0:c0 + n].rearrange("(cc p) f -> p cc f", p=128),
                  wr=[tok])
            toks.append(tok)
            off += n
        return toks

    def phase_A(self, l, s):
        P, sb, ps, nc = self.P, self.sb, self.ps, self.nc
        m = sb.mark()
        uT = sb.take(NCH * S, BF16).rearrange("p (c t) -> p c t", c=NCH)
        wbuf = [sb.take(NCH * 512, BF16).rearrange("p (c f) -> p c f", c=NCH) for _ in range(2)]
        wsrc = self.w_in[l]
        pre_slabs = fm_slabs()[:2]
        pre_toks = [self.load_w_slab(wbuf[k], k, wsrc, pre_slabs[k][0]) for k in range(2)]
        m2 = sb.mark()
        self.rmsnorm_to_uT(lambda j: ('hT', s, j), self.hT[s], self.g1, l * NCH, uT)
        stgF = [sb.take(S, BF16) for _ in range(2)]
        stgT = [sb.take(512, BF16) for _ in range(2)]
        stgG = [sb.take(18, F32) for _ in range(2)]
        wsrc = self.w_in[l]
        utoks = lambda c, t0, t1: [('uT', c, i) for i in range(t0 // 256, (t1 + 255) // 256)]
        nslab = 0
        nst = 0
        for segs, tiles in fm_slabs():
            k = nslab % 2
            nslab += 1
            wb = wbuf[k]
            if nslab <= 2:
                wtoks = pre_toks[k]
            else:
                wtoks = self.load_w_slab(wb, k, wsrc, segs)
            for ft, pti in enumerate(tiles):
                sk = nst % 2
                nst += 1
                for j in range(4):
                    b = self.bank()
                    for c in range(NCH):
                        P.op('pe', lambda e, b=b, c=c, wb=wb, ft=ft, j=j: e.matmul(
                            ps[:, b, :], wb[:, c, ft * 128:(ft + 1) * 128], uT[:, c, j * 512:(j + 1) * 512],
                            start=(c == 0), stop=(c == NCH - 1)),
                            rd=wtoks + utoks(c, j * 512, (j + 1) * 512), wr=[('ps', b)])
                    if j % 2 == 0:
                        P.op('act', lambda e, b=b, sk=sk, j=j: e.copy(out=stgF[sk][:, j * 512:(j + 1) * 512],
                                                                     in_=ps[:, b, :]),
                             rd=[('ps', b)], wr=[('stgF', sk, j)])
                    else:
                        P.op('dve', lambda e, b=b, sk=sk, j=j: e.tensor_copy(out=stgF[sk][:, j * 512:(j + 1) * 512],
                                                                            in_=ps[:, b, :]),
                             rd=[('ps', b)], wr=[('stgF', sk, j)])
                P.dma('sp', self.PT[pti], stgF[sk], rd=[('stgF', sk, j) for j in range(4)], wr=[('PT', pti)])
        ntt = 0
        for segs, voff, nv, gates in tok_slabs():
            k = nslab % 2
            nslab += 1
            wb = wbuf[k]
            wtoks = self.load_w_slab(wb, k, wsrc, segs)
            ncol = sum(n for _, n in segs)
            for tt in range(16):
                sk = ntt % 2
                ntt += 1
                b = self.bank()
                for c in range(NCH):
                    P.op('pe', lambda e, b=b, c=c, wb=wb, tt=tt, ncol=ncol: e.matmul(
                        ps[:, b, 0:ncol], uT[:, c, tt * 128:(tt + 1) * 128], wb[:, c, 0:ncol],
                        start=(c == 0), stop=(c == NCH - 1)),
                        rd=wtoks + utoks(c, tt * 128, (tt + 1) * 128), wr=[('ps', b)])
                if tt % 2 == 0:
                    P.op('act', lambda e, b=b, sk=sk, nv=nv: e.copy(out=stgT[sk][:, 0:nv], in_=ps[:, b, 0:nv]),
                         rd=[('ps', b)], wr=[('stgT', sk)])
                else:
                    P.op('dve', lambda e, b=b, sk=sk, nv=nv: e.tensor_copy(out=stgT[sk][:, 0:nv], in_=ps[:, b, 0:nv]),
                         rd=[('ps', b)], wr=[('stgT', sk)])
                P.dma('sp', self.VT[:, voff // 128:(voff + nv) // 128, tt, :],
                      stgT[sk][:, 0:nv].rearrange("p (h d) -> p h d", d=128), rd=[('stgT', sk)],
                      wr=[('VT', tt, voff)])
                if gates:
                    P.op('act', lambda e, b=b, sk=sk, nv=nv: e.activation(out=stgG[sk], in_=ps[:, b, nv:nv + 18],
                                                                         func=AF.Sigmoid),
                         rd=[('ps', b)], wr=[('stgG', sk)])
                    P.dma('sp', self.GT[tt], stgG[sk], rd=[('stgG', sk)], wr=[('GT', tt)])
        P.barrier()
        sb.release(m2)
        wff = sb.take(NCH * 6, BF16).rearrange("p (c f) -> p c f", c=NCH)
        P.dma('pool', wff, wsrc[:, C_FF:C_FF + 6].rearrange("(cc p) f -> p cc f", p=128), wr=[('wff',)])
        e1 = sb.take(S, F32)
        onesF = sb.take(S, F32)
        cs = sb.take(S, F32)
        P.op('pool', lambda e: e.memset(onesF[0:6, :], 1.0), wr=[('onesF',)])
        for j in range(4):
            b = self.bank()
            for c in range(NCH):
                P.op('pe', lambda e, b=b, c=c, j=j: e.matmul(ps[0:6, b, :], wff[:, c, :], uT[:, c, j * 512:(j + 1) * 512],
                                                            start=(c == 0), stop=(c == NCH - 1)),
                     rd=[('wff',)] + utoks(c, j * 512, (j + 1) * 512), wr=[('ps', b)])
            P.op('act', lambda e, b=b, j=j: e.activation(out=e1[0:6, j * 512:(j + 1) * 512], in_=ps[0:6, b, :],
                                                        func=AF.Exp, scale=-1.0, bias=self.nfb[0:6, l:l + 1]),
                 rd=[('ps', b), ('nfb',)], wr=[('e1', j)])
        e1t = [('e1', j) for j in range(4)]
        P.op('dve', lambda e: e.tensor_scalar(out=e1[0:6, :], in0=e1[0:6, :], scalar1=1.0, scalar2=None, op0=ALU.add),
             rd=e1t, wr=e1t)
        P.op('act', lambda e: e.activation(out=e1[0:6, :], in_=e1[0:6, :], func=AF.Ln), rd=e1t, wr=e1t)
        P.op('dve', lambda e: e.tensor_tensor_scan(out=cs[0:6, :], data0=onesF[0:6, :], data1=e1[0:6, :], initial=0.0,
                                                  op0=ALU.mult, op1=ALU.add), rd=e1t + [('onesF',)], wr=[('cs',)])
        sq128 = math.sqrt(128.0)
        P.op('dve', lambda e: e.tensor_scalar(out=cs[0:6, :], in0=cs[0:6, :], scalar1=-sq128, scalar2=None,
                                              op0=ALU.mult), rd=[('cs',)], wr=[('cs',)])
        pcs = [sb.take(S, BF16) for _ in range(3)]
        ncs = [sb.take(S, BF16) for _ in range(3)]
        for r in range(3):
            P.op('dve', lambda e, r=r: e.tensor_copy(out=pcs[r][0:6, :], in_=cs[0:6, :]), rd=[('cs',)], wr=[('pcs', r)])
            if r < 2:
                P.op('dve', lambda e, r=r: e.tensor_tensor(out=cs[0:6, :], in0=cs[0:6, :], in1=pcs[r][0:6, :],
                                                          op=ALU.subtract), rd=[('cs',), ('pcs', r)], wr=[('cs',)])
            P.op('dve', lambda e, r=r: e.tensor_scalar(out=ncs[r][0:6, :], in0=pcs[r][0:6, :], scalar1=-1.0,
                                                      scalar2=None, op0=ALU.mult), rd=[('pcs', r)], wr=[('ncs', r)])
            P.dma('sp', self.FAR[:, r, :], pcs[r][0:6, :], rd=[('pcs', r)], wr=[('FAR', r)])
            P.dma('sp', self.FAL[:, 3 + r, :], ncs[r][0:6, :], rd=[('ncs', r)], wr=[('FAL', r)])
        P.barrier()
        sb.release(m)


    class Stream:
        pass

    def emit_S(self, st, qc, buf, bufid):
        P, ps = self.P, self.ps
        if st.mode == 'causal':
            kts = range(0, 4 * qc + 4)
        else:
            kts = range(max(0, 4 * qc - 4), 4 * qc + 4)
        ktmin = kts[0] if st.mode == 'window' else 0
        for kt in kts:
            qi0 = max(kt, 4 * qc)
            qi1 = 4 * qc + 3 if st.mode == 'causal' else min(kt + 4, 4 * qc + 3)
            q0, q1 = qi0 * 128, (qi1 + 1) * 128
            n = q1 - q0
            b = self.bank()
            mms = [(st.lhs(kt), st.QT[:, q0:q1])] + st.extra(kt, q0, q1)
            for idx, (lh, rh) in enumerate(mms):
                P.op('pe', lambda e, b=b, n=n, lh=lh, rh=rh, idx=idx, last=len(mms) - 1: e.matmul(
                    ps[:, b, 0:n], lh, rh, start=(idx == 0), stop=(idx == last)),
                    rd=st.rtoks, wr=[('ps', b)])
            for qi in range(qi0, qi1 + 1):
                tab = st.table(qi - kt)
                if tab is not None:
                    a = (qi - qi0) * 128
                    P.op('dve', lambda e, b=b, a=a, tab=tab: e.tensor_tensor(
                        out=ps[:, b, a:a + 128], in0=ps[:, b, a:a + 128], in1=tab, op=ALU.add),
                        rd=[('ps', b)], wr=[('ps', b)])
            off = q0 - qc * 512
            slot = kt - ktmin
            if st.cb is not None:
                fn = lambda e, b=b, n=n, off=off, slot=slot: e.activation(
                    out=buf[:, slot, off:off + n], in_=ps[:, b, 0:n], func=AF.Exp, scale=st.scale, bias=st.cb)
            else:
                fn = lambda e, b=b, n=n, off=off, slot=slot: e.activation(
                    out=buf[:, slot, off:off + n], in_=ps[:, b, 0:n], func=AF.Exp, scale=st.scale)
            P.op('act', fn, rd=[('ps', b)], wr=[('pt', bufid, slot)])

    def emit_PV(self, st, qc, buf, bufid, ji):
        P, ps = self.P, self.ps
        ktmin = max(0, 4 * qc - 4) if st.mode == 'window' else 0
        for qi in range(4 * qc, 4 * qc + 4):
            kts = range(0, qi + 1) if st.mode == 'causal' else range(max(0, qi - 4), qi + 1)
            b = self.bank()
            a = (qi - 4 * qc) * 128
            for idx, kt in enumerate(kts):
                slot = kt - ktmin
                P.op('pe', lambda e, b=b, a=a, slot=slot, kt=kt, idx=idx, last=len(kts) - 1: e.matmul(
                    ps[:, b, 0:st.ncols], buf[:, slot, a:a + 128], st.V(kt), start=(idx == 0), stop=(idx == last)),
                    rd=[('pt', bufid, slot)] + st.vtoks, wr=[('ps', b)])
            st.epi(qi, b, ji)

    def run_jobs(self, jobs, ring, per_head=None, after_head=None):
        n = len(jobs)

        def hook(pi):
            if per_head is not None and (pi + 1) % per_head == 0:
                after_head(pi // per_head)
        self.emit_S(jobs[0][0], jobs[0][1], ring[0], 0)
        for i in range(n):
            st, qc = jobs[i]
            if i + 1 < n:
                self.emit_S(jobs[i + 1][0], jobs[i + 1][1], ring[(i + 1) % 2], (i + 1) % 2)
            if i >= 1:
                pst, pqc = jobs[i - 1]
                pst.post_a(pqc, i - 1)
            self.emit_PV(st, qc, ring[i % 2], i % 2, i)
            if i >= 1:
                pst.post_b(pqc, i - 1)
                hook(i - 1)
        pst, pqc = jobs[n - 1]
        pst.post_a(pqc, n - 1)
        pst.post_b(pqc, n - 1)
        hook(n - 1)

    def post_transpose(self, obf, par, mixstg, k, qc, head, evac_eng):
        P, ps = self.P, self.ps
        bt = self.bank()
        psb = ps[:, bt, 0:256].bitcast(BF16)
        for j in range(4):
            P.op('pe', lambda e, j=j: e.transpose(out=psb[:, j * 128:(j + 1) * 128], in_=obf[par][:, j, :],
                                                  identity=self.identB),
                 rd=[('obf', par, j)], wr=[('ps', bt)])
        if evac_eng == 'act':
            P.op('act', lambda e: e.copy(out=mixstg[k][:, qc * 512:(qc + 1) * 512], in_=psb),
                 rd=[('ps', bt)], wr=[('mix', k, qc)])
        else:
            P.op('dve', lambda e: e.tensor_copy(out=mixstg[k][:, qc * 512:(qc + 1) * 512], in_=psb),
                 rd=[('ps', bt)], wr=[('mix', k, qc)])
        if qc == 3:
            P.dma('sp', self.MT[head], mixstg[k], rd=[('mix', k, q) for q in range(4)], wr=[('MT', head)])

    def load_V(self, Vt, k, col):
        self.P.dma('sp', Vt[:, :, 0:128], self.VT[:, col // 128, :, :], wr=[('V', k)])

    def phase_B_fox(self, l, s):
        P, sb, ps = self.P, self.sb, self.ps
        m = sb.mark()
        ring = [sb.take(16 * 512, BF16).rearrange("p (k q) -> p k q", k=16) for _ in range(2)]
        KT = [sb.take(S, BF16) for _ in range(2)]
        QT = [sb.take(S, BF16) for _ in range(2)]
        AL = [sb.take(S, BF16) for _ in range(2)]
        AR = [sb.take(S, BF16) for _ in range(2)]
        V = [sb.take(16 * 130, BF16).rearrange("p (k d) -> p k d", k=16) for _ in range(2)]
        mixstg = [sb.take(S, BF16) for _ in range(2)]
        obf = [sb.take(4 * 128, BF16).rearrange("p (j d) -> p j d", j=4) for _ in range(2)]
        raw = [sb.take(4 * 130, F32).rearrange("p (j d) -> p j d", j=4) for _ in range(2)]
        rr = [sb.take(4, F32) for _ in range(2)]
        for k in range(2):
            P.op('pool', lambda e, k=k: e.memset(AL[k], 0.0), wr=[('AL', k)])
            P.op('pool', lambda e, k=k: e.memset(AR[k], 0.0), wr=[('AR', k)])
            P.op('pool', lambda e, k=k: e.memset(V[k][:, :, 128:129], 1.0), wr=[('V', k)])
            P.op('pool', lambda e, k=k: e.memset(V[k][:, :, 129:130], 0.0), wr=[('V', k)])
        jobs = []
        sc = 128.0 ** -0.5
        for h in range(6):
            k = h % 2
            st = self.Stream()
            st.mode = 'causal'
            st.head = h
            st.k = k
            st.lhs = lambda kt, k=k: KT[k][:, kt * 128:(kt + 1) * 128]
            st.QT = QT[k]
            st.extra = lambda kt, q0, q1, k=k: [(AL[k][:, kt * 128:(kt + 1) * 128], AR[k][:, q0:q1])]
            st.table = lambda mm: self.NM0F if mm == 0 else None
            st.cb = None
            st.scale = sc
            st.V = lambda kt, k=k: V[k][:, kt, 0:129]
            st.ncols = 129
            st.rtoks = [('KT', k), ('QT', k), ('AL', k), ('AR', k)]
            st.vtoks = [('V', k)]
            st.loaded = False

            def epi(qi, b, ji):
                j = qi % 4
                par = ji % 2
                P.op('dve', lambda e: e.tensor_copy(out=raw[par][:, j, 0:129], in_=ps[:, b, 0:129]),
                     rd=[('ps', b)], wr=[('raw', par, j)])
            st.epi = epi

            def post(qc, ji, h=h, k=k):
                par = ji % 2
                rt = [('raw', par, j) for j in range(4)]
                P.op('dve', lambda e: e.reciprocal(out=rr[par].unsqueeze(2), in_=raw[par][:, :, 128:129]),
                     rd=rt, wr=[('rr', par)])
                P.op('dve', lambda e: e.tensor_tensor(out=obf[par], in0=raw[par][:, :, 0:128],
                                                      in1=rr[par].unsqueeze(2).to_broadcast([128, 4, 128]), op=ALU.mult),
                     rd=rt + [('rr', par)], wr=[('obf', par, j) for j in range(4)])
            st.post_a = post
            st.post_b = lambda qc, ji, h=h, k=k: self.post_transpose(obf, ji % 2, mixstg, k, qc, h, 'act')
            for qc in range(4):
                jobs.append((st, qc))
        def load(h):
            k = h % 2
            P.dma('sp', KT[k], self.PT[PT_FK + h], wr=[('KT', k)])
            P.dma('sp', QT[k], self.PT[PT_FQ + h], wr=[('QT', k)])
            self.load_V(V[k], k, VT_FV + 128 * h)
            P.dma('sp', AL[k][0:6, :], self.FAL[h], wr=[('AL', k)])
            P.dma('sp', AR[k][0:6, :], self.FAR[h], wr=[('AR', k)])
        load(0)
        load(1)
        self.run_jobs(jobs, ring, 4, lambda hi: load(hi + 2) if hi + 2 < 6 else None)
        P.barrier()
        sb.release(m)

    def phase_B_diff(self, l, s):
        P, sb, ps = self.P, self.sb, self.ps
        m = sb.mark()
        ring = [sb.take(16 * 512, BF16).rearrange("p (k q) -> p k q", k=16) for _ in range(2)]
        KT0 = [sb.take(S, BF16) for _ in range(2)]
        KT1 = [sb.take(S, BF16) for _ in range(2)]
        QT = [sb.take(S, BF16) for _ in range(2)]
        V = [sb.take(16 * 130, BF16).rearrange("p (k d) -> p k d", k=16) for _ in range(2)]
        mixstg = [sb.take(S, BF16) for _ in range(2)]
        obf = [sb.take(4 * 128, BF16).rearrange("p (j d) -> p j d", j=4) for _ in range(2)]
        t0 = [sb.take(4 * 128, F32).rearrange("p (j d) -> p j d", j=4) for _ in range(2)]
        od = sb.take(4 * 128, F32).rearrange("p (j d) -> p j d", j=4)
        sqd = sb.take(4 * 128, F32).rearrange("p (j d) -> p j d", j=4)
        raw = [sb.take(4 * 130, F32).rearrange("p (j d) -> p j d", j=4) for _ in range(2)]
        rr = [sb.take(4, F32) for _ in range(2)]
        ssq = sb.take(4, F32)
        mhalf = sb.take(4, F32)
        P.op('pool', lambda e: e.memset(mhalf, -0.5), wr=[('mhalf',)])
        gsub = sb.take(128, F32)
        lamb = sb.take(256, F32)
        pr = sb.take(128, F32)
        s12 = sb.take(2, F32)
        nlam = sb.take(1, F32)
        lam_init = 0.8 - 0.6 * math.exp(-0.3 * l)
        P.dma('sp', lamb, self.lam4[l:l + 1].rearrange("o a d -> o (a d)").partition_broadcast(128), wr=[('lamb',)])
        P.dma('sp', gsub, self.diff_subln_g[l:l + 1, :].partition_broadcast(128), wr=[('gsub',)])
        P.op('dve', lambda e: e.tensor_scalar(out=gsub, in0=gsub, scalar1=1.0 - lam_init, scalar2=None, op0=ALU.mult),
             rd=[('gsub',)], wr=[('gsub',)])
        P.op('dve', lambda e: e.tensor_tensor(out=pr[:, 0:64], in0=lamb[:, 0:64], in1=lamb[:, 64:128], op=ALU.mult),
             rd=[('lamb',)], wr=[('pr', 0)])
        P.op('dve', lambda e: e.tensor_tensor(out=pr[:, 64:128], in0=lamb[:, 128:192], in1=lamb[:, 192:256], op=ALU.mult),
             rd=[('lamb',)], wr=[('pr', 1)])
        P.op('dve', lambda e: e.tensor_reduce(out=s12, in_=pr.rearrange("p (a d) -> p a d", a=2), axis=AX.X, op=ALU.add),
             rd=[('pr', 0), ('pr', 1)], wr=[('s12',)])
        P.op('act', lambda e: e.activation(out=s12, in_=s12, func=AF.Exp), rd=[('s12',)], wr=[('s12',)])
        P.op('dve', lambda e: e.tensor_tensor(out=nlam, in0=s12[:, 1:2], in1=s12[:, 0:1], op=ALU.subtract),
             rd=[('s12',)], wr=[('nlam',)])
        P.op('dve', lambda e: e.tensor_scalar(out=nlam, in0=nlam, scalar1=-lam_init, scalar2=None, op0=ALU.add),
             rd=[('nlam',)], wr=[('nlam',)])
        for k in range(2):
            P.op('pool', lambda e, k=k: e.memset(KT0[k][64:128, :], 0.0), wr=[('KT0z', k)])
            P.op('pool', lambda e, k=k: e.memset(KT1[k][0:64, :], 0.0), wr=[('KT1z', k)])
            P.op('pool', lambda e, k=k: e.memset(V[k][:, :, 128:129], 1.0), wr=[('V', k)])
            P.op('pool', lambda e, k=k: e.memset(V[k][:, :, 129:130], 0.0), wr=[('V', k)])
        jobs = []
        for h in range(4):
            k = h % 2
            for mp in range(2):
                st = self.Stream()
                st.mode = 'causal'
                KTm = KT0 if mp == 0 else KT1
                st.lhs = lambda kt, k=k, KTm=KTm: KTm[k][:, kt * 128:(kt + 1) * 128]
                st.QT = QT[k]
                st.extra = lambda kt, q0, q1: []
                st.table = lambda mm, h=h: self.BT[:, h, 0, :] if mm == 0 else (self.BT[:, h, 1, :] if mm == 1 else None)
                st.cb = self.CB[:, h:h + 1]
                st.scale = 0.125
                st.V = lambda kt, k=k: V[k][:, kt, 0:129]
                st.ncols = 129
                st.rtoks = [('KT0', k), ('KT1', k), ('KT0z', k), ('KT1z', k), ('QT', k)]
                st.vtoks = [('V', k)]
                def epi(qi, b, ji, mp=mp):
                    j = qi % 4
                    P.op('dve', lambda e: e.tensor_copy(out=raw[mp][:, j, 0:129], in_=ps[:, b, 0:129]),
                         rd=[('ps', b)], wr=[('raw', mp, j)])
                st.epi = epi
                if mp == 0:
                    def post(qc, ji):
                        par = (ji // 2) % 2
                        rt = [('raw', 0, j) for j in range(4)]
                        P.op('dve', lambda e: e.reciprocal(out=rr[0].unsqueeze(2), in_=raw[0][:, :, 128:129]),
                             rd=rt, wr=[('rr', 0)])
                        P.op('dve', lambda e: e.tensor_tensor(out=t0[par], in0=raw[0][:, :, 0:128],
                                                              in1=rr[0].unsqueeze(2).to_broadcast([128, 4, 128]), op=ALU.mult),
                             rd=rt + [('rr', 0)], wr=[('t0', par)])
                    st.post_a = post
                    st.post_b = lambda qc, ji: None
                else:
                    def post(qc, ji, h=h, k=k):
                        par = (ji // 2) % 2
                        rt = [('raw', 1, j) for j in range(4)]
                        P.op('dve', lambda e: e.reciprocal(out=rr[1].unsqueeze(2), in_=raw[1][:, :, 128:129]),
                             rd=rt, wr=[('rr', 1)])
                        P.op('dve', lambda e: e.tensor_scalar(out=rr[1], in0=rr[1], scalar1=nlam[:, 0:1], scalar2=None,
                                                              op0=ALU.mult), rd=[('rr', 1), ('nlam',)], wr=[('rr', 1)])
                        P.op('dve', lambda e: e.tensor_tensor(out=od, in0=raw[1][:, :, 0:128],
                                                              in1=rr[1].unsqueeze(2).to_broadcast([128, 4, 128]), op=ALU.mult),
                             rd=rt + [('rr', 1)], wr=[('od',)])
                        P.op('pool', lambda e: e.tensor_tensor(out=od, in0=od, in1=t0[par], op=ALU.add),
                             rd=[('od',), ('t0', par)], wr=[('od',)])
                        P.op('pool', lambda e: e.tensor_tensor(out=sqd, in0=od, in1=od, op=ALU.mult),
                             rd=[('od',)], wr=[('sqd',)])
                        P.op('dve', lambda e: e.tensor_reduce(out=ssq, in_=sqd, axis=AX.X, op=ALU.add),
                             rd=[('sqd',)], wr=[('ssq',)])
                        P.op('dve', lambda e: e.tensor_scalar(out=ssq, in0=ssq, scalar1=EPS * 128.0, scalar2=1.0 / 128.0,
                                                              op0=ALU.add, op1=ALU.mult), rd=[('ssq',)], wr=[('ssq',)])
                        P.op('pool', lambda e: e.tensor_tensor(out=ssq, in0=ssq, in1=mhalf, op=ALU.pow),
                             rd=[('ssq',), ('mhalf',)], wr=[('ssq',)])
                        P.op('dve', lambda e: e.tensor_tensor(out=od, in0=od, in1=ssq.unsqueeze(2).to_broadcast([128, 4, 128]),
                                                              op=ALU.mult), rd=[('od',), ('ssq',)], wr=[('od',)])
                        P.op('pool', lambda e: e.tensor_tensor(out=obf[par], in0=od,
                                                               in1=gsub.unsqueeze(1).to_broadcast([128, 4, 128]), op=ALU.mult),
                             rd=[('od',), ('gsub',)], wr=[('obf', par, j) for j in range(4)])
                    st.post_a = post
                    st.post_b = lambda qc, ji, h=h, k=k: self.post_transpose(obf, (ji // 2) % 2, mixstg, k, qc, 6 + h, 'act')
                st.mp = mp
                jobs.append(st)
        joblist = []
        for h in range(4):
            for qc in range(4):
                joblist.append((jobs[2 * h], qc))
                joblist.append((jobs[2 * h + 1], qc))

        def load(h):
            k = h % 2
            P.dma('sp', KT0[k][0:64, :], self.PT[PT_DK + h, 0:64, :], wr=[('KT0', k)])
            P.dma('sp', KT1[k][64:128, :], self.PT[PT_DK + h, 64:128, :], wr=[('KT1', k)])
            P.dma('sp', QT[k], self.PT[PT_DQ + h], wr=[('QT', k)])
            self.load_V(V[k], k, VT_DV + 128 * h)
        load(0)
        load(1)
        self.run_jobs(joblist, ring, 8, lambda hi: load(hi + 2) if hi + 2 < 4 else None)
        P.barrier()
        sb.release(m)

    def phase_B_nsa(self, l, s):
        P, sb, ps = self.P, self.sb, self.ps
        m = sb.mark()
        sc = 128.0 ** -0.5
        NB = 4096.0
        kcmpT = [sb.take(128, BF16) for _ in range(2)]
        Rt = [sb.take(162, BF16) for _ in range(2)]
        gates = sb.take(16 * 18, F32).rearrange("p (a c) -> p a c", a=16)
        P.dma('sp', gates, self.GT.rearrange("a p c -> p a c"), wr=[('gates',)])
        for g in range(2):
            P.op('pool', lambda e, g=g: e.memset(Rt[g][:, 128:129], 1.0), wr=[('Rt1', g)])
            P.op('dve', lambda e, g=g: e.tensor_copy(out=Rt[g][0:NCMP, 129:161], in_=self.OVb[0:NCMP, :]), wr=[('Rt2', g)])
        m1 = sb.mark()
        w1 = [sb.take(32 * 128, BF16).rearrange("p (l j) -> p l j", l=32) for _ in range(2)]
        w2 = [sb.take(128, BF16) for _ in range(2)]
        posT = sb.take(32, BF16)
        xcT = [sb.take(S, BF16) for _ in range(4)]
        pb = sb.take(2, F32)
        xs = sb.take(128, F32)
        x2 = sb.take(128, F32)
        yy = sb.take(128, F32)
        gT = sb.take(128, BF16)
        for kv in range(2):
            P.dma('pool', w1[kv], self.nsa_cmp_w1[l, kv].rearrange("(l d) j -> d l j", d=128), wr=[('w1', kv)])
            P.dma('pool', w2[kv], self.nsa_cmp_w2[l, kv], wr=[('w2', kv)])
        posF = sb.take(128, F32)
        P.dma('sp', posF[0:32, :], self.nsa_cmp_pos[l], wr=[('posF',)])
        bp = self.bank()
        P.op('pe', lambda e: e.transpose(out=ps[:, bp, 0:32], in_=posF[0:32, :], identity=self.identF[0:32, 0:32]),
             rd=[('posF',)], wr=[('ps', bp)])
        P.op('dve', lambda e: e.tensor_copy(out=posT, in_=ps[:, bp, 0:32]), rd=[('ps', bp)], wr=[('posT',)])
        for kv in range(2):
            b2 = self.bank()
            for li in range(32):
                P.op('pe', lambda e, b2=b2, kv=kv, li=li: e.matmul(ps[:, b2, 0:1], w1[kv][:, li, :], posT[:, li:li + 1],
                                                                  start=(li == 0), stop=(li == 31)),
                     rd=[('w1', kv), ('posT',)], wr=[('ps', b2)])
            P.op('dve', lambda e, b2=b2, kv=kv: e.tensor_copy(out=pb[:, kv:kv + 1], in_=ps[:, b2, 0:1]),
                 rd=[('ps', b2)], wr=[('pb', kv)])
        n1banks = {}
        for g in range(2):
            for kv in range(2):
                xk = 2 * g + kv
                xc = xcT[xk]
                P.dma('sp', xc, self.PT[(PT_NKC if kv == 0 else PT_NVC) + g], wr=[('xc', xk)])
                b = self.bank()
                n1banks[(g, kv)] = b
                for li in range(32):
                    P.op('pe', lambda e, b=b, kv=kv, li=li, xc=xc: e.matmul(
                        ps[:, b, 0:NCMP], w1[kv][:, li, :], xc[:, li:li + 16 * (NCMP - 1) + 1:16],
                        start=(li == 0), stop=(li == 31)), rd=[('w1', kv), ('xc', xk)], wr=[('ps', b)])
        for g in range(2):
            for kv in range(2):
                b = n1banks[(g, kv)]
                P.op('act', lambda e, b=b, kv=kv: e.activation(out=xs[:, 0:NCMP], in_=ps[:, b, 0:NCMP], func=AF.Identity,
                                                              bias=pb[:, kv:kv + 1]),
                     rd=[('ps', b), ('pb', kv)], wr=[('xs',)])
                P.op('dve', lambda e: e.tensor_tensor(out=x2[:, 0:NCMP], in0=xs[:, 0:NCMP], in1=xs[:, 0:NCMP], op=ALU.mult),
                     rd=[('xs',)], wr=[('x2',)])
                P.op('dve', lambda e: e.tensor_scalar(out=x2[:, 0:NCMP], in0=x2[:, 0:NCMP], scalar1=0.044715, scalar2=1.0,
                                                      op0=ALU.mult, op1=ALU.add), rd=[('x2',)], wr=[('x2',)])
                P.op('dve', lambda e: e.tensor_tensor(out=yy[:, 0:NCMP], in0=x2[:, 0:NCMP], in1=xs[:, 0:NCMP], op=ALU.mult),
                     rd=[('x2',), ('xs',)], wr=[('yy',)])
                P.op('act', lambda e: e.activation(out=yy[:, 0:NCMP], in_=yy[:, 0:NCMP], func=AF.Sigmoid,
                                                   scale=1.5957691216057308), rd=[('yy',)], wr=[('yy',)])
                P.op('dve', lambda e: e.tensor_tensor(out=gT[:, 0:NCMP], in0=xs[:, 0:NCMP], in1=yy[:, 0:NCMP], op=ALU.mult),
                     rd=[('xs',), ('yy',)], wr=[('gT',)])
                b3 = self.bank()
                if kv == 0:
                    P.op('pe', lambda e, b3=b3: e.matmul(ps[:, b3, 0:NCMP], w2[0], gT[:, 0:NCMP], start=True, stop=True),
                         rd=[('w2', 0), ('gT',)], wr=[('ps', b3)])
                    P.op('dve', lambda e, b3=b3, g=g: e.tensor_copy(out=kcmpT[g][:, 0:NCMP], in_=ps[:, b3, 0:NCMP]),
                         rd=[('ps', b3)], wr=[('kcmpT', g)])
                else:
                    P.op('pe', lambda e, b3=b3: e.matmul(ps[0:NCMP, b3, 0:128], gT[:, 0:NCMP], w2[1], start=True, stop=True),
                         rd=[('w2', 1), ('gT',)], wr=[('ps', b3)])
                    P.op('dve', lambda e, b3=b3, g=g: e.tensor_copy(out=Rt[g][0:NCMP, 0:128], in_=ps[0:NCMP, b3, 0:128]),
                         rd=[('ps', b3)], wr=[('Rt0', g)])
        P.barrier()
        sb.release(m1)
        ring = [sb.take(16 * 512, BF16).rearrange("p (k q) -> p k q", k=16) for _ in range(2)]
        QT3 = [sb.take(S, BF16) for _ in range(3)]
        KS = sb.take(S, BF16)
        KW = sb.take(S, BF16)
        VS = sb.take(16 * 130, BF16).rearrange("p (k d) -> p k d", k=16)
        VW = sb.take(16 * 130, BF16).rearrange("p (k d) -> p k d", k=16)
        ETs = [sb.take(S, BF16) for _ in range(2)]
        Ycs = [sb.take(S, F32) for _ in range(2)]
        negselT = sb.take(S, BF16)
        imp = sb.take(16 * 32, F32).rearrange("p (a m) -> p a m", a=16)
        ocmp = [sb.take(16 * 128, BF16).rearrange("p (a d) -> p a d", a=16) for _ in range(3)]
        mixstg = [sb.take(S, BF16) for _ in range(2)]
        acc = [sb.take(4 * 128, F32).rearrange("p (j d) -> p j d", j=4) for _ in range(2)]
        obf = [sb.take(4 * 128, BF16).rearrange("p (j d) -> p j d", j=4) for _ in range(2)]
        rc16s = [sb.take(16, F32) for _ in range(3)]
        rg16s = [sb.take(16, F32) for _ in range(3)]
        craws = [sb.take(16 * 162, F32).rearrange("p (a d) -> p a d", a=16) for _ in range(2)]
        imp2 = sb.take(16 * 32, F32).rearrange("p (a m) -> p a m", a=16)
        scb = sb.take(16 * 32, F32).rearrange("p (a m) -> p a m", a=16)
        selm = sb.take(16 * 32, F32).rearrange("p (a m) -> p a m", a=16)
        mx = sb.take(16 * 8, F32).rearrange("p (a m) -> p a m", a=16)
        nsb = sb.take(16 * 32, BF16).rearrange("p (a m) -> p a m", a=16)
        raw = [sb.take(4 * 130, F32).rearrange("p (j d) -> p j d", j=4) for _ in range(2)]
        rr = [sb.take(4, F32) for _ in range(2)]
        tmpw = sb.take(4 * 128, F32).rearrange("p (j d) -> p j d", j=4)
        P.op('pool', lambda e: e.memset(negselT, 0.0), wr=[('nsT', q) for q in range(4)])
        for Vt, nm in ((VS, 'VS'), (VW, 'VW')):
            P.op('pool', lambda e, Vt=Vt: e.memset(Vt[:, :, 128:129], 1.0), wr=[(nm,)])
            P.op('pool', lambda e, Vt=Vt: e.memset(Vt[:, :, 129:130], 0.0), wr=[(nm,)])
        for g in range(2):
            P.dma('sp', KS, self.PT[PT_NKS + g], wr=[('KS',)])
            P.dma('sp', KW, self.PT[PT_NKW + g], wr=[('KW',)])
            P.dma('sp', VS[:, :, 0:128], self.VT[:, VT_NVS // 128 + g, :, :],
                  wr=[('VS',)])
            P.dma('sp', VW[:, :, 0:128], self.VT[:, VT_NVW // 128 + g, :, :],
                  wr=[('VW',)])
            cnt = 0
            for hh in range(3):
                h = 3 * g + hh
                yk = hh % 2
                Ycr = Ycs[yk][:, ::-1]
                if hh == 0:
                    for h2 in range(3):
                        P.dma('sp', QT3[h2], self.PT[PT_NQ + 3 * g + h2], wr=[('QT3', h2)])
                    for h2 in range(2):
                        P.dma('sp', Ycs[h2], bass.AP(tensor=self.TCr_h, offset=(4 + 3 * g + h2) * 4352 + 287,
                                                     ap=[[16, 128], [1, S]]), wr=[('Yc', h2)])
                if hh == 2:
                    P.dma('sp', Ycs[0], bass.AP(tensor=self.TCr_h, offset=(4 + h) * 4352 + 287,
                                                ap=[[16, 128], [1, S]]), wr=[('Yc', 0)])
                ek = hh % 2
                ETk = ETs[ek]
                crawk = craws[ek]
                rck = rc16s[hh]
                rgk = rg16s[hh]
                for qc in range(4):
                    b = self.bank()
                    q0, q1 = qc * 512, (qc + 1) * 512
                    P.op('pe', lambda e, b=b, hh=hh, q0=q0, q1=q1, g=g: e.matmul(
                        ps[0:NCMP, b, :], kcmpT[g][:, 0:NCMP], QT3[hh][:, q0:q1], start=True, stop=True),
                        rd=[('kcmpT', g), ('QT3', hh)], wr=[('ps', b)])
                    P.op('dve', lambda e, b=b, q0=q0, q1=q1, Ycr=Ycr: e.tensor_tensor(
                        out=ps[0:NCMP, b, :], in0=ps[0:NCMP, b, :], in1=Ycr[0:NCMP, q0:q1], op=ALU.add),
                        rd=[('ps', b), ('Yc', yk)], wr=[('ps', b)])
                    P.op('act', lambda e, b=b, q0=q0, q1=q1, h=h, ETk=ETk: e.activation(
                        out=ETk[0:NCMP, q0:q1], in_=ps[0:NCMP, b, :], func=AF.Exp, scale=sc,
                        bias=self.CB[0:NCMP, 4 + h:5 + h]), rd=[('ps', b)], wr=[('ET', ek, qc)])
                for qi in range(16):
                    b = self.bank()
                    P.op('pe', lambda e, b=b, qi=qi, g=g, ETk=ETk: e.matmul(
                        ps[:, b, 0:161], ETk[0:NCMP, qi * 128:(qi + 1) * 128], Rt[g][0:NCMP, 0:161], start=True, stop=True),
                        rd=[('ET', ek, qi // 4), ('Rt0', g), ('Rt1', g), ('Rt2', g)], wr=[('ps', b)])
                    if qi % 2 == 0:
                        P.op('act', lambda e, b=b, qi=qi, crawk=crawk: e.copy(out=crawk[:, qi, 0:161], in_=ps[:, b, 0:161]),
                             rd=[('ps', b)], wr=[('craw', ek, qi)])
                    else:
                        P.op('dve', lambda e, b=b, qi=qi, crawk=crawk: e.tensor_copy(out=crawk[:, qi, 0:161], in_=ps[:, b, 0:161]),
                             rd=[('ps', b)], wr=[('craw', ek, qi)])
                ct = [('craw', ek, qi) for qi in range(16)]
                P.op('dve', lambda e, crawk=crawk, rck=rck: e.tensor_scalar(
                    out=rck.unsqueeze(2), in0=crawk[:, :, 128:129], scalar1=1e-30, scalar2=None, op0=ALU.max),
                    rd=ct, wr=[('rc16', hh)])
                P.op('dve', lambda e, rck=rck: e.reciprocal(out=rck, in_=rck), rd=[('rc16', hh)], wr=[('rc16', hh)])
                P.op('dve', lambda e, h=h, rck=rck, rgk=rgk: e.tensor_tensor(
                    out=rgk.unsqueeze(2), in0=rck.unsqueeze(2), in1=gates[:, :, 3 * h:3 * h + 1], op=ALU.mult),
                    rd=[('rc16', hh), ('gates',)], wr=[('rg16', hh)])
                P.op('pool', lambda e, hh=hh, crawk=crawk, rgk=rgk: e.tensor_tensor(
                    out=ocmp[hh], in0=crawk[:, :, 0:128], in1=rgk.unsqueeze(2).to_broadcast([128, 16, 128]), op=ALU.mult),
                    rd=ct + [('rg16', hh)], wr=[('ocmp', hh, qi) for qi in range(16)])
                if hh == 0:
                    P.op('dve', lambda e, crawk=crawk, rck=rck: e.tensor_tensor(
                        out=imp, in0=crawk[:, :, 129:161], in1=rck.unsqueeze(2).to_broadcast([128, 16, 32]), op=ALU.mult),
                        rd=ct + [('rc16', hh)], wr=[('imp',)])
                else:
                    P.op('dve', lambda e, crawk=crawk, rck=rck: e.tensor_tensor(
                        out=imp2, in0=crawk[:, :, 129:161], in1=rck.unsqueeze(2).to_broadcast([128, 16, 32]), op=ALU.mult),
                        rd=ct + [('rc16', hh)], wr=[('imp2',)])
                    P.op('dve', lambda e: e.tensor_tensor(out=imp, in0=imp, in1=imp2, op=ALU.add),
                         rd=[('imp',), ('imp2',)], wr=[('imp',)])
            P.op('dve', lambda e: e.tensor_tensor(out=scb, in0=imp, in1=self.FM, op=ALU.add), rd=[('imp',)], wr=[('scb',)])
            for qi in range(16):
                P.op('dve', lambda e, qi=qi: e.max(out=mx[:, qi, :], in_=scb[:, qi, :]), rd=[('scb',)], wr=[('mx', qi)])
            P.op('dve', lambda e: e.tensor_tensor(out=selm, in0=scb, in1=mx[:, :, 7:8].to_broadcast([128, 16, 32]),
                                                  op=ALU.is_ge), rd=[('scb',)] + [('mx', qi) for qi in range(16)], wr=[('selm',)])
            P.op('dve', lambda e: e.tensor_scalar(out=nsb, in0=selm, scalar1=-1.0, scalar2=NB, op0=ALU.add, op1=ALU.mult),
                 rd=[('selm',)], wr=[('nsb',)])
            for qc in range(4):
                bt = self.bank()
                psb = ps[0:32, bt, 0:256].bitcast(BF16)
                for j in range(4):
                    qi = 4 * qc + j
                    P.op('pe', lambda e, j=j, qi=qi, psb=psb: e.transpose(out=psb[:, j * 128:(j + 1) * 128], in_=nsb[:, qi, :],
                                                                          identity=self.identB),
                         rd=[('nsb',)], wr=[('ps', bt)])
                P.op('act', lambda e, psb=psb, qc=qc: e.copy(out=negselT[0:32, qc * 512:(qc + 1) * 512], in_=psb),
                     rd=[('ps', bt)], wr=[('nsT', qc)])
            joblist = []
            for hh in range(3):
                h = 3 * g + hh
                tab = lambda mm, h=h: (self.BT[:, 4 + h, 0, :] if mm == 0 else (self.BT[:, 4 + h, 1, :] if mm == 1 else None))
                ss = self.Stream()
                ss.mode = 'causal'
                ss.lhs = lambda kt: KS[:, kt * 128:(kt + 1) * 128]
                ss.QT = QT3[hh]
                ss.extra = lambda kt, q0, q1: [(self.Epad[:, kt * 128:(kt + 1) * 128], negselT[:, q0:q1])]
                ss.table = tab
                ss.cb = self.CB[:, 4 + h:5 + h]
                ss.scale = sc
                ss.V = lambda kt: VS[:, kt, 0:129]
                ss.ncols = 129
                ss.rtoks = [('KS',), ('QT3', hh)] + [('nsT', q) for q in range(4)]
                ss.vtoks = [('VS',)]

                def epi_s(qi, b, ji):
                    j = qi % 4
                    P.op('dve', lambda e: e.tensor_copy(out=raw[0][:, j, 0:129], in_=ps[:, b, 0:129]),
                         rd=[('ps', b)], wr=[('raw', 0, j)])

                def post_s(qc, ji, h=h, hh=hh):
                    par = (ji // 2) % 2
                    rt = [('raw', 0, j) for j in range(4)]
                    P.op('dve', lambda e: e.reciprocal(out=rr[0].unsqueeze(2), in_=raw[0][:, :, 128:129]),
                         rd=rt, wr=[('rr', 0)])
                    P.op('dve', lambda e: e.tensor_tensor(out=rr[0].unsqueeze(2), in0=rr[0].unsqueeze(2),
                                                          in1=gates[:, 4 * qc:4 * qc + 4, 3 * h + 1:3 * h + 2], op=ALU.mult),
                         rd=[('rr', 0), ('gates',)], wr=[('rr', 0)])
                    P.op('dve', lambda e: e.tensor_tensor(out=acc[par], in0=raw[0][:, :, 0:128],
                                                          in1=rr[0].unsqueeze(2).to_broadcast([128, 4, 128]), op=ALU.mult),
                         rd=rt + [('rr', 0)], wr=[('acc', par)])
                    P.op('pool', lambda e: e.tensor_tensor(out=acc[par], in0=acc[par], in1=ocmp[hh][:, 4 * qc:4 * qc + 4, :],
                                                           op=ALU.add),
                         rd=[('acc', par)] + [('ocmp', hh, q) for q in range(4 * qc, 4 * qc + 4)], wr=[('acc', par)])
                ss.epi = epi_s
                ss.post_a = post_s
                ss.post_b = lambda qc, ji: None
                sw = self.Stream()
                sw.mode = 'window'
                sw.lhs = lambda kt: KW[:, kt * 128:(kt + 1) * 128]
                sw.QT = QT3[hh]
                sw.extra = lambda kt, q0, q1: []
                sw.table = lambda mm, tab=tab: (self.W4F if mm == 4 else tab(mm))
                sw.cb = self.CB[:, 4 + h:5 + h]
                sw.scale = sc
                sw.V = lambda kt: VW[:, kt, 0:129]
                sw.ncols = 129
                sw.rtoks = [('KW',), ('QT3', hh)]
                sw.vtoks = [('VW',)]

                def epi_w(qi, b, ji):
                    j = qi % 4
                    P.op('dve', lambda e: e.tensor_copy(out=raw[1][:, j, 0:129], in_=ps[:, b, 0:129]),
                         rd=[('ps', b)], wr=[('raw', 1, j)])

                def post_w(qc, ji, h=h, hh=hh):
                    par = (ji // 2) % 2
                    rt = [('raw', 1, j) for j in range(4)]
                    P.op('dve', lambda e: e.reciprocal(out=rr[1].unsqueeze(2), in_=raw[1][:, :, 128:129]),
                         rd=rt, wr=[('rr', 1)])
                    P.op('dve', lambda e: e.tensor_tensor(out=rr[1].unsqueeze(2), in0=rr[1].unsqueeze(2),
                                                          in1=gates[:, 4 * qc:4 * qc + 4, 3 * h + 2:3 * h + 3], op=ALU.mult),
                         rd=[('rr', 1), ('gates',)], wr=[('rr', 1)])
                    P.op('dve', lambda e: e.tensor_tensor(out=tmpw, in0=raw[1][:, :, 0:128],
                                                          in1=rr[1].unsqueeze(2).to_broadcast([128, 4, 128]), op=ALU.mult),
                         rd=rt + [('rr', 1)], wr=[('tmpw',)])
                    P.op('pool', lambda e: e.tensor_tensor(out=obf[par], in0=tmpw, in1=acc[par], op=ALU.add),
                         rd=[('tmpw',), ('acc', par)], wr=[('obf', par, j) for j in range(4)])
                sw.epi = epi_w
                sw.post_a = post_w
                sw.post_b = lambda qc, ji, h=h, hh=hh: self.post_transpose(obf, (ji // 2) % 2, mixstg, hh % 2, qc, 10 + h, 'act')
                for qc in range(4):
                    joblist.append((ss, qc))
                    joblist.append((sw, qc))
            self.run_jobs(joblist, ring)
        P.barrier()
        sb.release(m)

    def phase_C(self, l, s):
        P, sb, ps = self.P, self.sb, self.ps
        m = sb.mark()
        csub = "1234"
        mixT = sb.take(NCH * S, BF16).rearrange("p (c t) -> p c t", c=NCH)
        for hd in range(16 if "1" in csub else 0):
            P.dma('sp', mixT[:, hd, :], self.MT[hd], wr=[('mixT', hd)])
        wbuf = [sb.take(NCH * 512, BF16).rearrange("p (c f) -> p c f", c=NCH) for _ in range(2)]
        hold = [sb.take(S, F32) for _ in range(2)]
        hnew = [sb.take(S, F32) for _ in range(2)]
        n = 0
        for i in range(4 if "1" in csub else 0):
            wb = wbuf[i % 2]
            wtoks = self.load_w_slab(wb, i % 2, self.w_out[l], [(512 * i, 512)])
            for ft in range(4):
                co = 4 * i + ft
                k2 = n % 2
                n += 1
                P.dma('sp', hold[k2], self.hT[s, :, co, :], rd=[('hTc', co)], wr=[('hold', k2)])
                for j in range(4):
                    b = self.bank()
                    for c in range(NCH):
                        P.op('pe', lambda e, b=b, c=c, wb=wb, ft=ft, j=j: e.matmul(
                            ps[:, b, :], wb[:, c, ft * 128:(ft + 1) * 128], mixT[:, c, j * 512:(j + 1) * 512],
                            start=(c == 0), stop=(c == NCH - 1)), rd=wtoks + [('mixT', c)], wr=[('ps', b)])
                    P.op('dve', lambda e, b=b, k2=k2, j=j: e.tensor_tensor(
                        out=hnew[k2][:, j * 512:(j + 1) * 512], in0=ps[:, b, :], in1=hold[k2][:, j * 512:(j + 1) * 512],
                        op=ALU.add), rd=[('ps', b), ('hold', k2)], wr=[('hnew', k2, j)])
                P.dma('sp', self.hT[s, :, co, :], hnew[k2], rd=[('hnew', k2, j) for j in range(4)], wr=[('hTc', co)])
        P.barrier()
        sb.release(m)
        u2T = sb.take(NCH * S, BF16).rearrange("p (c t) -> p c t", c=NCH)
        cw = sb.take(3 * 88, F32).rearrange("p (t f) -> p t f", t=3)
        cbv = sb.take(88, F32)
        wbuf = [sb.take(NCH * 512, BF16).rearrange("p (c f) -> p c f", c=NCH) for _ in range(2)]
        up_segs = lambda i: [(256 * i, 256), (DFF + 256 * i, 256)]
        pre_toks = [self.load_w_slab(wbuf[k], k, self.ffn_w_up[l], up_segs(k)) for k in range(2)]
        m2 = sb.mark()
        cwA = sb.take(4 * 128, F32).rearrange("p (t q) -> p t q", t=4)
        P.dma('sp', cwA[0:88, 0:3, :], self.ffn_conv_w[l].rearrange("t (f p) -> f t p", p=128), wr=[('cwA', 0)])
        P.dma('sp', cwA[0:88, 3, :], self.ffn_conv_b[l].rearrange("(f p) -> f p", p=128), wr=[('cwA', 1)])
        for t in range(4):
            b = self.bank()
            P.op('pe', lambda e, b=b, t=t: e.transpose(out=ps[:, b, 0:88], in_=cwA[0:88, t, :], identity=self.identF[0:88, 0:88]),
                 rd=[('cwA', 0), ('cwA', 1)], wr=[('ps', b)])
            if t < 3:
                P.op('dve', lambda e, b=b, t=t: e.tensor_copy(out=cw[:, t, :], in_=ps[:, b, 0:88]), rd=[('ps', b)], wr=[('cw',)])
            else:
                P.op('dve', lambda e, b=b: e.tensor_copy(out=cbv, in_=ps[:, b, 0:88]), rd=[('ps', b)], wr=[('cw',)])
        self.rmsnorm_to_uT(lambda j: ('none',), self.hT[s], self.g2, l * NCH, u2T)
        P.barrier()
        sb.release(m2)
        hgu = [sb.take(S + 2, F32) for _ in range(4)]
        cgu = [sb.take(S, F32) for _ in range(2)]
        aT = [sb.take(S, BF16) for _ in range(2)]
        for k in range(4):
            P.op('pool', lambda e, k=k: e.memset(hgu[k][:, 0:2], 0.0), wr=[('hgz', k)])
        utoks = lambda c, t0, t1: [('uT', c, i) for i in range(t0 // 256, (t1 + 255) // 256)]
        nt = 0
        for i in range(22 if "3" in csub else 0):
            wb = wbuf[i % 2]
            wtoks = pre_toks[i] if i < 2 else self.load_w_slab(wb, i % 2, self.ffn_w_up[l], up_segs(i))
            for ft in range(2):
                f = 2 * i + ft
                par = nt % 2
                nt += 1
                for br in range(2):
                    hb = hgu[2 * par + br]
                    hk = 2 * par + br
                    wcol = br * 256 + ft * 128
                    fi = f + 44 * br
                    for j in range(4):
                        b = self.bank()
                        for c in range(NCH):
                            P.op('pe', lambda e, b=b, c=c, wb=wb, wcol=wcol, j=j: e.matmul(
                                ps[:, b, :], wb[:, c, wcol:wcol + 128], u2T[:, c, j * 512:(j + 1) * 512],
                                start=(c == 0), stop=(c == NCH - 1)), rd=wtoks, wr=[('ps', b)])
                        P.op('act', lambda e, b=b, hb=hb, j=j: e.copy(out=hb[:, 2 + j * 512:2 + (j + 1) * 512], in_=ps[:, b, :]),
                             rd=[('ps', b), ('hgz', hk)], wr=[('hgu', hk, j)])
                    cg = cgu[br]
                    htok = [('hgu', hk, j) for j in range(4)]
                    P.op('dve', lambda e, hb=hb, cg=cg, fi=fi: e.tensor_scalar(
                        out=cg, in0=hb[:, 2:S + 2], scalar1=cw[:, 2, fi:fi + 1], scalar2=cbv[:, fi:fi + 1],
                        op0=ALU.mult, op1=ALU.add), rd=htok, wr=[('cgu', br)])
                    P.op('dve', lambda e, hb=hb, cg=cg, fi=fi: e.scalar_tensor_tensor(
                        out=cg, in0=hb[:, 1:S + 1], scalar=cw[:, 1, fi:fi + 1], in1=cg, op0=ALU.mult, op1=ALU.add),
                        rd=htok + [('cgu', br)], wr=[('cgu', br)])
                    P.op('dve', lambda e, hb=hb, cg=cg, fi=fi: e.scalar_tensor_tensor(
                        out=cg, in0=hb[:, 0:S], scalar=cw[:, 0, fi:fi + 1], in1=cg, op0=ALU.mult, op1=ALU.add),
                        rd=htok + [('cgu', br)], wr=[('cgu', br)])
                P.op('act', lambda e: e.activation(out=cgu[0], in_=cgu[0], func=AF.Silu), rd=[('cgu', 0)], wr=[('cgu', 0)])
                P.op('dve', lambda e, par=par: e.tensor_tensor(out=aT[par], in0=cgu[0], in1=cgu[1], op=ALU.mult),
                     rd=[('cgu', 0), ('cgu', 1)], wr=[('aT', par)])
                P.dma('sp', self.AT[f], aT[par], rd=[('aT', par)], wr=[('AT', f)])
        P.barrier()
        sb.release(m)
        m = sb.mark()
        actH = sb.take(NFF * 1024, BF16).rearrange("p (f t) -> p f t", f=NFF)
        wd = [sb.take(NFF * 256, BF16).rearrange("p (f c) -> p f c", f=NFF) for _ in range(2)]
        hold4 = [sb.take(1024, F32) for _ in range(2)]
        hnew4 = [sb.take(1024, F32) for _ in range(2)]
        nw = 0
        n = 0
        for half in range(2 if "4" in csub else 0):
            t0 = half * 1024
            for f in range(NFF):
                P.dma('sp', actH[:, f, :], self.AT[f, :, t0:t0 + 1024], wr=[('actH', f)])
            for i in range(8):
                wk = nw % 2
                nw += 1
                P.dma('pool', wd[wk], self.ffn_w_down[l, :, 256 * i:256 * (i + 1)].rearrange("(fc p) c -> p fc c", p=128),
                      wr=[('wd', wk)])
                for ft in range(2):
                    co = 2 * i + ft
                    k2 = n % 2
                    n += 1
                    P.dma('sp', hold4[k2], self.hT[s, :, co, t0:t0 + 1024], rd=[('hTd', co, half)], wr=[('hold4', k2)])
                    for j in range(2):
                        b = self.bank()
                        for fc in range(NFF):
                            P.op('pe', lambda e, b=b, fc=fc, wk=wk, ft=ft, j=j: e.matmul(
                                ps[:, b, :], wd[wk][:, fc, ft * 128:(ft + 1) * 128], actH[:, fc, j * 512:(j + 1) * 512],
                                start=(fc == 0), stop=(fc == NFF - 1)), rd=[('wd', wk), ('actH', fc)], wr=[('ps', b)])
                        P.op('dve', lambda e, b=b, k2=k2, j=j: e.tensor_tensor(
                            out=hnew4[k2][:, j * 512:(j + 1) * 512], in0=ps[:, b, :], in1=hold4[k2][:, j * 512:(j + 1) * 512],
                            op=ALU.add), rd=[('ps', b), ('hold4', k2)], wr=[('hnew4', k2, j)])
                    P.dma('sp', self.hT[s, :, co, t0:t0 + 1024], hnew4[k2], rd=[('hnew4', k2, j) for j in range(2)],
                          wr=[('hTd', co, half)])
        P.barrier()
        sb.release(m)

    def phase_Z(self, s):
        P, sb, ps = self.P, self.sb, self.ps
        m = sb.mark()
        yb = [sb.take(NCH * 256, F32).rearrange("p (c t) -> p c t", c=NCH) for _ in range(2)]
        ostg = [sb.take(D, F32) for _ in range(2)]
        cnt = [0]

        def out_cb(i, hb, rstd):
            y = yb[i % 2]
            for c in range(NCH):
                P.op('dve', lambda e, c=c: e.scalar_tensor_tensor(
                    out=y[:, c, :], in0=hb[:, c, :], scalar=self.g3[:, c:c + 1], in1=rstd, op0=ALU.mult, op1=ALU.mult),
                    rd=[('hb', i % 2), ('rstd',)], wr=[('yb', i % 2, c)])
            for tt in range(2):
                ok = cnt[0] % 2
                cnt[0] += 1
                for cg4 in range(4):
                    b = self.bank()
                    for cc in range(4):
                        c = cg4 * 4 + cc
                        P.op('pe', lambda e, b=b, cc=cc, c=c, tt=tt: e.transpose(
                            out=ps[:, b, cc * 128:(cc + 1) * 128], in_=y[:, c, tt * 128:(tt + 1) * 128], identity=self.identF),
                            rd=[('yb', i % 2, c)], wr=[('ps', b)])
                    if cg4 % 2 == 0:
                        P.op('act', lambda e, b=b, cg4=cg4, ok=ok: e.copy(out=ostg[ok][:, cg4 * 512:(cg4 + 1) * 512], in_=ps[:, b, :]),
                             rd=[('ps', b)], wr=[('ostg', ok, cg4)])
                    else:
                        P.op('dve', lambda e, b=b, cg4=cg4, ok=ok: e.tensor_copy(out=ostg[ok][:, cg4 * 512:(cg4 + 1) * 512], in_=ps[:, b, :]),
                             rd=[('ps', b)], wr=[('ostg', ok, cg4)])
                r0 = i * 256 + tt * 128
                P.dma('sp', self.out[s, r0:r0 + 128, :], ostg[ok], rd=[('ostg', ok, q) for q in range(4)], wr=[('out', s, r0)])
        self.rmsnorm_to_uT(lambda j: ('none',), self.hT[s], self.g3, 0, None, out_cb=out_cb)
        P.barrier()
        sb.release(m)

CST_IDENT = 0
CST_NM0 = 128
CST_W4 = 256
CST_OH = 384
CST_INVSC = 640
CST_OV = 641
CST_FM = 673
CST_E = 1185
CST_W = 3233


def t5_bucket_np(n):
    n = np.maximum(n, 0)
    nf = np.maximum(n, 1).astype(np.float32)
    large = 16 + (np.log(nf / np.float32(16)) / np.float32(math.log(128 / 16)) * np.float32(16)).astype(np.int32)
    large = np.minimum(large, 31)
    return np.where(n < 16, n, large)


def make_cst():
    c = np.zeros((128, CST_W), np.float32)
    c[:, CST_IDENT:CST_IDENT + 128] = np.eye(128, dtype=np.float32)
    p = np.arange(128)[:, None]
    q = np.arange(128)[None, :]
    c[:, CST_NM0:CST_NM0 + 128] = np.where(p > q, -BIG, 0.0)
    c[:, CST_W4:CST_W4 + 128] = np.where(q >= p, -BIG, 0.0)
    bk = t5_bucket_np(np.arange(256))
    oh = np.zeros((32, 256), np.float32)
    oh[bk, np.arange(256)] = 1.0
    oh[31, :] -= 1.0
    c[0:32, CST_OH:CST_OH + 256] = oh[:, ::-1]
    c[0:4, CST_INVSC] = 8.0
    c[4:10, CST_INVSC] = math.sqrt(128.0)
    cs_ = np.arange(NCMP) * 16
    ss_ = np.arange(NSEL) * 64
    ov = np.clip(np.minimum(cs_[:, None] + 32, ss_[None, :] + 64) - np.maximum(cs_[:, None], ss_[None, :]), 0, None) / 32.0
    c[0:NCMP, CST_OV:CST_OV + 32] = ov
    t = np.arange(S)
    blk = t // 64
    j = np.arange(NSEL)
    valid = j[None, :] <= blk[:, None]
    forced = (j[None, :] == 0) | (j[None, :] == blk[:, None]) | (j[None, :] == blk[:, None] - 1)
    fm = np.where(valid, np.where(forced, 1e4, 0.0), -1e30).astype(np.float32)
    c[:, CST_FM:CST_FM + 512] = fm.reshape(16, 128, 32).transpose(1, 0, 2).reshape(128, 512)
    k = np.arange(S)
    e = np.zeros((32, S), np.float32)
    e[k // 64, k] = 1.0
    c[0:32, CST_E:CST_E + S] = e
    return c


def make_in_map(inputs, seqs, L):
    f = lambda a: np.ascontiguousarray(np.asarray(a, dtype=np.float32))
    m = {
        "x": f(inputs["x"][seqs]),
        "attn_norm_g": f(inputs["attn_norm_g"][:L]),
        "w_in": f(inputs["w_in"][:L]),
        "fox_f_bias": f(inputs["fox_f_bias"][:L]),
        "diff_lam4": f(np.stack([inputs["diff_lq1"][:L], inputs["diff_lk1"][:L],
                                 inputs["diff_lq2"][:L], inputs["diff_lk2"][:L]], axis=1)),
        "diff_subln_g": f(inputs["diff_subln_g"][:L]),
        "nsa_cmp_pos": f(inputs["nsa_cmp_pos"][:L]),
        "nsa_cmp_w1": f(np.stack([inputs["nsa_cmp_wk1"][:L], inputs["nsa_cmp_wv1"][:L]], axis=1)),
        "nsa_cmp_w2": f(np.stack([inputs["nsa_cmp_wk2"][:L], inputs["nsa_cmp_wv2"][:L]], axis=1)),
        "w_out": f(inputs["w_out"][:L]),
        "ffn_norm_g": f(inputs["ffn_norm_g"][:L]),
        "ffn_w_up": f(inputs["ffn_w_up"][:L]),
        "ffn_conv_w": f(inputs["ffn_conv_w"][:L]),
        "ffn_conv_b": f(inputs["ffn_conv_b"][:L]),
        "ffn_w_down": f(inputs["ffn_w_down"][:L]),
        "rel_bias": f(inputs["rel_bias"]),
        "final_norm_g": f(inputs["final_norm_g"]),
        "cst": make_cst(),
    }
    return m


def kernel(**inputs):
    L, NSEQ, NCORE = 4, 2, 8
    b = Builder(L, NSEQ)
    nc = b.build(phases=("x0", "A", "Bf", "Bd", "Bn", "C", "Z"))
    in_maps = [make_in_map(inputs, slice(NSEQ * i, NSEQ * (i + 1)), L) for i in range(NCORE)]
    res = run_bass_kernel_spmd(nc, in_maps, core_ids=list(range(NCORE)))
    return np.concatenate([np.asarray(r["out"]) for r in res.results], axis=0).astype(np.float32)
```
